# Optimizing a Trainium2 kernel written in Bass

```python
import math
import jax
import jax.numpy as jnp
from jax import lax
import numpy as np


D_MODEL = 1024
BATCH = 4
SEQ = 4096
DEPTH = 4

N_MIXERS = 3
N_A = (DEPTH + 2) // N_MIXERS
N_B = (DEPTH + 1) // N_MIXERS
N_C = DEPTH // N_MIXERS

DA_HEADS = 8
DA_HEAD_DIM = 64
DA_QK = 2 * DA_HEADS * DA_HEAD_DIM
DA_V = DA_HEADS * 2 * DA_HEAD_DIM
HG_EXPAND = 128
HG_HEADS = D_MODEL // HG_EXPAND
HG_KDIM = HG_EXPAND
HG_VDIM = D_MODEL // HG_HEADS
HG_FDIM = HG_HEADS * HG_KDIM
HG_CHUNK = 64
SW_Q_HEADS = 16
SW_KV_HEADS = 2
SW_HEAD_DIM = 64
SW_GROUP = SW_Q_HEADS // SW_KV_HEADS
SW_WINDOW = 128
SW_IN = SW_Q_HEADS * SW_HEAD_DIM + 2 * SW_KV_HEADS * SW_HEAD_DIM
N_BUCKETS = 32
MAX_DISTANCE = 128
N_BIAS_HEADS = 16
BLOCK = 128
D_FF = 4 * D_MODEL
PLE_DIM = 256
ALPHA = (2 * DEPTH) ** 0.25
BETA = (8 * DEPTH) ** -0.25
LN_EPS = 1e-5
RMS_EPS = 1e-6

kernel_name = "hybrid_diffattn_hgrn2_swa_sink_trunk"


def layer_norm(x, g, b):
    x32 = x.astype(jnp.float32)
    mu = jnp.mean(x32, axis=-1, keepdims=True)
    var = jnp.mean(jnp.square(x32 - mu), axis=-1, keepdims=True)
    y = (x32 - mu) * lax.rsqrt(var + LN_EPS) * g.astype(jnp.float32) + b.astype(jnp.float32)
    return y.astype(x.dtype)


def rms_norm(x, g):
    x32 = x.astype(jnp.float32)
    y = x32 * lax.rsqrt(jnp.mean(jnp.square(x32), axis=-1, keepdims=True) + RMS_EPS)
    return (y * g.astype(jnp.float32)).astype(x.dtype)


def t5_bucket(rel):
    n = jnp.maximum(rel, 0)
    max_exact = N_BUCKETS // 2
    nf = jnp.maximum(n, 1).astype(jnp.float32)
    large = max_exact + (jnp.log(nf / max_exact) / math.log(MAX_DISTANCE / max_exact)
                         * (N_BUCKETS - max_exact)).astype(jnp.int32)
    large = jnp.minimum(large, N_BUCKETS - 1)
    return jnp.where(n < max_exact, n, large)


def diff_attention(h, w_in, lam, subln, w_out, rel_bias, layer_idx):
    B, S, _ = h.shape
    H, d = DA_HEADS, DA_HEAD_DIM
    q, k, v = jnp.split(h @ w_in, [DA_QK, 2 * DA_QK], axis=-1)
    q = q.reshape(B, S, 2 * H, d).transpose(0, 2, 1, 3) * (d ** -0.5)
    k = k.reshape(B, S, 2 * H, d).transpose(0, 2, 1, 3)
    v = v.reshape(B, S, H, 2 * d).transpose(0, 2, 1, 3)
    lam_init = 0.8 - 0.6 * math.exp(-0.3 * layer_idx)
    lam32 = lam.astype(jnp.float32)
    lam_full = jnp.exp(jnp.sum(lam32[0] * lam32[1])) - jnp.exp(jnp.sum(lam32[2] * lam32[3])) + lam_init
    pos = jnp.arange(S, dtype=jnp.int32)
    outs = []
    for blk in range(S // BLOCK):
        qs = blk * BLOCK
        ke = qs + BLOCK
        s = jnp.einsum('bmqd,bmkd->bmqk', q[:, :, qs:ke], k[:, :, :ke]).astype(jnp.float32)
        rel = pos[qs:ke, None] - pos[None, :ke]
        bias = rel_bias[t5_bucket(rel)].astype(jnp.float32).transpose(2, 0, 1)
        s = jnp.where(rel >= 0, s + bias, -jnp.inf)
        a = jax.nn.softmax(s, axis=-1).reshape(B, H, 2, BLOCK, ke)
        a = a[:, :, 0] - lam_full * a[:, :, 1]
        outs.append(jnp.einsum('bhqk,bhkv->bhqv', a.astype(v.dtype), v[:, :, :ke]))
    o = jnp.concatenate(outs, axis=2)
    o = rms_norm(o, subln) * (1.0 - lam_init)
    o = o.transpose(0, 2, 1, 3).reshape(B, S, DA_V)
    return o @ w_out


def hgrn2(h, w_in, gnorm, w_out, lower_bound):
    B, S, _ = h.shape
    H, dk, dv, C = HG_HEADS, HG_KDIM, HG_VDIM, HG_CHUNK
    q, f, i, g = jnp.split(h @ w_in, 4, axis=-1)
    q = jax.nn.silu(q.astype(jnp.float32))
    lb = lower_bound.astype(jnp.float32)
    f = lb + (1.0 - lb) * jax.nn.sigmoid(f.astype(jnp.float32))
    logf = jnp.log(f)
    kin = 1.0 - f

    def to_chunks(t, dim):
        return t.reshape(B, S // C, C, H, dim).transpose(1, 0, 3, 2, 4)

    qc, kc, gc = to_chunks(q, dk), to_chunks(kin, dk), to_chunks(logf, dk)
    vc = to_chunks(i.astype(jnp.float32), dv)
    causal = jnp.tril(jnp.ones((C, C), dtype=bool))

    def step(state, inp):
        q_, k_, v_, g_ = inp
        G = jnp.cumsum(g_, axis=2)
        G_last = G[:, :, -1:, :]
        o_inter = jnp.einsum('bhtk,bhkv->bhtv', q_ * jnp.exp(G), state)
        diff = G[:, :, :, None, :] - G[:, :, None, :, :]
        decay = jnp.exp(jnp.where(causal[:, :, None], diff, -jnp.inf))
        A = jnp.einsum('bhtk,bhsk,bhtsk->bhts', q_, k_, decay)
        o = o_inter + jnp.einsum('bhts,bhsv->bhtv', A, v_)
        new_state = (jnp.exp(G_last[:, :, 0, :])[..., None] * state
                     + jnp.einsum('bhsk,bhsv->bhkv', k_ * jnp.exp(G_last - G), v_))
        return new_state, o

    s0 = jnp.zeros((B, H, dk, dv), jnp.float32)
    _, o = lax.scan(step, s0, (qc, kc, vc, gc))
    o = o.transpose(1, 0, 3, 2, 4).reshape(B, S, H, dv)
    o = rms_norm(o, gnorm) * jax.nn.silu(g.astype(jnp.float32).reshape(B, S, H, dv))
    o = o.reshape(B, S, H * dv).astype(h.dtype)
    return o @ w_out


def swa_sink_attention(h, w_in, sinks, w_out, rel_bias):
    B, S, _ = h.shape
    Hq, Hk, G, d, W = SW_Q_HEADS, SW_KV_HEADS, SW_GROUP, SW_HEAD_DIM, BLOCK
    NB = S // W
    q, k, v = jnp.split(h @ w_in, [Hq * d, Hq * d + Hk * d], axis=-1)
    q = q.reshape(B, NB, W, Hk, G, d) * (d ** -0.5)
    k = k.reshape(B, NB, W, Hk, d)
    v = v.reshape(B, NB, W, Hk, d)

    def prev_block(t):
        return jnp.concatenate([jnp.zeros_like(t[:, :1]), t[:, :-1]], axis=1)

    kb = jnp.concatenate([prev_block(k), k], axis=2)
    vb = jnp.concatenate([prev_block(v), v], axis=2)
    s = jnp.einsum('bnqhgd,bnkhd->bnhgqk', q, kb).astype(jnp.float32)
    qi = jnp.arange(W, dtype=jnp.int32)[:, None]
    kj = jnp.arange(2 * W, dtype=jnp.int32)[None, :]
    rel = qi + W - kj
    bias = rel_bias[t5_bucket(rel)].astype(jnp.float32).transpose(2, 0, 1).reshape(Hk, G, W, 2 * W)
    in_window = (rel >= 0) & (rel < SW_WINDOW)
    key_valid = (jnp.arange(NB, dtype=jnp.int32)[:, None, None] * W + kj[None] - W) >= 0
    mask = in_window[None] & key_valid
    s = jnp.where(mask[None, :, None, None], s + bias, -jnp.inf)
    sink = sinks.astype(jnp.float32).reshape(Hk, G)[None, None, :, :, None, None]
    m = jnp.maximum(jnp.max(s, axis=-1, keepdims=True), sink)
    e = jnp.exp(s - m)
    a = e / (jnp.sum(e, axis=-1, keepdims=True) + jnp.exp(sink - m))
    o = jnp.einsum('bnhgqk,bnkhd->bnqhgd', a.astype(vb.dtype), vb).reshape(B, S, Hq * d)
    return o @ w_out


def setup_inputs(seed: int = 0) -> dict:
    key = jax.random.key(seed)
    ks = jax.random.split(key, 24)
    f32 = jnp.float32
    D = D_MODEL

    def nrm(k, shape, scale):
        return jax.random.normal(k, shape, f32) * scale

    return {
        "x": nrm(ks[0], (BATCH, SEQ, D), 1.0),
        "p": nrm(ks[1], (DEPTH, BATCH, SEQ, PLE_DIM), 1.0),
        "rel_bias": nrm(ks[2], (N_BUCKETS, N_BIAS_HEADS), 0.5),
        "hg_lower_bound": nrm(ks[3], (DEPTH, HG_FDIM), 0.1),
        "da_w_in": nrm(ks[4], (N_A, D, 2 * DA_QK + DA_V), D ** -0.5),
        "da_lambda": nrm(ks[5], (N_A, 4, DA_HEAD_DIM), 0.1),
        "da_subln": 1.0 + nrm(ks[6], (N_A, 2 * DA_HEAD_DIM), 0.02),
        "da_w_out": nrm(ks[7], (N_A, DA_V, D), DA_V ** -0.5 * BETA),
        "hg_w_in": nrm(ks[8], (N_B, D, 2 * HG_FDIM + 2 * D), D ** -0.5),
        "hg_gnorm": 1.0 + nrm(ks[9], (N_B, HG_VDIM), 0.02),
        "hg_w_out": nrm(ks[10], (N_B, D, D), D ** -0.5 * BETA),
        "sw_w_in": nrm(ks[11], (N_C, D, SW_IN), D ** -0.5),
        "sw_sinks": nrm(ks[12], (N_C, SW_Q_HEADS), 0.5),
        "sw_w_out": nrm(ks[13], (N_C, SW_Q_HEADS * SW_HEAD_DIM, D), (SW_Q_HEADS * SW_HEAD_DIM) ** -0.5 * BETA),
        "ln_g": 1.0 + nrm(ks[14], (DEPTH, 2, D), 0.02),
        "ln_b": nrm(ks[15], (DEPTH, 2, D), 0.02),
        "w_up": nrm(ks[16], (DEPTH, D, D_FF), D ** -0.5),
        "w_down": nrm(ks[17], (DEPTH, D_FF, D), D_FF ** -0.5 * BETA),
        "w_ple": nrm(ks[18], (DEPTH, PLE_DIM, D), PLE_DIM ** -0.5 * 0.5),
        "w_ple_gate": nrm(ks[19], (DEPTH, D, D), D ** -0.5),
    }


def reference(x, p, rel_bias, hg_lower_bound, da_w_in, da_lambda, da_subln, da_w_out,
              hg_w_in, hg_gnorm, hg_w_out, sw_w_in, sw_sinks, sw_w_out,
              ln_g, ln_b, w_up, w_down, w_ple, w_ple_gate):
    lb_sm = jax.nn.softmax(hg_lower_bound.astype(jnp.float32), axis=0)
    lb_all = jnp.cumsum(lb_sm, axis=0) - lb_sm[0]
    h = x
    for i in range(DEPTH):
        kind, j = i % N_MIXERS, i // N_MIXERS
        if kind == 0:
            y = diff_attention(h, da_w_in[j], da_lambda[j], da_subln[j], da_w_out[j], rel_bias, i)
        elif kind == 1:
            y = hgrn2(h, hg_w_in[j], hg_gnorm[j], hg_w_out[j], lb_all[i])
        else:
            y = swa_sink_attention(h, sw_w_in[j], sw_sinks[j], sw_w_out[j], rel_bias)
        h = layer_norm(ALPHA * h + y, ln_g[i, 0], ln_b[i, 0])
        u = jnp.square(jax.nn.relu(h @ w_up[i])) @ w_down[i]
        h = layer_norm(ALPHA * h + u, ln_g[i, 1], ln_b[i, 1])
        h = h + jax.nn.sigmoid(h @ w_ple_gate[i]) * (p[i] @ w_ple[i])
    return h
```

```python
import math
from contextlib import ExitStack
import numpy as np
import concourse.bass as bass
import concourse.mybir as mybir
from concourse.bass_utils import run_bass_kernel_spmd

F32 = mybir.dt.float32
BF16 = mybir.dt.bfloat16
ALU = mybir.AluOpType
AF = mybir.ActivationFunctionType
AX = mybir.AxisListType

T = 4096
D = 1024
NT = T // 128
NG = T // 512
DEPTH = 4
ALPHA = (2 * DEPTH) ** 0.25
LN_EPS = 1e-5
RMS_EPS = 1e-6
NEG = -30000.0
DBG = {}


class Op:
    __slots__ = ("eng", "fn", "deps", "dma", "sig", "sem", "tick", "prev_dma")


class Sched:
    ENG = ["pe", "act", "dve", "pool", "sp"]
    NDS = 8

    def __init__(self):
        self.ops = {e: [] for e in self.ENG}
        self.lastw = {}
        self.readers = {}
        self.ndma = {e: 0 for e in self.ENG}
        self.pending = {e: [] for e in self.ENG}
        self.recent_dma = {e: [] for e in self.ENG}

    def fence(self):
        lasts = []
        for e in self.ENG:
            comp = [o for o in self.ops[e] if not o.dma]
            if comp:
                lasts.append(comp[-1])
            lasts.extend(self.recent_dma[e])
        for e in self.ENG:
            self.pending[e] = list(lasts)

    def add(self, eng, fn, r=(), w=(), dma=False):
        op = Op()
        op.eng, op.fn, op.dma, op.sig = eng, fn, dma, False
        op.deps = []
        op.sem = None
        op.tick = 0
        op.prev_dma = None

        def dep(d, raw):
            if d is op or d in op.deps:
                return
            if (not dma) and (not d.dma) and d.eng == eng:
                if eng == "pe" or not raw:
                    return
            op.deps.append(d)

        for k in r:
            lw = self.lastw.get(k)
            if lw is not None:
                dep(lw, True)
        for k in w:
            lw = self.lastw.get(k)
            if lw is not None:
                dep(lw, False)
            for rd in self.readers.get(k, ()):
                dep(rd, False)
        for k in r:
            self.readers.setdefault(k, []).append(op)
        for k in w:
            self.lastw[k] = op
            self.readers[k] = []
        if self.pending[eng]:
            for d in self.pending[eng]:
                if d is op or d in op.deps:
                    continue
                if (not dma) and (not d.dma) and d.eng == eng:
                    continue
                op.deps.append(d)
            self.pending[eng] = []
        for d in op.deps:
            d.sig = True
        if dma:
            op.sig = True
            self.recent_dma[eng] = (self.recent_dma[eng] + [op])[-self.NDS:]
        self.ops[eng].append(op)
        return op

    def emit(self, nc, stack):
        esem = {e: stack.enter_context(nc.semaphore("s_" + e)) for e in self.ENG}
        dsem = {e: [stack.enter_context(nc.semaphore("d_%s%d" % (e, i))) for i in range(self.NDS)]
                for e in self.ENG}
        fin = stack.enter_context(nc.semaphore("fin"))
        for e in self.ENG:
            cnt = 0
            nd = 0
            for op in self.ops[e]:
                if op.dma:
                    op.sem = dsem[e][nd % self.NDS]
                    op.tick = 16 * (nd // self.NDS + 1)
                    nd += 1
                elif op.sig:
                    cnt += 1
                    op.sem = esem[e]
                    op.tick = cnt
        block = stack.enter_context(nc.Block())

        def run(e, eng):
            known = {}
            for op in self.ops[e]:
                waits = [(d.sem, d.tick) for d in op.deps]
                if op.dma and op.tick > 16:
                    waits.append((op.sem, op.tick - 16))
                for sem, val in waits:
                    if known.get(id(sem), 0) >= val:
                        continue
                    eng.wait_ge(sem, val)
                    known[id(sem)] = val
                ins = op.fn(eng)
                if op.dma:
                    ins.then_inc(op.sem, 16)
                elif op.sig:
                    ins.then_inc(op.sem, 1)
            nd = 0
            last = {}
            for op in self.ops[e]:
                if op.dma:
                    last[id(op.sem)] = (op.sem, op.tick)
            for sem, val in last.values():
                if known.get(id(sem), 0) < val:
                    eng.wait_ge(sem, val)

        @block.tensor
        def _(eng):
            run("pe", eng)

        @block.scalar
        def _(eng):
            run("act", eng)

        @block.vector
        def _(eng):
            run("dve", eng)

        @block.gpsimd
        def _(eng):
            run("pool", eng)

        @block.sync
        def _(eng):
            run("sp", eng)


class Rot:
    def __init__(self, name, tiles):
        self.name, self.tiles, self.i = name, tiles, 0

    def next(self):
        i = self.i % len(self.tiles)
        self.i += 1
        return (self.name, i), self.tiles[i]


def t5_bucket_np(rel):
    n = np.maximum(rel, 0)
    nf = np.maximum(n, 1).astype(np.float32)
    large = 16 + (np.log(nf / np.float32(16)) / np.float32(math.log(8.0)) * np.float32(16)).astype(np.int32)
    large = np.minimum(large, 31)
    return np.where(n < 16, n, large)


def build_program(layers, in_from_x=True):
    nc = bass.Bass("TRN2", target_bir_lowering=False)
    S = Sched()
    st = ExitStack()

    def din(name, shape, dt=F32):
        return nc.dram_tensor(name, list(shape), dt, kind="ExternalInput").ap()

    def dscr(name, shape, dt):
        return nc.dram_tensor(name, list(shape), dt, kind="Internal").ap()

    x_in = din("x", [T, D])
    p_in = din("p", [DEPTH, T, 256])
    out_d = nc.dram_tensor("out", [T, D], F32, kind="ExternalOutput").ap()
    ident_d = din("ident", [128, 128])
    tabD_d = din("tabD", [128, 16, 128])
    tabP_d = din("tabP", [128, 16, 128])
    c31_d = din("c31", [128, 16])
    swT_d = din("swT", [128, 16, 256])
    esk_d = din("sinks", [128, 16])
    hgc_d = din("hgc", [128, 4, 128])
    lnb_d = din("lnbc", [DEPTH, 4, 128, D])
    lam_d = din("lam", [2, 128, 256])
    subln_d = din("subln", [2, 128, 128])
    gn_d = din("gnorm", [128, D])
    lb_d = din("hglb", [128, 4, D])
    w = {}
    w["da_in"] = din("da_w_in", [2, D, 3072])
    w["da_out"] = din("da_w_out", [2, D, D])
    w["hg_in"] = din("hg_w_in", [1, D, 4096])
    w["hg_out"] = din("hg_w_out", [1, D, D])
    w["sw_in"] = din("sw_w_in", [1, D, 1280])
    w["sw_out"] = din("sw_w_out", [1, D, D])
    w["up"] = din("w_up", [DEPTH, D, 4096])
    w["down"] = din("w_down", [DEPTH, 4096, D])
    w["ple"] = din("w_ple", [DEPTH, 256, D])
    w["gate"] = din("w_ple_gate", [DEPTH, D, D])

    hbuf = dscr("hbuf", [T, D], F32)
    obuf = dscr("obuf", [T, D], BF16)
    qT_d = dscr("qT", [8, 128, T], BF16)
    kT_d = dscr("kT", [8, 128, T], BF16)
    s1_d = dscr("s1", [T, D], BF16)
    s2_d = dscr("s2", [T, D], BF16)
    s3_d = dscr("s3", [T, D], BF16)
    s4_d = dscr("s4", [T, D], BF16)
    lf_d = dscr("lf", [T, D], F32)

    def sb(name, shape, dt):
        return st.enter_context(nc.sbuf_tensor("sb_" + name, list(shape), dt))

    def ps(name, shape, dt):
        return st.enter_context(nc.psum_tensor("ps_" + name, list(shape), dt))

    mm = Rot("mm", [ps("mm%d" % i, [128, 512], F32) for i in range(2)])
    acc = [ps("acc%d" % i, [128, 512], F32) for i in range(4)]
    trb = ps("trb", [128, 1024], BF16)
    aux = ps("aux", [128, 512], F32)

    ident = sb("ident", [128, 128], BF16)
    identf = sb("identf", [128, 128], F32)
    c31 = sb("c31", [128, 16], F32)
    epsT = sb("epsT", [128, 2], F32)
    zeros = sb("zeros", [128, 512], BF16)
    wsl = Rot("w", [sb("w%d" % i, [128, 8, 512], BF16) for i in range(3)])
    ev512 = Rot("ev", [sb("ev%d" % i, [128, 512], BF16) for i in range(4)])
    evf = Rot("evf", [sb("evf%d" % i, [128, 512], F32) for i in range(3)])
    small = Rot("sm", [sb("sm%d" % i, [128, 16], F32) for i in range(6)])
    ARENA_N = 73728
    arena_t = sb("arena", [128, ARENA_N], BF16)

    class Arena:
        off = 0
        cnt = 0

    def take(shape, dt):
        n = 1
        for s_ in shape[1:]:
            n *= s_
        ne = n * 2 if dt == F32 else n
        ap = arena_t[:, Arena.off:Arena.off + ne]
        Arena.off += ne
        assert Arena.off <= ARENA_N, Arena.off
        if dt == F32:
            ap = ap.bitcast(F32)
        if len(shape) == 3:
            ap = ap.rearrange("p (a b) -> p a b", b=shape[2])
        return ap

    def takes(name, n, shape, dt):
        Arena.cnt += 1
        return Rot("%s_%d" % (name, Arena.cnt), [take(shape, dt) for _ in range(n)])

    com = {}

    def alloc_common():
        com["stg_f"] = takes("stgf", 1, [128, 4, D], F32)
        com["stg_b"] = takes("stgb", 2, [128, 4, D], BF16)
        com["xT"] = takes("xT", 2, [128, 8, 512], BF16)

    def new_phase(mark):
        S.fence()
        Arena.off = mark

    const_ops = []

    def dma(eng, out, in_, r, wk):
        return S.add(eng, lambda e, o=out, i=in_: e.dma_start(out=o, in_=i), r=r, w=wk, dma=True)

    dma("sp", identf[:], ident_d[:, :], [], [("identf",)])
    S.add("dve", lambda e: e.tensor_copy(out=ident[:], in_=identf[:]), r=[("identf",)], w=[("ident",)])
    dma("sp", c31[:], c31_d[:, :], [], [("c31",)])
    S.add("pool", lambda e: e.memset(zeros[:], 0.0), r=[], w=[("zeros",)])
    S.add("pool", lambda e: e.memset(epsT[:, 0:1], LN_EPS), r=[], w=[("eps",)])
    S.add("pool", lambda e: e.memset(epsT[:, 1:2], RMS_EPS), r=[], w=[("eps",)])

    def transposes(src_tile, src_key, nkc, xt_tile, xt_key, tt):
        for kc in range(nkc):
            S.add("pe", lambda e, kc=kc: e.transpose(trb[:, kc * 128:(kc + 1) * 128],
                                                     src_tile[:, tt, kc * 128:(kc + 1) * 128], ident[:]),
                  r=[src_key, ("ident",)], w=[("trb", kc)])
        S.add("dve", lambda e: e.tensor_copy(
            out=xt_tile[:, 0:nkc, tt * 128:(tt + 1) * 128],
            in_=trb[:, 0:nkc * 128].rearrange("p (k t) -> p k t", t=128)),
            r=[("trb", kc) for kc in range(nkc)], w=[xt_key])

    def load_T(src_rows_ap, src_keys, is_f32, nfeat=D):
        nkc = nfeat // 128
        bkey, bt = com["stg_b"].next()
        if is_f32:
            fkey, ft = com["stg_f"].next()
            dma("sp", ft[:, :, 0:nfeat], src_rows_ap.rearrange("(t p) d -> p t d", p=128), src_keys, [fkey])
            for tt in range(4):
                S.add("act", lambda e, tt=tt: e.copy(out=bt[:, tt, 0:nfeat], in_=ft[:, tt, 0:nfeat]),
                      r=[fkey], w=[(bkey, tt)])
        else:
            dma("sp", bt[:, :, 0:nfeat], src_rows_ap.rearrange("(t p) d -> p t d", p=128), src_keys,
                [(bkey, tt) for tt in range(4)])
        xkey, xt = com["xT"].next()
        for tt in range(4):
            transposes(bt, (bkey, tt), nkc, xt, (xkey, tt), tt)
        return xkey, xt

    def proj(xt, xkeys, KC, wd, N, mode, evac, lhs_fn=None):
        npieces = max(1, KC // 8)
        kper = min(KC, 8)
        c0 = 0
        while c0 < N:
            ncols = min(512, N - c0)
            for kp in range(npieces):
                wkey, wt = wsl.next()
                dma("pool", wt[:, 0:kper, 0:ncols],
                    wd[kp * 1024:kp * 1024 + kper * 128, c0:c0 + ncols].rearrange("(c p) n -> p c n", p=128),
                    [], [wkey])
                if mode == "tok":
                    for tt in range(4):
                        pk = [("acc", tt, 0), ("acc", tt, 1)]
                        for k in range(kper):
                            kk = kp * 8 + k
                            S.add("pe", lambda e, tt=tt, k=k, kk=kk, wt=wt, ncols=ncols, kp=kp: e.matmul(
                                acc[tt][:, 0:ncols], xt[:, kk, tt * 128:(tt + 1) * 128], wt[:, k, 0:ncols],
                                start=(kk == 0), stop=(kk == KC - 1)),
                                r=[wkey] + xkeys, w=pk)
                        if kp == npieces - 1:
                            evac(tt, c0, ncols, acc[tt], pk)
                else:
                    for sbk in range(ncols // 128):
                        pk, pt = mm.next()
                        for k in range(kper):
                            S.add("pe", lambda e, k=k, sbk=sbk, wt=wt, pt=pt: e.matmul(
                                pt[:, :], wt[:, k, sbk * 128:(sbk + 1) * 128], xt[:, k, :],
                                start=(k == 0), stop=(k == kper - 1)),
                                r=[wkey] + xkeys, w=[pk])
                        evac((c0 // 128) + sbk, pt, [pk])
            c0 += ncols

    def store(dst_ap, src_ap, rkeys, wkeys, eng="sp"):
        dma(eng, dst_ap, src_ap, rkeys, wkeys)

    def layer_norm(ht, hkey, tt, gi):
        k1, sm = small.next()
        lnbc = com["lnbc"]
        xin = ht[:, tt, :]
        S.add("dve", lambda e: e.bn_stats(out=sm[:, 0:6], in_=ht[:, tt, 0:512]), r=[(hkey, tt)], w=[k1])
        k2, sm2 = small.next()
        S.add("dve", lambda e: e.bn_stats(out=sm2[:, 0:6], in_=ht[:, tt, 512:1024]), r=[(hkey, tt)], w=[k2])
        k3, sm3 = small.next()
        S.add("dve", lambda e: e.tensor_copy(out=sm3[:, 0:6], in_=sm[:, 0:6]), r=[k1], w=[k3])
        S.add("dve", lambda e: e.tensor_copy(out=sm3[:, 6:12], in_=sm2[:, 0:6]), r=[k2], w=[k3])
        k4, sm4 = small.next()
        S.add("dve", lambda e: e.bn_aggr(out=sm4[:, 0:2], in_=sm3[:, 0:12]), r=[k3], w=[k4])
        S.add("act", lambda e: e.activation(out=sm4[:, 3:4], in_=sm4[:, 1:2], func=AF.Sqrt, bias=epsT[:, 0:1],
                                            scale=1.0), r=[k4, ("eps",)], w=[(k4, "s")])
        S.add("dve", lambda e: e.reciprocal(out=sm4[:, 2:3], in_=sm4[:, 3:4]), r=[(k4, "s")], w=[k4])
        S.add("dve", lambda e: e.tensor_scalar(out=xin, in0=xin, scalar1=sm4[:, 0:1], scalar2=sm4[:, 2:3],
                                               op0=ALU.subtract, op1=ALU.mult), r=[k4, (hkey, tt)], w=[(hkey, tt)])
        S.add("pool", lambda e: e.tensor_tensor(out=xin, in0=xin, in1=lnbc[:, gi, :], op=ALU.mult),
              r=[(hkey, tt), ("lnbc",)], w=[(hkey, tt)])
        S.add("pool", lambda e: e.tensor_tensor(out=xin, in0=xin, in1=lnbc[:, gi + 1, :], op=ALU.add),
              r=[(hkey, tt), ("lnbc",)], w=[(hkey, tt)])

    def phase_c(li, w_out_ap, h_src, h_dst):
        new_phase(Arena.lmark)
        alloc_common()
        stg_b, xT = com["stg_b"], com["xT"]
        uT = take([128, 32, 512], BF16)
        gate = take([128, 4, D], F32)
        lnbc = take([128, 4, D], F32)
        com["lnbc"] = lnbc
        hres = takes("hres", 1, [128, 4, D], F32)
        dma("sp", lnbc, lnb_d[li].rearrange("g p d -> p g d"), [], [("lnbc",)])
        for tg in range(NG):
            rows = slice(tg * 512, (tg + 1) * 512)
            okeys = [("obuf", tg * 4 + tt) for tt in range(4)]
            xk, xt = load_T(obuf[rows, :], okeys, False)
            hkey, ht = hres.next()
            dma("sp", ht, h_src[rows, :].rearrange("(t p) d -> p t d", p=128),
                [("hbuf", tg)], [(hkey, tt) for tt in range(4)])

            def ev_res(tt, c0, ncols, pt, pk, ht=ht, hkey=hkey):
                S.add("dve", lambda e: e.scalar_tensor_tensor(
                    out=ht[:, tt, c0:c0 + ncols], in0=ht[:, tt, c0:c0 + ncols], scalar=ALPHA,
                    in1=pt[:, 0:ncols], op0=ALU.mult, op1=ALU.add), r=pk + [(hkey, tt)], w=[(hkey, tt)])

            proj(xt, [(xk, tt) for tt in range(4)], 8, w_out_ap, D, "tok", ev_res)

            def norm_and_T(gi, ht=ht, hkey=hkey):
                bkey, bt = stg_b.next()
                for tt in range(4):
                    layer_norm(ht, hkey, tt, gi)
                    S.add("act", lambda e, tt=tt: e.copy(out=bt[:, tt, :], in_=ht[:, tt, :]),
                          r=[(hkey, tt)], w=[(bkey, tt)])
                xk2, xt2 = xT.next()
                for tt in range(4):
                    transposes(bt, (bkey, tt), 8, xt2, (xk2, tt), tt)
                return xk2, xt2

            xk1, xt1 = norm_and_T(0)

            def ev_up(fc, pt, pk):
                fk, ft = evf.next()
                S.add("act", lambda e: e.activation(out=ft[:], in_=pt[:, :], func=AF.Relu), r=pk, w=[fk])
                S.add("dve", lambda e: e.tensor_tensor(out=uT[:, fc, :], in0=ft[:], in1=ft[:], op=ALU.mult),
                      r=[fk], w=[("uT", fc)])

            proj(xt1, [(xk1, tt) for tt in range(4)], 8, w["up"][li], 4096, "feat", ev_up)
            proj(uT, [("uT", fc) for fc in range(32)], 32, w["down"][li], D, "tok", ev_res)
            xk2, xt2 = norm_and_T(2)

            def ev_gate(tt, c0, ncols, pt, pk):
                S.add("act", lambda e: e.activation(out=gate[:, tt, c0:c0 + ncols], in_=pt[:, 0:ncols],
                                                    func=AF.Sigmoid), r=pk, w=[("gate", tt)])

            proj(xt2, [(xk2, tt) for tt in range(4)], 8, w["gate"][li], D, "tok", ev_gate)
            pk_, pxt = load_T(p_in[li, rows, :], [], True, nfeat=256)

            def ev_ple(tt, c0, ncols, pt, pk, ht=ht, hkey=hkey):
                fk, ft = evf.next()
                S.add("dve", lambda e: e.tensor_tensor(out=ft[:, 0:ncols], in0=pt[:, 0:ncols],
                                                       in1=gate[:, tt, c0:c0 + ncols], op=ALU.mult),
                      r=pk + [("gate", tt)], w=[fk])
                S.add("pool", lambda e: e.tensor_tensor(out=ht[:, tt, c0:c0 + ncols], in0=ht[:, tt, c0:c0 + ncols],
                                                        in1=ft[:, 0:ncols], op=ALU.add),
                      r=[fk, (hkey, tt)], w=[(hkey, tt)])

            proj(pxt, [(pk_, tt) for tt in range(4)], 2, w["ple"][li], D, "tok", ev_ple)
            store(h_dst[rows, :].rearrange("(t p) d -> p t d", p=128), ht,
                  [(hkey, tt) for tt in range(4)], [("hbuf", tg)] if h_dst is hbuf else [("outd", tg)])

    def swa_layer(li, j, h_src, h_dst):
        new_phase(0)
        Arena.lmark = Arena.off
        alloc_common()
        for tg in range(NG):
            rows = slice(tg * 512, (tg + 1) * 512)
            xk, xt = load_T(h_src[rows, :], [("hbuf", tg)], True)

            def ev(tt, c0, ncols, pt, pk, tg=tg):
                ek, et = ev512.next()
                r0 = tg * 512 + tt * 128
                if c0 < 1024:
                    S.add("act", lambda e: e.activation(out=et[:, 0:ncols], in_=pt[:, 0:ncols], func=AF.Copy,
                                                        scale=0.125), r=pk, w=[ek])
                    store(s1_d[r0:r0 + 128, c0:c0 + ncols], et[:, 0:ncols], [ek], [("s1", tg * 4 + tt, c0)])
                else:
                    S.add("act", lambda e: e.copy(out=et[:, 0:ncols], in_=pt[:, 0:ncols]), r=pk, w=[ek])
                    store(s2_d[r0:r0 + 128, 0:ncols], et[:, 0:ncols], [ek], [("s2", tg * 4 + tt)])

            proj(xt, [(xk, tt) for tt in range(4)], 8, w["sw_in"][j], 1280, "tok", ev)
        new_phase(Arena.lmark)
        swT = take([128, 16, 256], F32)
        esk = take([128, 16], F32)
        qTt = take([128, 8, 128], BF16)
        kdup = take([128, 4, 128], BF16)
        kTd = [take([128, 4, 128], BF16) for i in range(2)]
        vx = [take([128, 2, 66], BF16) for i in range(2)]
        osb = takes("sw_o", 2, [128, D], BF16)
        qin = takes("sw_qin", 2, [128, D], BF16)
        kvin = takes("sw_kvin", 2, [128, 256], BF16)
        ptb = takes("sw_pt", 3, [128, 256], BF16)
        dma("sp", swT, swT_d[:, :, :], [], [("swT",)])
        dma("sp", esk, esk_d[:, :], [], [("esk",)])
        S.add("act", lambda e: e.activation(out=esk, in_=esk, func=AF.Exp), r=[("esk",)], w=[("esk2",)])
        for i in range(2):
            S.add("pool", lambda e, i=i: e.memset(vx[i][:, :, 64:65], 1.0), r=[], w=[("sw_vx", i)])
        S.add("pool", lambda e: e.memset(kdup[:], 0.0), r=[], w=[("sw_kdup", kh) for kh in range(2)])
        for t in range(NT):
            cur, prv = t % 2, (t + 1) % 2
            qk, qt = qin.next()
            dma("sp", qt[:], s1_d[t * 128:(t + 1) * 128, :], [("s1", t, 0), ("s1", t, 512)], [qk])
            kvk, kvt = kvin.next()
            dma("sp", kvt[:], s2_d[t * 128:(t + 1) * 128, 0:256], [("s2", t)], [kvk])
            for hc in range(8):
                S.add("pe", lambda e, hc=hc, qt=qt: e.transpose(trb[:, hc * 128:(hc + 1) * 128],
                                                               qt[:, hc * 128:(hc + 1) * 128], ident[:]),
                      r=[qk, ("ident",)], w=[("trb", hc)])
            S.add("dve", lambda e: e.tensor_copy(out=qTt[:], in_=trb[:, :].rearrange("p (k t) -> p k t", t=128)),
                  r=[("trb", hc) for hc in range(8)], w=[("sw_qT",)])
            for kh in range(2):
                for r_ in range(2):
                    S.add("pool", lambda e, kh=kh, r_=r_, kvt=kvt: e.tensor_copy(
                        out=kdup[:, kh * 2 + r_, r_ * 64:(r_ + 1) * 64], in_=kvt[:, kh * 64:(kh + 1) * 64]),
                        r=[kvk], w=[("sw_kdup", kh)])
            for k4 in range(4):
                S.add("pe", lambda e, k4=k4: e.transpose(trb[:, k4 * 128:(k4 + 1) * 128], kdup[:, k4, :], ident[:]),
                      r=[("sw_kdup", k4 // 2), ("ident",)], w=[("trb", k4)])
            S.add("dve", lambda e, cur=cur: e.tensor_copy(
                out=kTd[cur][:], in_=trb[:, 0:512].rearrange("p (k t) -> p k t", t=128)),
                r=[("trb", k4) for k4 in range(4)], w=[("sw_kTd", cur)])
            S.add("act", lambda e, cur=cur, kvt=kvt: e.copy(
                out=vx[cur][:, :, 0:64], in_=kvt[:, 128:256].rearrange("p (k d) -> p k d", d=64)),
                r=[kvk], w=[("sw_vx", cur)])
            ok, ot = osb.next()
            nb = [6, 6, 4]
            for hq in range(16):
                kvh = hq // 8
                base = (hq % 2) * 64
                nkeys = 256 if t > 0 else 128
                pk, pt = mm.next()
                S.add("pe", lambda e, pt=pt, base=base, kvh=kvh, hq=hq, cur=cur: e.matmul(
                    pt[:, 0:128], kTd[cur][:, kvh * 2 + hq % 2, :], qTt[:, hq // 2, :],
                    start=True, stop=True), r=[("sw_kTd", cur), ("sw_qT",)], w=[pk])
                if t > 0:
                    S.add("pe", lambda e, pt=pt, base=base, kvh=kvh, hq=hq, prv=prv: e.matmul(
                        pt[:, 128:256], kTd[prv][:, kvh * 2 + hq % 2, :], qTt[:, hq // 2, :],
                        start=True, stop=True), r=[("sw_kTd", prv), ("sw_qT",)], w=[pk])
                fk, ft = evf.next()
                S.add("dve", lambda e, pt=pt, ft=ft, hq=hq, nkeys=nkeys: e.tensor_tensor(
                    out=ft[:, 0:nkeys], in0=pt[:, 0:nkeys], in1=swT[:, hq, 0:nkeys], op=ALU.add),
                    r=[pk, ("swT",)], w=[fk])
                ptk, ptt = ptb.next()
                S.add("act", lambda e, ft=ft, ptt=ptt, nkeys=nkeys: e.activation(
                    out=ptt[:, 0:nkeys], in_=ft[:, 0:nkeys], func=AF.Exp), r=[fk], w=[ptk])
                bank, slot = hq // 6, hq % 6
                S.add("pe", lambda e, ptt=ptt, bank=bank, slot=slot, kvh=kvh, cur=cur, t=t: e.matmul(
                    acc[bank][:, slot * 65:(slot + 1) * 65], ptt[:, 0:128], vx[cur][:, kvh, 0:65],
                    start=True, stop=(t == 0)), r=[ptk, ("sw_vx", cur)], w=[("acc", bank, 0), ("acc", bank, 1)])
                if t > 0:
                    S.add("pe", lambda e, ptt=ptt, bank=bank, slot=slot, kvh=kvh, prv=prv: e.matmul(
                        acc[bank][:, slot * 65:(slot + 1) * 65], ptt[:, 128:256], vx[prv][:, kvh, 0:65],
                        start=False, stop=True), r=[ptk, ("sw_vx", prv)], w=[("acc", bank, 0), ("acc", bank, 1)])
            for bank in range(3):
                n = nb[bank]
                sk, sm = small.next()
                av = acc[bank][:, 0:n * 65].rearrange("p (h d) -> p h d", d=65)
                S.add("dve", lambda e, av=av, sm=sm, n=n, bank=bank: e.tensor_tensor(
                    out=sm[:, 0:n], in0=av[:, :, 64], in1=esk[:, bank * 6:bank * 6 + n], op=ALU.add),
                    r=[("acc", bank, 0), ("acc", bank, 1), ("esk2",)], w=[sk])
                S.add("dve", lambda e, sm=sm, n=n: e.reciprocal(out=sm[:, 8:8 + n], in_=sm[:, 0:n]), r=[sk], w=[sk])
                for s_ in range(n):
                    hq = bank * 6 + s_
                    S.add("dve", lambda e, av=av, sm=sm, s_=s_, hq=hq, ot=ot: e.tensor_scalar(
                        out=ot[:, hq * 64:(hq + 1) * 64], in0=av[:, s_, 0:64], scalar1=sm[:, 8 + s_:9 + s_],
                        scalar2=None, op0=ALU.mult), r=[("acc", bank, 0), ("acc", bank, 1), sk], w=[ok])
            store(obuf[t * 128:(t + 1) * 128, :], ot[:], [ok], [("obuf", t)])
        phase_c(li, w["sw_out"][j], h_src, h_dst)

    def da_layer(li, j, h_src, h_dst):
        lam_init = 0.8 - 0.6 * math.exp(-0.3 * li)
        new_phase(0)
        lamt = take([128, 256], F32)
        lsm = take([128, 8], F32)
        sublnS = take([128, 128], F32)
        Arena.lmark = Arena.off
        alloc_common()
        dma("sp", lamt, lam_d[j], [], [("lamt",)])
        dma("sp", sublnS, subln_d[j], [], [("subln",)])
        S.add("dve", lambda e: e.tensor_tensor(out=lamt[:, 0:64], in0=lamt[:, 0:64], in1=lamt[:, 64:128],
                                               op=ALU.mult), r=[("lamt",)], w=[("lamt",)])
        S.add("dve", lambda e: e.tensor_tensor(out=lamt[:, 128:192], in0=lamt[:, 128:192], in1=lamt[:, 192:256],
                                               op=ALU.mult), r=[("lamt",)], w=[("lamt",)])
        S.add("dve", lambda e: e.tensor_reduce(out=lsm[:, 0:1], in_=lamt[:, 0:64], axis=AX.X, op=ALU.add),
              r=[("lamt",)], w=[("lsm",)])
        S.add("dve", lambda e: e.tensor_reduce(out=lsm[:, 1:2], in_=lamt[:, 128:192], axis=AX.X, op=ALU.add),
              r=[("lamt",)], w=[("lsm",)])
        S.add("act", lambda e: e.activation(out=lsm[:, 2:4], in_=lsm[:, 0:2], func=AF.Exp),
              r=[("lsm",)], w=[("lsm2",)])
        S.add("dve", lambda e: e.scalar_tensor_tensor(out=lsm[:, 4:5], in0=lsm[:, 3:4], scalar=-lam_init,
                                                      in1=lsm[:, 2:3], op0=ALU.add, op1=ALU.subtract),
              r=[("lsm2",)], w=[("lsm3",)])
        S.add("dve", lambda e: e.tensor_scalar(out=sublnS[:], in0=sublnS[:], scalar1=(1.0 - lam_init),
                                               scalar2=None, op0=ALU.mult), r=[("subln",)], w=[("subln",)])
        for tg in range(NG):
            rows = slice(tg * 512, (tg + 1) * 512)
            xk, xt = load_T(h_src[rows, :], [("hbuf", tg)], True)
            xkeys = [(xk, tt) for tt in range(4)]

            def ev_qk(fc, pt, pk, tg=tg):
                ek, et = ev512.next()
                if fc < 8:
                    S.add("act", lambda e: e.activation(out=et[:], in_=pt[:, :], func=AF.Copy, scale=0.125),
                          r=pk, w=[ek])
                    store(qT_d[fc, :, tg * 512:(tg + 1) * 512], et[:], [ek], [("qT", fc, tg)])
                else:
                    S.add("act", lambda e: e.copy(out=et[:], in_=pt[:, :]), r=pk, w=[ek])
                    store(kT_d[fc - 8, :, tg * 512:(tg + 1) * 512], et[:], [ek], [("kT", fc - 8, tg)])

            proj(xt, xkeys, 8, w["da_in"][j][:, 0:2048], 2048, "feat", ev_qk)

            def ev_v(tt, c0, ncols, pt, pk, tg=tg):
                ek, et = ev512.next()
                r0 = tg * 512 + tt * 128
                S.add("act", lambda e: e.copy(out=et[:, 0:ncols], in_=pt[:, 0:ncols]), r=pk, w=[ek])
                store(s1_d[r0:r0 + 128, c0:c0 + ncols], et[:, 0:ncols], [ek], [("s1", tg * 4 + tt, c0)])

            proj(xt, xkeys, 8, w["da_in"][j][:, 2048:3072], 1024, "tok", ev_v)
        new_phase(Arena.lmark)
        tabD = take([128, 16, 128], F32)
        tabP = take([128, 16, 128], F32)
        kTs = [[take([128, T], BF16) for c in range(2)] for i in range(2)]
        vxs = [take([128, 32, 132], BF16) for i in range(2)]
        qTg = takes("da_q", 2, [128, 512], BF16)
        ptb = takes("da_pt", 4, [128, 512], BF16)
        osb = takes("da_o", 3, [128, 128], BF16)
        tmpf = takes("da_tf", 4, [128, 128], F32)
        dma("sp", tabD, tabD_d[:, :, :], [], [("tabD",)])
        dma("sp", tabP, tabP_d[:, :, :], [], [("tabP",)])
        for i in range(2):
            S.add("pool", lambda e, i=i: e.memset(vxs[i][:, :, 128:129], 1.0), r=[], w=[("da_kv", i)])
            S.add("pool", lambda e, i=i: e.memset(kTs[i][0][64:128, :], 0.0), r=[], w=[("da_kv", i)])
            S.add("pool", lambda e, i=i: e.memset(kTs[i][1][0:64, :], 0.0), r=[], w=[("da_kv", i)])
        for h in range(DBG.get("da_heads", 8)):
            sl = h % 2
            kkey = ("da_kv", sl)
            dma("sp", kTs[sl][0][0:64, :], kT_d[h, 0:64, :], [("kT", h, tg) for tg in range(NG)], [kkey])
            dma("sp", kTs[sl][1][64:128, :], kT_d[h, 64:128, :], [("kT", h, tg) for tg in range(NG)], [kkey])
            for q4 in range(NT // 8):
                dma("sp", vxs[sl][:, q4 * 8:(q4 + 1) * 8, 0:128],
                    s1_d[q4 * 1024:(q4 + 1) * 1024, h * 128:(h + 1) * 128].rearrange("(t p) v -> p t v", p=128),
                    [("s1", t, c0) for t in range(q4 * 8, q4 * 8 + 8) for c0 in (0, 512)], [kkey])
            for g in range(DBG.get("da_groups", NG)):
                qk, qt = qTg.next()
                dma("sp", qt[:], qT_d[h, :, g * 512:(g + 1) * 512], [("qT", h, g)], [qk])
                for bank in range(4):
                    S.add("pe", lambda e, bank=bank: e.matmul(acc[bank][:, :], zeros[:, 0:128], zeros[:, :],
                                                              start=True, stop=False),
                          r=[("zeros",)], w=[("acc", bank, 0), ("acc", bank, 1)])
                for c in range(2):
                    m = 2 * h + c
                    for kb in range(4 * g + 4):
                        j0 = max(0, kb - 4 * g)
                        pk, pt = mm.next()
                        S.add("pe", lambda e, pt=pt, c=c, kb=kb, j0=j0, qt=qt, sl=sl: e.matmul(
                            pt[:, j0 * 128:512], kTs[sl][c][:, kb * 128:(kb + 1) * 128],
                            qt[:, j0 * 128:512], start=True, stop=True),
                            r=[kkey, qk], w=[pk])
                        if DBG.get("da_stage", 4) < 2:
                            continue
                        ptk, ptt = ptb.next()
                        jc = j0
                        for jj in range(j0, 4):
                            dlt = 4 * g + jj - kb
                            if dlt >= 2:
                                break
                            tab = tabD if dlt == 0 else tabP
                            fk, ft = tmpf.next()
                            S.add("dve", lambda e, pt=pt, ft=ft, jj=jj, tab=tab, m=m: e.tensor_tensor(
                                out=ft[:], in0=pt[:, jj * 128:(jj + 1) * 128], in1=tab[:, m, :], op=ALU.add),
                                r=[pk, ("tabD",), ("tabP",)], w=[fk])
                            S.add("act", lambda e, ft=ft, ptt=ptt, jj=jj: e.activation(
                                out=ptt[:, jj * 128:(jj + 1) * 128], in_=ft[:], func=AF.Exp),
                                r=[fk], w=[(ptk, jj)])
                            jc = jj + 1
                        if jc < 4:
                            S.add("act", lambda e, pt=pt, ptt=ptt, jc=jc, m=m: e.activation(
                                out=ptt[:, jc * 128:512], in_=pt[:, jc * 128:512], func=AF.Exp,
                                bias=c31[:, m:m + 1], scale=1.0),
                                r=[pk, ("c31",)], w=[(ptk, jj) for jj in range(jc, 4)])
                        for jj in range(j0, 4 if DBG.get("da_stage", 4) >= 3 else 0):
                            bank = 2 * c + jj // 2
                            off = (jj % 2) * 256
                            S.add("pe", lambda e, ptt=ptt, jj=jj, bank=bank, off=off, kb=kb, sl=sl, g=g: e.matmul(
                                acc[bank][:, off:off + 129], ptt[:, jj * 128:(jj + 1) * 128], vxs[sl][:, kb, 0:129],
                                start=False, stop=(kb == 4 * g + jj)),
                                r=[(ptk, jj), kkey], w=[("acc", bank, jj % 2)])
                for jj in range(4 if DBG.get("da_stage", 4) >= 4 else 0):
                    a1 = acc[jj // 2][:, (jj % 2) * 256:(jj % 2) * 256 + 129]
                    a2 = acc[2 + jj // 2][:, (jj % 2) * 256:(jj % 2) * 256 + 129]
                    k1 = ("acc", jj // 2, jj % 2)
                    k2 = ("acc", 2 + jj // 2, jj % 2)
                    sk, sm = small.next()
                    S.add("dve", lambda e, a1=a1, sm=sm: e.reciprocal(out=sm[:, 0:1], in_=a1[:, 128:129]),
                          r=[k1], w=[sk])
                    S.add("dve", lambda e, a2=a2, sm=sm: e.reciprocal(out=sm[:, 1:2], in_=a2[:, 128:129]),
                          r=[k2], w=[sk])
                    S.add("dve", lambda e, sm=sm: e.tensor_tensor(out=sm[:, 2:3], in0=sm[:, 1:2], in1=lsm[:, 4:5],
                                                                  op=ALU.mult), r=[sk, ("lsm3",)], w=[sk])
                    fk, ft = tmpf.next()
                    S.add("dve", lambda e, a2=a2, sm=sm, ft=ft: e.tensor_scalar(
                        out=ft[:], in0=a2[:, 0:128], scalar1=sm[:, 2:3], scalar2=None, op0=ALU.mult),
                        r=[k2, sk], w=[fk])
                    fk2, ft2 = tmpf.next()
                    S.add("dve", lambda e, a1=a1, sm=sm, ft=ft, ft2=ft2: e.scalar_tensor_tensor(
                        out=ft2[:], in0=a1[:, 0:128], scalar=sm[:, 0:1], in1=ft[:], op0=ALU.mult, op1=ALU.add),
                        r=[k1, sk, fk], w=[fk2])
                    S.add("act", lambda e, ft=ft, ft2=ft2, sm=sm: e.activation(
                        out=ft[:], in_=ft2[:], func=AF.Square, accum_out=sm[:, 3:4]), r=[fk2, sk], w=[fk, (sk, "b")])
                    S.add("act", lambda e, sm=sm: e.activation(
                        out=sm[:, 4:5], in_=sm[:, 3:4], func=AF.Sqrt, bias=epsT[:, 1:2], scale=1.0 / 128.0),
                        r=[(sk, "b"), ("eps",)], w=[(sk, "c")])
                    S.add("dve", lambda e, sm=sm: e.reciprocal(out=sm[:, 5:6], in_=sm[:, 4:5]),
                          r=[(sk, "c")], w=[(sk, "d")])
                    ok, ot = osb.next()
                    S.add("dve", lambda e, sm=sm, ft2=ft2, ot=ot: e.scalar_tensor_tensor(
                        out=ot[:], in0=ft2[:], scalar=sm[:, 5:6], in1=sublnS[:], op0=ALU.mult, op1=ALU.mult),
                        r=[fk2, (sk, "d"), ("subln",)], w=[ok])
                    t = 4 * g + jj
                    store(obuf[t * 128:(t + 1) * 128, h * 128:(h + 1) * 128], ot[:], [ok], [("obuf", t)])
        phase_c(li, w["da_out"][j], h_src, h_dst)

    def hg_layer(li, j, h_src, h_dst):
        new_phase(0)
        hgc = take([128, 4, 128], F32)
        lbv = take([128, D], F32)
        oml = take([128, D], F32)
        gnt = take([128, D], F32)
        Arena.lmark = Arena.off
        lbt = take([128, 4, D], F32)
        dma("sp", hgc, hgc_d[:, :, :], [], [("hgc",)])
        dma("sp", gnt, gn_d[:, :], [], [("hg_gn",)])
        dma("sp", lbt, lb_d[:, :, :], [], [("lbt",)])
        S.add("act", lambda e: e.activation(out=lbt, in_=lbt, func=AF.Exp), r=[("lbt",)], w=[("lbt",)])
        S.add("dve", lambda e: e.tensor_tensor(out=oml[:], in0=lbt[:, 0, :], in1=lbt[:, 1, :], op=ALU.add),
              r=[("lbt",)], w=[("oml",)])
        S.add("dve", lambda e: e.tensor_tensor(out=oml[:], in0=oml[:], in1=lbt[:, 2, :], op=ALU.add),
              r=[("oml",), ("lbt",)], w=[("oml",)])
        S.add("dve", lambda e: e.tensor_tensor(out=oml[:], in0=oml[:], in1=lbt[:, 3, :], op=ALU.add),
              r=[("oml",), ("lbt",)], w=[("oml",)])
        S.add("dve", lambda e: e.reciprocal(out=oml[:], in_=oml[:]), r=[("oml",)], w=[("oml",)])
        S.add("dve", lambda e: e.tensor_copy(out=lbv[:], in_=lbt[:, 1, :]), r=[("lbt",)], w=[("lbv",)])
        for d_ in range(2, li + 1):
            S.add("dve", lambda e, d_=d_: e.tensor_tensor(out=lbv[:], in0=lbv[:], in1=lbt[:, d_, :], op=ALU.add),
                  r=[("lbt",), ("lbv",)], w=[("lbv",)])
        S.add("dve", lambda e: e.tensor_tensor(out=lbv[:], in0=lbv[:], in1=oml[:], op=ALU.mult),
              r=[("lbv",), ("oml",)], w=[("lbv",)])
        S.add("dve", lambda e: e.tensor_scalar(out=oml[:], in0=lbv[:], scalar1=-1.0, scalar2=1.0,
                                               op0=ALU.mult, op1=ALU.add), r=[("lbv",)], w=[("oml",)])
        new_phase(Arena.lmark)
        alloc_common()
        for tg in range(NG):
            rows = slice(tg * 512, (tg + 1) * 512)
            xk, xt = load_T(h_src[rows, :], [("hbuf", tg)], True)

            def ev(tt, c0, ncols, pt, pk, tg=tg):
                r0 = tg * 512 + tt * 128
                t = tg * 4 + tt
                kind = c0 // 1024
                cc = c0 % 1024
                if kind == 1:
                    fk, ft = evf.next()
                    S.add("act", lambda e: e.activation(out=ft[:], in_=pt[:, :], func=AF.Sigmoid), r=pk, w=[fk])
                    S.add("dve", lambda e: e.tensor_tensor(out=ft[:], in0=ft[:], in1=oml[:, cc:cc + 512],
                                                           op=ALU.mult), r=[fk, ("oml",)], w=[fk])
                    S.add("dve", lambda e: e.tensor_tensor(out=ft[:], in0=ft[:], in1=lbv[:, cc:cc + 512],
                                                           op=ALU.add), r=[fk, ("lbv",)], w=[fk])
                    ek, et = ev512.next()
                    S.add("dve", lambda e: e.tensor_scalar(out=et[:], in0=ft[:], scalar1=-1.0, scalar2=1.0,
                                                           op0=ALU.mult, op1=ALU.add), r=[fk], w=[ek])
                    store(s2_d[r0:r0 + 128, cc:cc + 512], et[:], [ek], [("s2", t, cc)])
                    fk2, ft2 = evf.next()
                    S.add("act", lambda e: e.activation(out=ft2[:], in_=ft[:], func=AF.Ln), r=[fk], w=[fk2])
                    store(lf_d[r0:r0 + 128, cc:cc + 512], ft2[:], [fk2], [("lf", t, cc)])
                else:
                    ek, et = ev512.next()
                    dst = {0: s1_d, 2: s3_d, 3: s4_d}[kind]
                    nm = {0: "s1", 2: "s3", 3: "s4"}[kind]
                    if kind == 2:
                        S.add("act", lambda e: e.copy(out=et[:], in_=pt[:, :]), r=pk, w=[ek])
                    else:
                        S.add("act", lambda e: e.activation(out=et[:], in_=pt[:, :], func=AF.Silu), r=pk, w=[ek])
                    store(dst[r0:r0 + 128, cc:cc + 512], et[:], [ek], [(nm, t, cc)])

            proj(xt, [(xk, tt) for tt in range(4)], 8, w["hg_in"][j], 4096, "tok", ev)
        new_phase(Arena.lmark)
        T2, T3, M2, IND = hgc[:, 0, :], hgc[:, 1, :], hgc[:, 2, :], hgc[:, 3, 0:2]
        lfb = takes("hg_lf", 2, [128, D], F32)
        inb = takes("hg_in", 2, [128, 4, D], BF16)
        exb = take([128, 3, D], F32)
        qg = take([128, D], BF16)
        kg = take([128, D], BF16)
        kd0 = take([128, D], BF16)
        kd1 = take([128, D], BF16)
        qgT = take([128, 8, 128], BF16)
        qgT0 = take([128, 8, 128], BF16)
        qgT1 = take([128, 8, 128], BF16)
        kgT = take([128, 8, 128], BF16)
        dec = take([128, 16], F32)
        Sst = take([128, 8, 128], F32)
        Sbf = [take([128, 8, 128], BF16) for i in range(2)]
        ATb = takes("hg_AT", 3, [128, 128], BF16)
        osq = take([128, 128], F32)
        ofl = take([128, D], F32)
        osb = takes("hg_o", 2, [128, D], BF16)
        lhi = take([128, D], BF16)
        llo = take([128, D], BF16)
        cstb = take([128, 4, 128], BF16)
        S.add("dve", lambda e: e.tensor_copy(out=cstb, in_=hgc), r=[("hgc",)], w=[("cstb",)])
        T2b, T3b, INDW = cstb[:, 0, :], cstb[:, 1, :], cstb[:, 3, :]
        S.add("pool", lambda e: e.memset(Sst[:], 0.0), r=[], w=[("hg_S", h) for h in range(8)])
        S.add("pool", lambda e: e.memset(Sbf[0][:], 0.0), r=[], w=[("hg_Sbf", 0, h) for h in range(8)])
        S.add("pool", lambda e: e.memset(qgT0[:], 0.0), r=[], w=[("hg_qgT0",)])
        S.add("pool", lambda e: e.memset(qgT1[:], 0.0), r=[], w=[("hg_qgT1",)])
        S.add("pool", lambda e: e.memset(kd0[:], 0.0), r=[], w=[("hg_kd", 0), ("hg_kd", 1)])
        S.add("pool", lambda e: e.memset(kd1[:], 0.0), r=[], w=[("hg_kd", 0), ("hg_kd", 1)])
        for t in range(DBG.get("hg_tiles", NT)):
            lk, lt = lfb.next()
            dma("sp", lt[:], lf_d[t * 128:(t + 1) * 128, :], [("lf", t, 0), ("lf", t, 512)], [lk])
            ik, it = inb.next()
            for n_, (nm, src) in enumerate((("s1", s1_d), ("s2", s2_d), ("s3", s3_d), ("s4", s4_d))):
                dma("sp", it[:, n_, :], src[t * 128:(t + 1) * 128, :], [(nm, t, 0), (nm, t, 512)], [(ik, n_)])
            S.add("act", lambda e, lt=lt: e.copy(out=lhi, in_=lt), r=[lk], w=[("lhi",)])
            S.add("dve", lambda e, lt=lt: e.tensor_tensor(out=llo, in0=lt, in1=lhi, op=ALU.subtract),
                  r=[lk, ("lhi",)], w=[("llo",)])
            for hf in range(2):
                for part, pkey, first in ((lhi, "lhi", True), (llo, "llo", False)):
                    S.add("pe", lambda e, hf=hf, part=part, first=first: e.matmul(
                        acc[hf][:, :], T2b, part[:, hf * 512:(hf + 1) * 512], start=first, stop=not first),
                        r=[(pkey,), ("cstb",)], w=[("acc", hf, 0), ("acc", hf, 1)])
                for part, pkey, first in ((lhi, "lhi", True), (llo, "llo", False)):
                    S.add("pe", lambda e, hf=hf, part=part, first=first: e.matmul(
                        acc[2 + hf][:, :], T3b, part[:, hf * 512:(hf + 1) * 512], start=first, stop=not first),
                        r=[(pkey,), ("cstb",)], w=[("acc", 2 + hf, 0), ("acc", 2 + hf, 1)])
            for hf in range(2):
                cs = slice(hf * 512, (hf + 1) * 512)
                S.add("act", lambda e, hf=hf, cs=cs: e.activation(out=exb[:, 0, cs], in_=acc[hf][:, :], func=AF.Exp),
                      r=[("acc", hf, 0), ("acc", hf, 1)], w=[("hg_ex", 0, hf)])
                S.add("act", lambda e, hf=hf, cs=cs: e.activation(out=exb[:, 2, cs], in_=acc[2 + hf][:, :],
                                                                  func=AF.Exp),
                      r=[("acc", 2 + hf, 0), ("acc", 2 + hf, 1)], w=[("hg_ex", 2, hf)])
                S.add("dve", lambda e, cs=cs, it=it: e.tensor_tensor(out=qg[:, cs], in0=it[:, 0, cs],
                                                                     in1=exb[:, 0, cs], op=ALU.mult),
                      r=[(ik, 0), ("hg_ex", 0, hf)], w=[("hg_qg", hf)])
                S.add("dve", lambda e, cs=cs: e.reciprocal(out=exb[:, 1, cs], in_=exb[:, 0, cs]),
                      r=[("hg_ex", 0, hf)], w=[("hg_ex", 1, hf)])
                S.add("dve", lambda e, cs=cs, it=it: e.tensor_tensor(out=kg[:, cs], in0=it[:, 1, cs],
                                                                     in1=exb[:, 1, cs], op=ALU.mult),
                      r=[(ik, 1), ("hg_ex", 1, hf)], w=[("hg_kg", hf)])
                S.add("dve", lambda e, cs=cs, it=it: e.tensor_tensor(out=kd0[0:64, cs], in0=it[0:64, 1, cs],
                                                                     in1=exb[0:64, 2, cs], op=ALU.mult),
                      r=[(ik, 1), ("hg_ex", 2, hf)], w=[("hg_kd", hf)])
                S.add("dve", lambda e, cs=cs, it=it: e.tensor_tensor(out=kd1[64:128, cs], in0=it[64:128, 1, cs],
                                                                     in1=exb[64:128, 2, cs], op=ALU.mult),
                      r=[(ik, 1), ("hg_ex", 2, hf)], w=[("hg_kd", hf)])
            for h in range(8):
                dk, dt_ = mm.next()
                for part, pkey, first in ((lhi, "lhi", True), (llo, "llo", False)):
                    S.add("pe", lambda e, h=h, part=part, first=first, dt_=dt_: e.matmul(
                        dt_[:, 0:128], part[:, h * 128:(h + 1) * 128], INDW, start=first, stop=not first),
                        r=[(pkey,), ("cstb",)], w=[dk])
                for c_ in range(2):
                    S.add("act", lambda e, h=h, dt_=dt_, c_=c_: e.activation(
                        out=dec[:, 2 * h + c_:2 * h + c_ + 1], in_=dt_[:, 64 * c_:64 * c_ + 1],
                        func=AF.Exp), r=[dk], w=[("hg_dec",)])
            for h in range(8):
                S.add("pe", lambda e, h=h: e.transpose(trb[:, h * 128:(h + 1) * 128], qg[:, h * 128:(h + 1) * 128],
                                                       ident[:]),
                      r=[("hg_qg", h // 4), ("ident",)], w=[("trb", h)])
            trv = trb[:, :].rearrange("p (k t) -> p k t", t=128)
            S.add("dve", lambda e: e.tensor_copy(out=qgT[:], in_=trv), r=[("trb", h) for h in range(8)],
                  w=[("hg_qgT",)])
            S.add("act", lambda e: e.copy(out=qgT0[:, :, 0:64], in_=trv[:, :, 0:64]),
                  r=[("trb", h) for h in range(8)], w=[("hg_qgT0",)])
            S.add("act", lambda e: e.copy(out=qgT1[:, :, 64:128], in_=trv[:, :, 64:128]),
                  r=[("trb", h) for h in range(8)], w=[("hg_qgT1",)])
            for h in range(8):
                S.add("pe", lambda e, h=h: e.transpose(trb[:, h * 128:(h + 1) * 128], kg[:, h * 128:(h + 1) * 128],
                                                       ident[:]),
                      r=[("hg_kg", h // 4), ("ident",)], w=[("trb", h)])
            S.add("dve", lambda e: e.tensor_copy(out=kgT[:], in_=trv), r=[("trb", h) for h in range(8)],
                  w=[("hg_kgT",)])
            ok, ot = osb.next()
            for h in range(8):
                hs = slice(h * 128, (h + 1) * 128)
                pk, pt = mm.next()
                S.add("pe", lambda e, pt=pt, h=h: e.matmul(pt[:, 0:128], kgT[:, h, :], qgT[:, h, :],
                                                           start=True, stop=True),
                      r=[("hg_kgT",), ("hg_qgT",)], w=[pk])
                ak, at = ATb.next()
                S.add("dve", lambda e, pt=pt, at=at: e.tensor_tensor(out=at[:], in0=pt[:, 0:128], in1=M2,
                                                                     op=ALU.mult), r=[pk, ("hgc",)], w=[ak])
                pk2, pt2 = mm.next()
                S.add("pe", lambda e, pt2=pt2, hs=hs, it=it: e.matmul(pt2[:, 0:128], kd0[:, hs], it[:, 2, hs],
                                                                      start=True, stop=True),
                      r=[("hg_kd", h // 4), (ik, 2)], w=[pk2])
                S.add("dve", lambda e, pt2=pt2, h=h: e.scalar_tensor_tensor(
                    out=Sst[:, h, :], in0=Sst[:, h, :], scalar=dec[:, 2 * h:2 * h + 1], in1=pt2[:, 0:128],
                    op0=ALU.mult, op1=ALU.add), r=[pk2, ("hg_dec",), ("hg_S", h)], w=[("hg_S", h)])
                S.add("act", lambda e, h=h: e.copy(out=Sbf[1][:, h, :], in_=Sst[:, h, :]),
                      r=[("hg_S", h)], w=[("hg_Sbf", 1, h)])
                obank = acc[h // 4][:, (h % 4) * 128:(h % 4 + 1) * 128]
                okey = ("acc", h // 4, h % 4 // 2)
                S.add("pe", lambda e, obank=obank, at=at, hs=hs, it=it: e.matmul(obank, at[:], it[:, 2, hs],
                                                                                 start=True, stop=False),
                      r=[ak, (ik, 2)], w=[okey])
                S.add("pe", lambda e, obank=obank, h=h: e.matmul(obank, qgT0[:, h, :], Sbf[0][:, h, :],
                                                                 start=False, stop=False),
                      r=[("hg_qgT0",), ("hg_Sbf", 0, h)], w=[okey])
                S.add("pe", lambda e, obank=obank, h=h: e.matmul(obank, qgT1[:, h, :], Sbf[1][:, h, :],
                                                                 start=False, stop=True),
                      r=[("hg_qgT1",), ("hg_Sbf", 1, h)], w=[okey])
                pk3, pt3 = mm.next()
                S.add("pe", lambda e, pt3=pt3, hs=hs, it=it: e.matmul(pt3[:, 0:128], kd1[:, hs],
                                                                      it[:, 2, hs], start=True, stop=True),
                      r=[("hg_kd", h // 4), (ik, 2)], w=[pk3])
                S.add("dve", lambda e, pt3=pt3, h=h: e.scalar_tensor_tensor(
                    out=Sst[:, h, :], in0=Sst[:, h, :], scalar=dec[:, 2 * h + 1:2 * h + 2], in1=pt3[:, 0:128],
                    op0=ALU.mult, op1=ALU.add), r=[pk3, ("hg_dec",), ("hg_S", h)], w=[("hg_S", h)])
                S.add("act", lambda e, h=h: e.copy(out=Sbf[0][:, h, :], in_=Sst[:, h, :]),
                      r=[("hg_S", h)], w=[("hg_Sbf", 0, h)])
                sk, sm = small.next()
                S.add("act", lambda e, obank=obank, hs=hs: e.copy(out=ofl[:, hs], in_=obank),
                      r=[okey], w=[("hg_of", h)])
                S.add("act", lambda e, sm=sm, hs=hs: e.activation(out=osq[:], in_=ofl[:, hs], func=AF.Square,
                                                                  accum_out=sm[:, 0:1]),
                      r=[("hg_of", h)], w=[("hg_osq",), sk])
                S.add("act", lambda e, sm=sm: e.activation(out=sm[:, 1:2], in_=sm[:, 0:1], func=AF.Sqrt,
                                                           bias=epsT[:, 1:2], scale=1.0 / 128.0),
                      r=[sk, ("eps",)], w=[(sk, "b")])
                S.add("dve", lambda e, sm=sm: e.reciprocal(out=sm[:, 2:3], in_=sm[:, 1:2]),
                      r=[(sk, "b")], w=[(sk, "c")])
                S.add("dve", lambda e, sm=sm, hs=hs: e.scalar_tensor_tensor(
                    out=ofl[:, hs], in0=ofl[:, hs], scalar=sm[:, 2:3], in1=gnt[:, hs], op0=ALU.mult, op1=ALU.mult),
                    r=[("hg_of", h), (sk, "c"), ("hg_gn",)], w=[("hg_of", h)])
                S.add("pool", lambda e, hs=hs, it=it, ot=ot: e.tensor_tensor(out=ot[:, hs], in0=ofl[:, hs],
                                                                             in1=it[:, 3, hs], op=ALU.mult),
                      r=[("hg_of", h), (ik, 3)], w=[ok])
            store(obuf[t * 128:(t + 1) * 128, :], ot[:], [ok], [("obuf", t)])
        phase_c(li, w["hg_out"][j], h_src, h_dst)

    for n, li in enumerate(layers):
        h_src = x_in if n == 0 else hbuf
        h_dst = out_d if n == len(layers) - 1 else hbuf
        kind, j = li % 3, li // 3
        if kind == 0:
            da_layer(li, j, h_src, h_dst)
        elif kind == 1:
            hg_layer(li, j, h_src, h_dst)
        else:
            swa_layer(li, j, h_src, h_dst)

    S.emit(nc, st)
    st.close()
    return nc


def host_consts(rel_bias, sw_sinks, ln_g, ln_b, da_lambda, da_subln, hg_gnorm, hg_lower_bound):
    f = np.float32
    c = {}
    c["ident"] = np.eye(128, dtype=f)
    k = np.arange(128)[:, None]
    q = np.arange(128)[None, :]
    rel_d = q - k
    bd = t5_bucket_np(rel_d)
    bp = t5_bucket_np(rel_d + 128)
    tabD = np.empty((128, 16, 128), f)
    tabP = np.empty((128, 16, 128), f)
    swT = np.empty((128, 16, 256), f)
    for m in range(16):
        tabD[:, m, :] = np.where(rel_d >= 0, rel_bias[bd, m], f(NEG))
        tabP[:, m, :] = rel_bias[bp, m]
        swT[:, m, 0:128] = tabD[:, m, :]
        swT[:, m, 128:256] = np.where(rel_d + 128 < 128, rel_bias[bp, m], f(NEG))
    c["tabD"], c["tabP"], c["swT"] = tabD, tabP, swT
    c["c31"] = np.ascontiguousarray(np.broadcast_to(rel_bias[31][None, :], (128, 16))).astype(f)
    c["sinks"] = np.ascontiguousarray(np.broadcast_to(sw_sinks[0][None, :], (128, 16))).astype(f)
    s = np.arange(128)[:, None]
    t = np.arange(128)[None, :]
    same = (s // 64) == (t // 64)
    hgc = np.zeros((128, 4, 128), f)
    hgc[:, 0, :] = (same & (s <= t)).astype(f)
    hgc[:, 1, :] = (same & (s > t)).astype(f)
    hgc[:, 2, :] = (same & (s <= t)).astype(f)
    hgc[:, 3, 0:64] = (np.arange(128) < 64).astype(f)[:, None]
    hgc[:, 3, 64:128] = (np.arange(128) >= 64).astype(f)[:, None]
    c["hgc"] = hgc
    lnbc = np.empty((DEPTH, 4, 128, D), f)
    for i in range(DEPTH):
        lnbc[i, 0] = ln_g[i, 0][None, :]
        lnbc[i, 1] = ln_b[i, 0][None, :]
        lnbc[i, 2] = ln_g[i, 1][None, :]
        lnbc[i, 3] = ln_b[i, 1][None, :]
    c["lnbc"] = lnbc
    c["lam"] = np.ascontiguousarray(np.broadcast_to(da_lambda.reshape(2, 1, 256), (2, 128, 256))).astype(f)
    c["subln"] = np.ascontiguousarray(np.broadcast_to(da_subln.reshape(2, 1, 128), (2, 128, 128))).astype(f)
    c["gnorm"] = np.ascontiguousarray(np.broadcast_to(np.tile(hg_gnorm[0], 8)[None, :], (128, D))).astype(f)
    c["hglb"] = np.ascontiguousarray(np.broadcast_to(hg_lower_bound[None, :, :], (128, 4, D))).astype(f)
    return c


_NC_CACHE = {}


def run_layers(layers, h_in, inputs):
    key = tuple(layers)
    if key not in _NC_CACHE:
        _NC_CACHE[key] = build_program(list(layers))
    nc = _NC_CACHE[key]
    f = np.float32
    g = lambda k: np.ascontiguousarray(np.asarray(inputs[k], dtype=f))
    consts = host_consts(g("rel_bias"), g("sw_sinks"), g("ln_g"), g("ln_b"), g("da_lambda"), g("da_subln"),
                         g("hg_gnorm"), g("hg_lower_bound"))
    shared = dict(consts)
    for k in ("da_w_in", "da_w_out", "hg_w_in", "hg_w_out", "sw_w_in", "sw_w_out", "w_up", "w_down", "w_ple",
              "w_ple_gate"):
        shared[k] = g(k)
    p = g("p")
    in_maps = []
    for c in range(8):
        b = c // 2
        m = dict(shared)
        m["x"] = np.ascontiguousarray(h_in[b])
        m["p"] = np.ascontiguousarray(p[:, b])
        in_maps.append(m)
    res = run_bass_kernel_spmd(nc, in_maps, core_ids=list(range(8)))
    out = np.empty((4, T, D), f)
    for b in range(4):
        out[b, 0:T // 2] = res.results[2 * b]["out"][0:T // 2]
        out[b, T // 2:T] = res.results[2 * b + 1]["out"][T // 2:T]
    return out


def kernel(**inputs):
    x = np.ascontiguousarray(np.asarray(inputs["x"], dtype=np.float32))
    return run_layers((0, 1, 2, 3), x, inputs)
```

```python
import math
from contextlib import ExitStack
import numpy as np
import concourse.bass as bass
import concourse.mybir as mybir
from concourse.bass_utils import run_bass_kernel_spmd

F32 = mybir.dt.float32
BF16 = mybir.dt.bfloat16
ALU = mybir.AluOpType
AF = mybir.ActivationFunctionType
AX = mybir.AxisListType

T = 4096
D = 1024
NT = T // 128
NG = T // 512
DEPTH = 4
ALPHA = (2 * DEPTH) ** 0.25
LN_EPS = 1e-5
RMS_EPS = 1e-6
NEG = -30000.0
DBG = {}


class Op:
    __slots__ = ("eng", "fn", "deps", "dma", "sig", "sem", "tick", "prev_dma")


class Sched:
    ENG = ["pe", "act", "dve", "pool", "sp"]
    NDS = 8

    def __init__(self):
        self.ops = {e: [] for e in self.ENG}
        self.lastw = {}
        self.readers = {}
        self.ndma = {e: 0 for e in self.ENG}
        self.pending = {e: [] for e in self.ENG}
        self.recent_dma = {e: [] for e in self.ENG}

    def fence(self):
        lasts = []
        for e in self.ENG:
            comp = [o for o in self.ops[e] if not o.dma]
            if comp:
                lasts.append(comp[-1])
            lasts.extend(self.recent_dma[e])
        for e in self.ENG:
            self.pending[e] = list(lasts)

    def add(self, eng, fn, r=(), w=(), dma=False):
        op = Op()
        op.eng, op.fn, op.dma, op.sig = eng, fn, dma, False
        op.deps = []
        op.sem = None
        op.tick = 0
        op.prev_dma = None

        def dep(d, raw):
            if d is op or d in op.deps:
                return
            if (not dma) and (not d.dma) and d.eng == eng:
                if eng == "pe" or not raw:
                    return
            op.deps.append(d)

        for k in r:
            lw = self.lastw.get(k)
            if lw is not None:
                dep(lw, True)
        for k in w:
            lw = self.lastw.get(k)
            if lw is not None:
                dep(lw, False)
            for rd in self.readers.get(k, ()):
                dep(rd, False)
        for k in r:
            self.readers.setdefault(k, []).append(op)
        for k in w:
            self.lastw[k] = op
            self.readers[k] = []
        if self.pending[eng]:
            for d in self.pending[eng]:
                if d is op or d in op.deps:
                    continue
                if (not dma) and (not d.dma) and d.eng == eng:
                    continue
                op.deps.append(d)
            self.pending[eng] = []
        for d in op.deps:
            d.sig = True
        if dma:
            op.sig = True
            self.recent_dma[eng] = (self.recent_dma[eng] + [op])[-self.NDS:]
        self.ops[eng].append(op)
        return op

    def emit(self, nc, stack):
        esem = {e: stack.enter_context(nc.semaphore("s_" + e)) for e in self.ENG}
        dsem = {e: [stack.enter_context(nc.semaphore("d_%s%d" % (e, i))) for i in range(self.NDS)]
                for e in self.ENG}
        fin = stack.enter_context(nc.semaphore("fin"))
        for e in self.ENG:
            cnt = 0
            nd = 0
            for op in self.ops[e]:
                if op.dma:
                    op.sem = dsem[e][nd % self.NDS]
                    op.tick = 16 * (nd // self.NDS + 1)
                    nd += 1
                elif op.sig:
                    cnt += 1
                    op.sem = esem[e]
                    op.tick = cnt
        block = stack.enter_context(nc.Block())

        def run(e, eng):
            known = {}
            for op in self.ops[e]:
                waits = [(d.sem, d.tick) for d in op.deps]
                if op.dma and op.tick > 16:
                    waits.append((op.sem, op.tick - 16))
                for sem, val in waits:
                    if known.get(id(sem), 0) >= val:
                        continue
                    eng.wait_ge(sem, val)
                    known[id(sem)] = val
                ins = op.fn(eng)
                if op.dma:
                    ins.then_inc(op.sem, 16)
                elif op.sig:
                    ins.then_inc(op.sem, 1)
            nd = 0
            last = {}
            for op in self.ops[e]:
                if op.dma:
                    last[id(op.sem)] = (op.sem, op.tick)
            for sem, val in last.values():
                if known.get(id(sem), 0) < val:
                    eng.wait_ge(sem, val)

        @block.tensor
        def _(eng):
            run("pe", eng)

        @block.scalar
        def _(eng):
            run("act", eng)

        @block.vector
        def _(eng):
            run("dve", eng)

        @block.gpsimd
        def _(eng):
            run("pool", eng)

        @block.sync
        def _(eng):
            run("sp", eng)


class Rot:
    def __init__(self, name, tiles):
        self.name, self.tiles, self.i = name, tiles, 0

    def next(self):
        i = self.i % len(self.tiles)
        self.i += 1
        return (self.name, i), self.tiles[i]


def t5_bucket_np(rel):
    n = np.maximum(rel, 0)
    nf = np.maximum(n, 1).astype(np.float32)
    large = 16 + (np.log(nf / np.float32(16)) / np.float32(math.log(8.0)) * np.float32(16)).astype(np.int32)
    large = np.minimum(large, 31)
    return np.where(n < 16, n, large)


def build_program(layers, in_from_x=True):
    nc = bass.Bass("TRN2", target_bir_lowering=False)
    S = Sched()
    st = ExitStack()

    def din(name, shape, dt=F32):
        return nc.dram_tensor(name, list(shape), dt, kind="ExternalInput").ap()

    def dscr(name, shape, dt):
        return nc.dram_tensor(name, list(shape), dt, kind="Internal").ap()

    x_in = din("x", [T, D])
    p_in = din("p", [DEPTH, T, 256])
    out_d = nc.dram_tensor("out", [T, D], F32, kind="ExternalOutput").ap()
    ident_d = din("ident", [128, 128])
    tabD_d = din("tabD", [128, 16, 128])
    tabP_d = din("tabP", [128, 16, 128])
    c31_d = din("c31", [128, 16])
    swT_d = din("swT", [128, 16, 256])
    esk_d = din("sinks", [128, 16])
    hgc_d = din("hgc", [128, 4, 128])
    lnb_d = din("lnbc", [DEPTH, 4, 128, D])
    lam_d = din("lam", [2, 128, 256])
    subln_d = din("subln", [2, 128, 128])
    gn_d = din("gnorm", [128, D])
    lb_d = din("hglb", [128, 4, D])
    w = {}
    w["da_in"] = din("da_w_in", [2, D, 3072])
    w["da_out"] = din("da_w_out", [2, D, D])
    w["hg_in"] = din("hg_w_in", [1, D, 4096])
    w["hg_out"] = din("hg_w_out", [1, D, D])
    w["sw_in"] = din("sw_w_in", [1, D, 1280])
    w["sw_out"] = din("sw_w_out", [1, D, D])
    w["up"] = din("w_up", [DEPTH, D, 4096])
    w["down"] = din("w_down", [DEPTH, 4096, D])
    w["ple"] = din("w_ple", [DEPTH, 256, D])
    w["gate"] = din("w_ple_gate", [DEPTH, D, D])

    hbuf = dscr("hbuf", [T, D], F32)
    obuf = dscr("obuf", [T, D], BF16)
    qT_d = dscr("qT", [8, 128, T], BF16)
    kT_d = dscr("kT", [8, 128, T], BF16)
    s1_d = dscr("s1", [T, D], BF16)
    s2_d = dscr("s2", [T, D], BF16)
    s3_d = dscr("s3", [T, D], BF16)
    s4_d = dscr("s4", [T, D], BF16)
    lf_d = dscr("lf", [T, D], F32)

    def sb(name, shape, dt):
        return st.enter_context(nc.sbuf_tensor("sb_" + name, list(shape), dt))

    def ps(name, shape, dt):
        return st.enter_context(nc.psum_tensor("ps_" + name, list(shape), dt))

    mm = Rot("mm", [ps("mm%d" % i, [128, 512], F32) for i in range(2)])
    acc = [ps("acc%d" % i, [128, 512], F32) for i in range(4)]
    trb = ps("trb", [128, 1024], BF16)
    aux = ps("aux", [128, 512], F32)

    ident = sb("ident", [128, 128], BF16)
    identf = sb("identf", [128, 128], F32)
    c31 = sb("c31", [128, 16], F32)
    epsT = sb("epsT", [128, 2], F32)
    zeros = sb("zeros", [128, 512], BF16)
    wsl = Rot("w", [sb("w%d" % i, [128, 8, 512], BF16) for i in range(5)])
    ev512 = Rot("ev", [sb("ev%d" % i, [128, 512], BF16) for i in range(4)])
    evf = Rot("evf", [sb("evf%d" % i, [128, 512], F32) for i in range(3)])
    small = Rot("sm", [sb("sm%d" % i, [128, 16], F32) for i in range(6)])
    ARENA_N = 73728
    arena_t = sb("arena", [128, ARENA_N], BF16)

    class Arena:
        off = 0
        cnt = 0

    def take(shape, dt):
        n = 1
        for s_ in shape[1:]:
            n *= s_
        ne = n * 2 if dt == F32 else n
        ap = arena_t[:, Arena.off:Arena.off + ne]
        Arena.off += ne
        assert Arena.off <= ARENA_N, Arena.off
        if dt == F32:
            ap = ap.bitcast(F32)
        if len(shape) == 3:
            ap = ap.rearrange("p (a b) -> p a b", b=shape[2])
        return ap

    def takes(name, n, shape, dt):
        Arena.cnt += 1
        return Rot("%s_%d" % (name, Arena.cnt), [take(shape, dt) for _ in range(n)])

    com = {}

    def alloc_common():
        com["stg_f"] = takes("stgf", 1, [128, 4, D], F32)
        com["stg_b"] = takes("stgb", 2, [128, 4, D], BF16)
        com["xT"] = takes("xT", 2, [128, 8, 512], BF16)

    def new_phase(mark):
        S.fence()
        Arena.off = mark

    const_ops = []

    def dma(eng, out, in_, r, wk):
        return S.add(eng, lambda e, o=out, i=in_: e.dma_start(out=o, in_=i), r=r, w=wk, dma=True)

    dma("sp", identf[:], ident_d[:, :], [], [("identf",)])
    S.add("dve", lambda e: e.tensor_copy(out=ident[:], in_=identf[:]), r=[("identf",)], w=[("ident",)])
    dma("sp", c31[:], c31_d[:, :], [], [("c31",)])
    S.add("pool", lambda e: e.memset(zeros[:], 0.0), r=[], w=[("zeros",)])
    S.add("pool", lambda e: e.memset(epsT[:, 0:1], LN_EPS), r=[], w=[("eps",)])
    S.add("pool", lambda e: e.memset(epsT[:, 1:2], RMS_EPS), r=[], w=[("eps",)])

    def transposes(src_tile, src_key, nkc, xt_tile, xt_key, tt):
        for kc in range(nkc):
            S.add("pe", lambda e, kc=kc: e.transpose(trb[:, kc * 128:(kc + 1) * 128],
                                                     src_tile[:, tt, kc * 128:(kc + 1) * 128], ident[:]),
                  r=[src_key, ("ident",)], w=[("trb", kc)])
        S.add("dve", lambda e: e.tensor_copy(
            out=xt_tile[:, 0:nkc, tt * 128:(tt + 1) * 128],
            in_=trb[:, 0:nkc * 128].rearrange("p (k t) -> p k t", t=128)),
            r=[("trb", kc) for kc in range(nkc)], w=[xt_key])

    def load_T(src_rows_ap, src_keys, is_f32, nfeat=D):
        nkc = nfeat // 128
        bkey, bt = com["stg_b"].next()
        if is_f32:
            fkey, ft = com["stg_f"].next()
            dma("sp", ft[:, :, 0:nfeat], src_rows_ap.rearrange("(t p) d -> p t d", p=128), src_keys, [fkey])
            for tt in range(4):
                S.add("act", lambda e, tt=tt: e.copy(out=bt[:, tt, 0:nfeat], in_=ft[:, tt, 0:nfeat]),
                      r=[fkey], w=[(bkey, tt)])
        else:
            dma("sp", bt[:, :, 0:nfeat], src_rows_ap.rearrange("(t p) d -> p t d", p=128), src_keys,
                [(bkey, tt) for tt in range(4)])
        xkey, xt = com["xT"].next()
        for tt in range(4):
            transposes(bt, (bkey, tt), nkc, xt, (xkey, tt), tt)
        return xkey, xt

    def proj(xt, xkeys, KC, wd, N, mode, evac, lhs_fn=None):
        npieces = max(1, KC // 8)
        kper = min(KC, 8)
        c0 = 0
        while c0 < N:
            ncols = min(512, N - c0)
            for kp in range(npieces):
                wkey, wt = wsl.next()
                dma("pool", wt[:, 0:kper, 0:ncols],
                    wd[kp * 1024:kp * 1024 + kper * 128, c0:c0 + ncols].rearrange("(c p) n -> p c n", p=128),
                    [], [wkey])
                if mode == "tok":
                    for tt in range(4):
                        pk = [("acc", tt, 0), ("acc", tt, 1)]
                        for k in range(kper):
                            kk = kp * 8 + k
                            S.add("pe", lambda e, tt=tt, k=k, kk=kk, wt=wt, ncols=ncols, kp=kp: e.matmul(
                                acc[tt][:, 0:ncols], xt[:, kk, tt * 128:(tt + 1) * 128], wt[:, k, 0:ncols],
                                start=(kk == 0), stop=(kk == KC - 1)),
                                r=[wkey] + xkeys, w=pk)
                        if kp == npieces - 1:
                            evac(tt, c0, ncols, acc[tt], pk)
                else:
                    for sbk in range(ncols // 128):
                        pk, pt = mm.next()
                        for k in range(kper):
                            S.add("pe", lambda e, k=k, sbk=sbk, wt=wt, pt=pt: e.matmul(
                                pt[:, :], wt[:, k, sbk * 128:(sbk + 1) * 128], xt[:, k, :],
                                start=(k == 0), stop=(k == kper - 1)),
                                r=[wkey] + xkeys, w=[pk])
                        evac((c0 // 128) + sbk, pt, [pk])
            c0 += ncols

    def store(dst_ap, src_ap, rkeys, wkeys, eng="sp"):
        dma(eng, dst_ap, src_ap, rkeys, wkeys)

    def layer_norm(ht, hkey, tt, gi):
        k1, sm = small.next()
        lnbc = com["lnbc"]
        xin = ht[:, tt, :]
        S.add("dve", lambda e: e.bn_stats(out=sm[:, 0:6], in_=ht[:, tt, 0:512]), r=[(hkey, tt)], w=[k1])
        k2, sm2 = small.next()
        S.add("dve", lambda e: e.bn_stats(out=sm2[:, 0:6], in_=ht[:, tt, 512:1024]), r=[(hkey, tt)], w=[k2])
        k3, sm3 = small.next()
        S.add("dve", lambda e: e.tensor_copy(out=sm3[:, 0:6], in_=sm[:, 0:6]), r=[k1], w=[k3])
        S.add("dve", lambda e: e.tensor_copy(out=sm3[:, 6:12], in_=sm2[:, 0:6]), r=[k2], w=[k3])
        k4, sm4 = small.next()
        S.add("dve", lambda e: e.bn_aggr(out=sm4[:, 0:2], in_=sm3[:, 0:12]), r=[k3], w=[k4])
        S.add("act", lambda e: e.activation(out=sm4[:, 3:4], in_=sm4[:, 1:2], func=AF.Sqrt, bias=epsT[:, 0:1],
                                            scale=1.0), r=[k4, ("eps",)], w=[(k4, "s")])
        S.add("dve", lambda e: e.reciprocal(out=sm4[:, 2:3], in_=sm4[:, 3:4]), r=[(k4, "s")], w=[k4])
        S.add("dve", lambda e: e.tensor_scalar(out=xin, in0=xin, scalar1=sm4[:, 0:1], scalar2=sm4[:, 2:3],
                                               op0=ALU.subtract, op1=ALU.mult), r=[k4, (hkey, tt)], w=[(hkey, tt)])
        S.add("dve", lambda e: e.tensor_tensor(out=xin, in0=xin, in1=lnbc[:, gi, :], op=ALU.mult),
              r=[(hkey, tt), ("lnbc",)], w=[(hkey, tt)])
        S.add("dve", lambda e: e.tensor_tensor(out=xin, in0=xin, in1=lnbc[:, gi + 1, :], op=ALU.add),
              r=[(hkey, tt), ("lnbc",)], w=[(hkey, tt)])

    def phase_c(li, w_out_ap, h_src, h_dst):
        new_phase(Arena.lmark)
        alloc_common()
        stg_b, xT = com["stg_b"], com["xT"]
        uT = take([128, 32, 512], BF16)
        gate = take([128, 4, D], F32)
        lnbc = take([128, 4, D], F32)
        com["lnbc"] = lnbc
        hres = takes("hres", 1, [128, 4, D], F32)
        dma("sp", lnbc, lnb_d[li].rearrange("g p d -> p g d"), [], [("lnbc",)])
        for tg in range(NG):
            rows = slice(tg * 512, (tg + 1) * 512)
            okeys = [("obuf", tg * 4 + tt) for tt in range(4)]
            xk, xt = load_T(obuf[rows, :], okeys, False)
            hkey, ht = hres.next()
            dma("sp", ht, h_src[rows, :].rearrange("(t p) d -> p t d", p=128),
                [("hbuf", tg)], [(hkey, tt) for tt in range(4)])

            def ev_res(tt, c0, ncols, pt, pk, ht=ht, hkey=hkey):
                S.add("dve", lambda e: e.scalar_tensor_tensor(
                    out=ht[:, tt, c0:c0 + ncols], in0=ht[:, tt, c0:c0 + ncols], scalar=ALPHA,
                    in1=pt[:, 0:ncols], op0=ALU.mult, op1=ALU.add), r=pk + [(hkey, tt)], w=[(hkey, tt)])

            proj(xt, [(xk, tt) for tt in range(4)], 8, w_out_ap, D, "tok", ev_res)

            def norm_and_T(gi, ht=ht, hkey=hkey):
                bkey, bt = stg_b.next()
                for tt in range(4):
                    layer_norm(ht, hkey, tt, gi)
                    S.add("act", lambda e, tt=tt: e.copy(out=bt[:, tt, :], in_=ht[:, tt, :]),
                          r=[(hkey, tt)], w=[(bkey, tt)])
                xk2, xt2 = xT.next()
                for tt in range(4):
                    transposes(bt, (bkey, tt), 8, xt2, (xk2, tt), tt)
                return xk2, xt2

            xk1, xt1 = norm_and_T(0)

            def ev_up(fc, pt, pk):
                fk, ft = evf.next()
                S.add("act", lambda e: e.activation(out=ft[:], in_=pt[:, :], func=AF.Relu), r=pk, w=[fk])
                S.add("dve", lambda e: e.tensor_tensor(out=uT[:, fc, :], in0=ft[:], in1=ft[:], op=ALU.mult),
                      r=[fk], w=[("uT", fc)])

            proj(xt1, [(xk1, tt) for tt in range(4)], 8, w["up"][li], 4096, "feat", ev_up)
            proj(uT, [("uT", fc) for fc in range(32)], 32, w["down"][li], D, "tok", ev_res)
            xk2, xt2 = norm_and_T(2)

            def ev_gate(tt, c0, ncols, pt, pk):
                S.add("act", lambda e: e.activation(out=gate[:, tt, c0:c0 + ncols], in_=pt[:, 0:ncols],
                                                    func=AF.Sigmoid), r=pk, w=[("gate", tt)])

            proj(xt2, [(xk2, tt) for tt in range(4)], 8, w["gate"][li], D, "tok", ev_gate)
            pk_, pxt = load_T(p_in[li, rows, :], [], True, nfeat=256)

            def ev_ple(tt, c0, ncols, pt, pk, ht=ht, hkey=hkey):
                fk, ft = evf.next()
                S.add("dve", lambda e: e.tensor_tensor(out=ft[:, 0:ncols], in0=pt[:, 0:ncols],
                                                       in1=gate[:, tt, c0:c0 + ncols], op=ALU.mult),
                      r=pk + [("gate", tt)], w=[fk])
                S.add("dve", lambda e: e.tensor_tensor(out=ht[:, tt, c0:c0 + ncols], in0=ht[:, tt, c0:c0 + ncols],
                                                       in1=ft[:, 0:ncols], op=ALU.add),
                      r=[fk, (hkey, tt)], w=[(hkey, tt)])

            proj(pxt, [(pk_, tt) for tt in range(4)], 2, w["ple"][li], D, "tok", ev_ple)
            store(h_dst[rows, :].rearrange("(t p) d -> p t d", p=128), ht,
                  [(hkey, tt) for tt in range(4)], [("hbuf", tg)] if h_dst is hbuf else [("outd", tg)])

    def swa_layer(li, j, h_src, h_dst):
        new_phase(0)
        Arena.lmark = Arena.off
        alloc_common()
        for tg in range(NG):
            rows = slice(tg * 512, (tg + 1) * 512)
            xk, xt = load_T(h_src[rows, :], [("hbuf", tg)], True)

            def ev(tt, c0, ncols, pt, pk, tg=tg):
                ek, et = ev512.next()
                r0 = tg * 512 + tt * 128
                if c0 < 1024:
                    S.add("act", lambda e: e.activation(out=et[:, 0:ncols], in_=pt[:, 0:ncols], func=AF.Copy,
                                                        scale=0.125), r=pk, w=[ek])
                    store(s1_d[r0:r0 + 128, c0:c0 + ncols], et[:, 0:ncols], [ek], [("s1", tg * 4 + tt, c0)])
                else:
                    S.add("act", lambda e: e.copy(out=et[:, 0:ncols], in_=pt[:, 0:ncols]), r=pk, w=[ek])
                    store(s2_d[r0:r0 + 128, 0:ncols], et[:, 0:ncols], [ek], [("s2", tg * 4 + tt)])

            proj(xt, [(xk, tt) for tt in range(4)], 8, w["sw_in"][j], 1280, "tok", ev)
        new_phase(Arena.lmark)
        swT = take([128, 16, 256], F32)
        esk = take([128, 16], F32)
        qTt = take([128, 8, 128], BF16)
        kdup = take([128, 4, 128], BF16)
        kTd = [take([128, 4, 128], BF16) for i in range(2)]
        vx = [take([128, 2, 66], BF16) for i in range(2)]
        osb = takes("sw_o", 2, [128, D], BF16)
        qin = takes("sw_qin", 2, [128, D], BF16)
        kvin = takes("sw_kvin", 2, [128, 256], BF16)
        ptb = takes("sw_pt", 3, [128, 256], BF16)
        dma("sp", swT, swT_d[:, :, :], [], [("swT",)])
        dma("sp", esk, esk_d[:, :], [], [("esk",)])
        S.add("act", lambda e: e.activation(out=esk, in_=esk, func=AF.Exp), r=[("esk",)], w=[("esk2",)])
        for i in range(2):
            S.add("pool", lambda e, i=i: e.memset(vx[i][:, :, 64:65], 1.0), r=[], w=[("sw_vx", i)])
        S.add("pool", lambda e: e.memset(kdup[:], 0.0), r=[], w=[("sw_kdup", kh) for kh in range(2)])
        for t in range(NT):
            cur, prv = t % 2, (t + 1) % 2
            qk, qt = qin.next()
            dma("sp", qt[:], s1_d[t * 128:(t + 1) * 128, :], [("s1", t, 0), ("s1", t, 512)], [qk])
            kvk, kvt = kvin.next()
            dma("sp", kvt[:], s2_d[t * 128:(t + 1) * 128, 0:256], [("s2", t)], [kvk])
            for hc in range(8):
                S.add("pe", lambda e, hc=hc, qt=qt: e.transpose(trb[:, hc * 128:(hc + 1) * 128],
                                                               qt[:, hc * 128:(hc + 1) * 128], ident[:]),
                      r=[qk, ("ident",)], w=[("trb", hc)])
            S.add("dve", lambda e: e.tensor_copy(out=qTt[:], in_=trb[:, :].rearrange("p (k t) -> p k t", t=128)),
                  r=[("trb", hc) for hc in range(8)], w=[("sw_qT",)])
            for kh in range(2):
                for r_ in range(2):
                    S.add("pool", lambda e, kh=kh, r_=r_, kvt=kvt: e.tensor_copy(
                        out=kdup[:, kh * 2 + r_, r_ * 64:(r_ + 1) * 64], in_=kvt[:, kh * 64:(kh + 1) * 64]),
                        r=[kvk], w=[("sw_kdup", kh)])
            for k4 in range(4):
                S.add("pe", lambda e, k4=k4: e.transpose(trb[:, k4 * 128:(k4 + 1) * 128], kdup[:, k4, :], ident[:]),
                      r=[("sw_kdup", k4 // 2), ("ident",)], w=[("trb", k4)])
            S.add("dve", lambda e, cur=cur: e.tensor_copy(
                out=kTd[cur][:], in_=trb[:, 0:512].rearrange("p (k t) -> p k t", t=128)),
                r=[("trb", k4) for k4 in range(4)], w=[("sw_kTd", cur)])
            S.add("act", lambda e, cur=cur, kvt=kvt: e.copy(
                out=vx[cur][:, :, 0:64], in_=kvt[:, 128:256].rearrange("p (k d) -> p k d", d=64)),
                r=[kvk], w=[("sw_vx", cur)])
            ok, ot = osb.next()
            nb = [6, 6, 4]
            for hq in range(16):
                kvh = hq // 8
                base = (hq % 2) * 64
                nkeys = 256 if t > 0 else 128
                pk, pt = mm.next()
                S.add("pe", lambda e, pt=pt, base=base, kvh=kvh, hq=hq, cur=cur: e.matmul(
                    pt[:, 0:128], kTd[cur][:, kvh * 2 + hq % 2, :], qTt[:, hq // 2, :],
                    start=True, stop=True), r=[("sw_kTd", cur), ("sw_qT",)], w=[pk])
                if t > 0:
                    S.add("pe", lambda e, pt=pt, base=base, kvh=kvh, hq=hq, prv=prv: e.matmul(
                        pt[:, 128:256], kTd[prv][:, kvh * 2 + hq % 2, :], qTt[:, hq // 2, :],
                        start=True, stop=True), r=[("sw_kTd", prv), ("sw_qT",)], w=[pk])
                fk, ft = evf.next()
                S.add("dve", lambda e, pt=pt, ft=ft, hq=hq, nkeys=nkeys: e.tensor_tensor(
                    out=ft[:, 0:nkeys], in0=pt[:, 0:nkeys], in1=swT[:, hq, 0:nkeys], op=ALU.add),
                    r=[pk, ("swT",)], w=[fk])
                ptk, ptt = ptb.next()
                S.add("act", lambda e, ft=ft, ptt=ptt, nkeys=nkeys: e.activation(
                    out=ptt[:, 0:nkeys], in_=ft[:, 0:nkeys], func=AF.Exp), r=[fk], w=[ptk])
                bank, slot = hq // 6, hq % 6
                S.add("pe", lambda e, ptt=ptt, bank=bank, slot=slot, kvh=kvh, cur=cur, t=t: e.matmul(
                    acc[bank][:, slot * 65:(slot + 1) * 65], ptt[:, 0:128], vx[cur][:, kvh, 0:65],
                    start=True, stop=(t == 0)), r=[ptk, ("sw_vx", cur)], w=[("acc", bank, 0), ("acc", bank, 1)])
                if t > 0:
                    S.add("pe", lambda e, ptt=ptt, bank=bank, slot=slot, kvh=kvh, prv=prv: e.matmul(
                        acc[bank][:, slot * 65:(slot + 1) * 65], ptt[:, 128:256], vx[prv][:, kvh, 0:65],
                        start=False, stop=True), r=[ptk, ("sw_vx", prv)], w=[("acc", bank, 0), ("acc", bank, 1)])
            for bank in range(3):
                n = nb[bank]
                sk, sm = small.next()
                av = acc[bank][:, 0:n * 65].rearrange("p (h d) -> p h d", d=65)
                S.add("dve", lambda e, av=av, sm=sm, n=n, bank=bank: e.tensor_tensor(
                    out=sm[:, 0:n], in0=av[:, :, 64], in1=esk[:, bank * 6:bank * 6 + n], op=ALU.add),
                    r=[("acc", bank, 0), ("acc", bank, 1), ("esk2",)], w=[sk])
                S.add("dve", lambda e, sm=sm, n=n: e.reciprocal(out=sm[:, 8:8 + n], in_=sm[:, 0:n]), r=[sk], w=[sk])
                for s_ in range(n):
                    hq = bank * 6 + s_
                    S.add("dve", lambda e, av=av, sm=sm, s_=s_, hq=hq, ot=ot: e.tensor_scalar(
                        out=ot[:, hq * 64:(hq + 1) * 64], in0=av[:, s_, 0:64], scalar1=sm[:, 8 + s_:9 + s_],
                        scalar2=None, op0=ALU.mult), r=[("acc", bank, 0), ("acc", bank, 1), sk], w=[ok])
            store(obuf[t * 128:(t + 1) * 128, :], ot[:], [ok], [("obuf", t)])
        phase_c(li, w["sw_out"][j], h_src, h_dst)

    def da_layer(li, j, h_src, h_dst):
        lam_init = 0.8 - 0.6 * math.exp(-0.3 * li)
        new_phase(0)
        lamt = take([128, 256], F32)
        lsm = take([128, 8], F32)
        sublnS = take([128, 128], F32)
        Arena.lmark = Arena.off
        alloc_common()
        dma("sp", lamt, lam_d[j], [], [("lamt",)])
        dma("sp", sublnS, subln_d[j], [], [("subln",)])
        S.add("dve", lambda e: e.tensor_tensor(out=lamt[:, 0:64], in0=lamt[:, 0:64], in1=lamt[:, 64:128],
                                               op=ALU.mult), r=[("lamt",)], w=[("lamt",)])
        S.add("dve", lambda e: e.tensor_tensor(out=lamt[:, 128:192], in0=lamt[:, 128:192], in1=lamt[:, 192:256],
                                               op=ALU.mult), r=[("lamt",)], w=[("lamt",)])
        S.add("dve", lambda e: e.tensor_reduce(out=lsm[:, 0:1], in_=lamt[:, 0:64], axis=AX.X, op=ALU.add),
              r=[("lamt",)], w=[("lsm",)])
        S.add("dve", lambda e: e.tensor_reduce(out=lsm[:, 1:2], in_=lamt[:, 128:192], axis=AX.X, op=ALU.add),
              r=[("lamt",)], w=[("lsm",)])
        S.add("act", lambda e: e.activation(out=lsm[:, 2:4], in_=lsm[:, 0:2], func=AF.Exp),
              r=[("lsm",)], w=[("lsm2",)])
        S.add("dve", lambda e: e.scalar_tensor_tensor(out=lsm[:, 4:5], in0=lsm[:, 3:4], scalar=-lam_init,
                                                      in1=lsm[:, 2:3], op0=ALU.add, op1=ALU.subtract),
              r=[("lsm2",)], w=[("lsm3",)])
        S.add("dve", lambda e: e.tensor_scalar(out=sublnS[:], in0=sublnS[:], scalar1=(1.0 - lam_init),
                                               scalar2=None, op0=ALU.mult), r=[("subln",)], w=[("subln",)])
        for tg in range(NG):
            rows = slice(tg * 512, (tg + 1) * 512)
            xk, xt = load_T(h_src[rows, :], [("hbuf", tg)], True)
            xkeys = [(xk, tt) for tt in range(4)]

            def ev_qk(fc, pt, pk, tg=tg):
                ek, et = ev512.next()
                if fc < 8:
                    S.add("act", lambda e: e.activation(out=et[:], in_=pt[:, :], func=AF.Copy, scale=0.125),
                          r=pk, w=[ek])
                    store(qT_d[fc, :, tg * 512:(tg + 1) * 512], et[:], [ek], [("qT", fc, tg)])
                else:
                    S.add("act", lambda e: e.copy(out=et[:], in_=pt[:, :]), r=pk, w=[ek])
                    store(kT_d[fc - 8, :, tg * 512:(tg + 1) * 512], et[:], [ek], [("kT", fc - 8, tg)])

            proj(xt, xkeys, 8, w["da_in"][j][:, 0:2048], 2048, "feat", ev_qk)

            def ev_v(tt, c0, ncols, pt, pk, tg=tg):
                ek, et = ev512.next()
                r0 = tg * 512 + tt * 128
                S.add("act", lambda e: e.copy(out=et[:, 0:ncols], in_=pt[:, 0:ncols]), r=pk, w=[ek])
                store(s1_d[r0:r0 + 128, c0:c0 + ncols], et[:, 0:ncols], [ek], [("s1", tg * 4 + tt, c0)])

            proj(xt, xkeys, 8, w["da_in"][j][:, 2048:3072], 1024, "tok", ev_v)
        new_phase(Arena.lmark)
        tabD = take([128, 16, 128], F32)
        tabP = take([128, 16, 128], F32)
        kTs = [[take([128, T], BF16) for c in range(2)] for i in range(2)]
        vxs = [take([128, 32, 132], BF16) for i in range(2)]
        qTg = takes("da_q", 2, [128, 512], BF16)
        ptb = takes("da_pt", 4, [128, 512], BF16)
        osb = takes("da_o", 3, [128, 128], BF16)
        tmpf = takes("da_tf", 4, [128, 128], F32)
        dma("sp", tabD, tabD_d[:, :, :], [], [("tabD",)])
        dma("sp", tabP, tabP_d[:, :, :], [], [("tabP",)])
        for i in range(2):
            S.add("pool", lambda e, i=i: e.memset(vxs[i][:, :, 128:129], 1.0), r=[], w=[("da_kv", i)])
            S.add("pool", lambda e, i=i: e.memset(kTs[i][0][64:128, :], 0.0), r=[], w=[("da_kv", i)])
            S.add("pool", lambda e, i=i: e.memset(kTs[i][1][0:64, :], 0.0), r=[], w=[("da_kv", i)])
        for h in range(DBG.get("da_heads", 8)):
            sl = h % 2
            kkey = ("da_kv", sl)
            dma("sp", kTs[sl][0][0:64, :], kT_d[h, 0:64, :], [("kT", h, tg) for tg in range(NG)], [kkey])
            dma("sp", kTs[sl][1][64:128, :], kT_d[h, 64:128, :], [("kT", h, tg) for tg in range(NG)], [kkey])
            for q4 in range(NT // 8):
                dma("sp", vxs[sl][:, q4 * 8:(q4 + 1) * 8, 0:128],
                    s1_d[q4 * 1024:(q4 + 1) * 1024, h * 128:(h + 1) * 128].rearrange("(t p) v -> p t v", p=128),
                    [("s1", t, c0) for t in range(q4 * 8, q4 * 8 + 8) for c0 in (0, 512)], [kkey])
            for g in range(DBG.get("da_groups", NG)):
                qk, qt = qTg.next()
                dma("sp", qt[:], qT_d[h, :, g * 512:(g + 1) * 512], [("qT", h, g)], [qk])
                for bank in range(4):
                    S.add("pe", lambda e, bank=bank: e.matmul(acc[bank][:, :], zeros[:, 0:128], zeros[:, :],
                                                              start=True, stop=False),
                          r=[("zeros",)], w=[("acc", bank, 0), ("acc", bank, 1)])
                for c in range(2):
                    m = 2 * h + c
                    for kb in range(4 * g + 4):
                        j0 = max(0, kb - 4 * g)
                        pk, pt = mm.next()
                        S.add("pe", lambda e, pt=pt, c=c, kb=kb, j0=j0, qt=qt, sl=sl: e.matmul(
                            pt[:, j0 * 128:512], kTs[sl][c][:, kb * 128:(kb + 1) * 128],
                            qt[:, j0 * 128:512], start=True, stop=True),
                            r=[kkey, qk], w=[pk])
                        if DBG.get("da_stage", 4) < 2:
                            continue
                        ptk, ptt = ptb.next()
                        jc = j0
                        for jj in range(j0, 4):
                            dlt = 4 * g + jj - kb
                            if dlt >= 2:
                                break
                            tab = tabD if dlt == 0 else tabP
                            fk, ft = tmpf.next()
                            S.add("dve", lambda e, pt=pt, ft=ft, jj=jj, tab=tab, m=m: e.tensor_tensor(
                                out=ft[:], in0=pt[:, jj * 128:(jj + 1) * 128], in1=tab[:, m, :], op=ALU.add),
                                r=[pk, ("tabD",), ("tabP",)], w=[fk])
                            S.add("act", lambda e, ft=ft, ptt=ptt, jj=jj: e.activation(
                                out=ptt[:, jj * 128:(jj + 1) * 128], in_=ft[:], func=AF.Exp),
                                r=[fk], w=[(ptk, jj)])
                            jc = jj + 1
                        if jc < 4:
                            S.add("act", lambda e, pt=pt, ptt=ptt, jc=jc, m=m: e.activation(
                                out=ptt[:, jc * 128:512], in_=pt[:, jc * 128:512], func=AF.Exp,
                                bias=c31[:, m:m + 1], scale=1.0),
                                r=[pk, ("c31",)], w=[(ptk, jj) for jj in range(jc, 4)])
                        for jj in range(j0, 4 if DBG.get("da_stage", 4) >= 3 else 0):
                            bank = 2 * c + jj // 2
                            off = (jj % 2) * 256
                            S.add("pe", lambda e, ptt=ptt, jj=jj, bank=bank, off=off, kb=kb, sl=sl, g=g: e.matmul(
                                acc[bank][:, off:off + 129], ptt[:, jj * 128:(jj + 1) * 128], vxs[sl][:, kb, 0:129],
                                start=False, stop=(kb == 4 * g + jj)),
                                r=[(ptk, jj), kkey], w=[("acc", bank, jj % 2)])
                for jj in range(4 if DBG.get("da_stage", 4) >= 4 else 0):
                    a1 = acc[jj // 2][:, (jj % 2) * 256:(jj % 2) * 256 + 129]
                    a2 = acc[2 + jj // 2][:, (jj % 2) * 256:(jj % 2) * 256 + 129]
                    k1 = ("acc", jj // 2, jj % 2)
                    k2 = ("acc", 2 + jj // 2, jj % 2)
                    sk, sm = small.next()
                    S.add("dve", lambda e, a1=a1, sm=sm: e.reciprocal(out=sm[:, 0:1], in_=a1[:, 128:129]),
                          r=[k1], w=[sk])
                    S.add("dve", lambda e, a2=a2, sm=sm: e.reciprocal(out=sm[:, 1:2], in_=a2[:, 128:129]),
                          r=[k2], w=[sk])
                    S.add("dve", lambda e, sm=sm: e.tensor_tensor(out=sm[:, 2:3], in0=sm[:, 1:2], in1=lsm[:, 4:5],
                                                                  op=ALU.mult), r=[sk, ("lsm3",)], w=[sk])
                    fk, ft = tmpf.next()
                    S.add("dve", lambda e, a2=a2, sm=sm, ft=ft: e.tensor_scalar(
                        out=ft[:], in0=a2[:, 0:128], scalar1=sm[:, 2:3], scalar2=None, op0=ALU.mult),
                        r=[k2, sk], w=[fk])
                    fk2, ft2 = tmpf.next()
                    S.add("dve", lambda e, a1=a1, sm=sm, ft=ft, ft2=ft2: e.scalar_tensor_tensor(
                        out=ft2[:], in0=a1[:, 0:128], scalar=sm[:, 0:1], in1=ft[:], op0=ALU.mult, op1=ALU.add),
                        r=[k1, sk, fk], w=[fk2])
                    S.add("act", lambda e, ft=ft, ft2=ft2, sm=sm: e.activation(
                        out=ft[:], in_=ft2[:], func=AF.Square, accum_out=sm[:, 3:4]), r=[fk2, sk], w=[fk, (sk, "b")])
                    S.add("act", lambda e, sm=sm: e.activation(
                        out=sm[:, 4:5], in_=sm[:, 3:4], func=AF.Sqrt, bias=epsT[:, 1:2], scale=1.0 / 128.0),
                        r=[(sk, "b"), ("eps",)], w=[(sk, "c")])
                    S.add("dve", lambda e, sm=sm: e.reciprocal(out=sm[:, 5:6], in_=sm[:, 4:5]),
                          r=[(sk, "c")], w=[(sk, "d")])
                    ok, ot = osb.next()
                    S.add("dve", lambda e, sm=sm, ft2=ft2, ot=ot: e.scalar_tensor_tensor(
                        out=ot[:], in0=ft2[:], scalar=sm[:, 5:6], in1=sublnS[:], op0=ALU.mult, op1=ALU.mult),
                        r=[fk2, (sk, "d"), ("subln",)], w=[ok])
                    t = 4 * g + jj
                    store(obuf[t * 128:(t + 1) * 128, h * 128:(h + 1) * 128], ot[:], [ok], [("obuf", t)])
        phase_c(li, w["da_out"][j], h_src, h_dst)

    def hg_layer(li, j, h_src, h_dst):
        new_phase(0)
        hgc = take([128, 4, 128], F32)
        lbv = take([128, D], F32)
        oml = take([128, D], F32)
        gnt = take([128, D], F32)
        Arena.lmark = Arena.off
        lbt = take([128, 4, D], F32)
        dma("sp", hgc, hgc_d[:, :, :], [], [("hgc",)])
        dma("sp", gnt, gn_d[:, :], [], [("hg_gn",)])
        dma("sp", lbt, lb_d[:, :, :], [], [("lbt",)])
        S.add("act", lambda e: e.activation(out=lbt, in_=lbt, func=AF.Exp), r=[("lbt",)], w=[("lbt",)])
        S.add("dve", lambda e: e.tensor_tensor(out=oml[:], in0=lbt[:, 0, :], in1=lbt[:, 1, :], op=ALU.add),
              r=[("lbt",)], w=[("oml",)])
        S.add("dve", lambda e: e.tensor_tensor(out=oml[:], in0=oml[:], in1=lbt[:, 2, :], op=ALU.add),
              r=[("oml",), ("lbt",)], w=[("oml",)])
        S.add("dve", lambda e: e.tensor_tensor(out=oml[:], in0=oml[:], in1=lbt[:, 3, :], op=ALU.add),
              r=[("oml",), ("lbt",)], w=[("oml",)])
        S.add("dve", lambda e: e.reciprocal(out=oml[:], in_=oml[:]), r=[("oml",)], w=[("oml",)])
        S.add("dve", lambda e: e.tensor_copy(out=lbv[:], in_=lbt[:, 1, :]), r=[("lbt",)], w=[("lbv",)])
        for d_ in range(2, li + 1):
            S.add("dve", lambda e, d_=d_: e.tensor_tensor(out=lbv[:], in0=lbv[:], in1=lbt[:, d_, :], op=ALU.add),
                  r=[("lbt",), ("lbv",)], w=[("lbv",)])
        S.add("dve", lambda e: e.tensor_tensor(out=lbv[:], in0=lbv[:], in1=oml[:], op=ALU.mult),
              r=[("lbv",), ("oml",)], w=[("lbv",)])
        S.add("dve", lambda e: e.tensor_scalar(out=oml[:], in0=lbv[:], scalar1=-1.0, scalar2=1.0,
                                               op0=ALU.mult, op1=ALU.add), r=[("lbv",)], w=[("oml",)])
        new_phase(Arena.lmark)
        alloc_common()
        for tg in range(NG):
            rows = slice(tg * 512, (tg + 1) * 512)
            xk, xt = load_T(h_src[rows, :], [("hbuf", tg)], True)

            def ev(tt, c0, ncols, pt, pk, tg=tg):
                r0 = tg * 512 + tt * 128
                t = tg * 4 + tt
                kind = c0 // 1024
                cc = c0 % 1024
                if kind == 1:
                    fk, ft = evf.next()
                    S.add("act", lambda e: e.activation(out=ft[:], in_=pt[:, :], func=AF.Sigmoid), r=pk, w=[fk])
                    S.add("dve", lambda e: e.tensor_tensor(out=ft[:], in0=ft[:], in1=oml[:, cc:cc + 512],
                                                           op=ALU.mult), r=[fk, ("oml",)], w=[fk])
                    S.add("dve", lambda e: e.tensor_tensor(out=ft[:], in0=ft[:], in1=lbv[:, cc:cc + 512],
                                                           op=ALU.add), r=[fk, ("lbv",)], w=[fk])
                    ek, et = ev512.next()
                    S.add("dve", lambda e: e.tensor_scalar(out=et[:], in0=ft[:], scalar1=-1.0, scalar2=1.0,
                                                           op0=ALU.mult, op1=ALU.add), r=[fk], w=[ek])
                    store(s2_d[r0:r0 + 128, cc:cc + 512], et[:], [ek], [("s2", t, cc)])
                    fk2, ft2 = evf.next()
                    S.add("act", lambda e: e.activation(out=ft2[:], in_=ft[:], func=AF.Ln), r=[fk], w=[fk2])
                    store(lf_d[r0:r0 + 128, cc:cc + 512], ft2[:], [fk2], [("lf", t, cc)])
                else:
                    ek, et = ev512.next()
                    dst = {0: s1_d, 2: s3_d, 3: s4_d}[kind]
                    nm = {0: "s1", 2: "s3", 3: "s4"}[kind]
                    if kind == 2:
                        S.add("act", lambda e: e.copy(out=et[:], in_=pt[:, :]), r=pk, w=[ek])
                    else:
                        S.add("act", lambda e: e.activation(out=et[:], in_=pt[:, :], func=AF.Silu), r=pk, w=[ek])
                    store(dst[r0:r0 + 128, cc:cc + 512], et[:], [ek], [(nm, t, cc)])

            proj(xt, [(xk, tt) for tt in range(4)], 8, w["hg_in"][j], 4096, "tok", ev)
        new_phase(Arena.lmark)
        T2, T3, M2, IND = hgc[:, 0, :], hgc[:, 1, :], hgc[:, 2, :], hgc[:, 3, 0:2]
        lfb = takes("hg_lf", 2, [128, D], F32)
        inb = takes("hg_in", 2, [128, 4, D], BF16)
        exb = take([128, 3, D], F32)
        qg = take([128, D], BF16)
        kg = take([128, D], BF16)
        kd0 = take([128, D], BF16)
        kd1 = take([128, D], BF16)
        qgT = take([128, 8, 128], BF16)
        qgT0 = take([128, 8, 128], BF16)
        qgT1 = take([128, 8, 128], BF16)
        kgT = take([128, 8, 128], BF16)
        dec = take([128, 16], F32)
        Sst = take([128, 8, 128], F32)
        Sbf = [take([128, 8, 128], BF16) for i in range(2)]
        ATb = takes("hg_AT", 3, [128, 128], BF16)
        osq = take([128, 128], F32)
        ofl = take([128, D], F32)
        osb = takes("hg_o", 2, [128, D], BF16)
        lhi = take([128, D], BF16)
        llo = take([128, D], BF16)
        cstb = take([128, 4, 128], BF16)
        S.add("dve", lambda e: e.tensor_copy(out=cstb, in_=hgc), r=[("hgc",)], w=[("cstb",)])
        T2b, T3b, INDW = cstb[:, 0, :], cstb[:, 1, :], cstb[:, 3, :]
        S.add("pool", lambda e: e.memset(Sst[:], 0.0), r=[], w=[("hg_S", h) for h in range(8)])
        S.add("pool", lambda e: e.memset(Sbf[0][:], 0.0), r=[], w=[("hg_Sbf", 0, h) for h in range(8)])
        S.add("pool", lambda e: e.memset(qgT0[:], 0.0), r=[], w=[("hg_qgT0",)])
        S.add("pool", lambda e: e.memset(qgT1[:], 0.0), r=[], w=[("hg_qgT1",)])
        S.add("pool", lambda e: e.memset(kd0[:], 0.0), r=[], w=[("hg_kd", 0), ("hg_kd", 1)])
        S.add("pool", lambda e: e.memset(kd1[:], 0.0), r=[], w=[("hg_kd", 0), ("hg_kd", 1)])
        for t in range(DBG.get("hg_tiles", NT)):
            lk, lt = lfb.next()
            dma("sp", lt[:], lf_d[t * 128:(t + 1) * 128, :], [("lf", t, 0), ("lf", t, 512)], [lk])
            ik, it = inb.next()
            for n_, (nm, src) in enumerate((("s1", s1_d), ("s2", s2_d), ("s3", s3_d), ("s4", s4_d))):
                dma("sp", it[:, n_, :], src[t * 128:(t + 1) * 128, :], [(nm, t, 0), (nm, t, 512)], [(ik, n_)])
            S.add("act", lambda e, lt=lt: e.copy(out=lhi, in_=lt), r=[lk], w=[("lhi",)])
            S.add("dve", lambda e, lt=lt: e.tensor_tensor(out=llo, in0=lt, in1=lhi, op=ALU.subtract),
                  r=[lk, ("lhi",)], w=[("llo",)])
            for hf in range(2):
                for part, pkey, first in ((lhi, "lhi", True), (llo, "llo", False)):
                    S.add("pe", lambda e, hf=hf, part=part, first=first: e.matmul(
                        acc[hf][:, :], T2b, part[:, hf * 512:(hf + 1) * 512], start=first, stop=not first),
                        r=[(pkey,), ("cstb",)], w=[("acc", hf, 0), ("acc", hf, 1)])
                for part, pkey, first in ((lhi, "lhi", True), (llo, "llo", False)):
                    S.add("pe", lambda e, hf=hf, part=part, first=first: e.matmul(
                        acc[2 + hf][:, :], T3b, part[:, hf * 512:(hf + 1) * 512], start=first, stop=not first),
                        r=[(pkey,), ("cstb",)], w=[("acc", 2 + hf, 0), ("acc", 2 + hf, 1)])
            for hf in range(2):
                cs = slice(hf * 512, (hf + 1) * 512)
                S.add("act", lambda e, hf=hf, cs=cs: e.activation(out=exb[:, 0, cs], in_=acc[hf][:, :], func=AF.Exp),
                      r=[("acc", hf, 0), ("acc", hf, 1)], w=[("hg_ex", 0, hf)])
                S.add("act", lambda e, hf=hf, cs=cs: e.activation(out=exb[:, 2, cs], in_=acc[2 + hf][:, :],
                                                                  func=AF.Exp),
                      r=[("acc", 2 + hf, 0), ("acc", 2 + hf, 1)], w=[("hg_ex", 2, hf)])
                S.add("dve", lambda e, cs=cs, it=it: e.tensor_tensor(out=qg[:, cs], in0=it[:, 0, cs],
                                                                     in1=exb[:, 0, cs], op=ALU.mult),
                      r=[(ik, 0), ("hg_ex", 0, hf)], w=[("hg_qg", hf)])
                S.add("dve", lambda e, cs=cs: e.reciprocal(out=exb[:, 1, cs], in_=exb[:, 0, cs]),
                      r=[("hg_ex", 0, hf)], w=[("hg_ex", 1, hf)])
                S.add("dve", lambda e, cs=cs, it=it: e.tensor_tensor(out=kg[:, cs], in0=it[:, 1, cs],
                                                                     in1=exb[:, 1, cs], op=ALU.mult),
                      r=[(ik, 1), ("hg_ex", 1, hf)], w=[("hg_kg", hf)])
                S.add("dve", lambda e, cs=cs, it=it: e.tensor_tensor(out=kd0[0:64, cs], in0=it[0:64, 1, cs],
                                                                     in1=exb[0:64, 2, cs], op=ALU.mult),
                      r=[(ik, 1), ("hg_ex", 2, hf)], w=[("hg_kd", hf)])
                S.add("dve", lambda e, cs=cs, it=it: e.tensor_tensor(out=kd1[64:128, cs], in0=it[64:128, 1, cs],
                                                                     in1=exb[64:128, 2, cs], op=ALU.mult),
                      r=[(ik, 1), ("hg_ex", 2, hf)], w=[("hg_kd", hf)])
            for h in range(8):
                dk, dt_ = mm.next()
                for part, pkey, first in ((lhi, "lhi", True), (llo, "llo", False)):
                    S.add("pe", lambda e, h=h, part=part, first=first, dt_=dt_: e.matmul(
                        dt_[:, 0:128], part[:, h * 128:(h + 1) * 128], INDW, start=first, stop=not first),
                        r=[(pkey,), ("cstb",)], w=[dk])
                for c_ in range(2):
                    S.add("act", lambda e, h=h, dt_=dt_, c_=c_: e.activation(
                        out=dec[:, 2 * h + c_:2 * h + c_ + 1], in_=dt_[:, 64 * c_:64 * c_ + 1],
                        func=AF.Exp), r=[dk], w=[("hg_dec",)])
            for h in range(8):
                S.add("pe", lambda e, h=h: e.transpose(trb[:, h * 128:(h + 1) * 128], qg[:, h * 128:(h + 1) * 128],
                                                       ident[:]),
                      r=[("hg_qg", h // 4), ("ident",)], w=[("trb", h)])
            trv = trb[:, :].rearrange("p (k t) -> p k t", t=128)
            S.add("dve", lambda e: e.tensor_copy(out=qgT[:], in_=trv), r=[("trb", h) for h in range(8)],
                  w=[("hg_qgT",)])
            S.add("act", lambda e: e.copy(out=qgT0[:, :, 0:64], in_=trv[:, :, 0:64]),
                  r=[("trb", h) for h in range(8)], w=[("hg_qgT0",)])
            S.add("act", lambda e: e.copy(out=qgT1[:, :, 64:128], in_=trv[:, :, 64:128]),
                  r=[("trb", h) for h in range(8)], w=[("hg_qgT1",)])
            for h in range(8):
                S.add("pe", lambda e, h=h: e.transpose(trb[:, h * 128:(h + 1) * 128], kg[:, h * 128:(h + 1) * 128],
                                                       ident[:]),
                      r=[("hg_kg", h // 4), ("ident",)], w=[("trb", h)])
            S.add("dve", lambda e: e.tensor_copy(out=kgT[:], in_=trv), r=[("trb", h) for h in range(8)],
                  w=[("hg_kgT",)])
            ok, ot = osb.next()
            for h in range(8):
                hs = slice(h * 128, (h + 1) * 128)
                pk, pt = mm.next()
                S.add("pe", lambda e, pt=pt, h=h: e.matmul(pt[:, 0:128], kgT[:, h, :], qgT[:, h, :],
                                                           start=True, stop=True),
                      r=[("hg_kgT",), ("hg_qgT",)], w=[pk])
                ak, at = ATb.next()
                S.add("dve", lambda e, pt=pt, at=at: e.tensor_tensor(out=at[:], in0=pt[:, 0:128], in1=M2,
                                                                     op=ALU.mult), r=[pk, ("hgc",)], w=[ak])
                pk2, pt2 = mm.next()
                S.add("pe", lambda e, pt2=pt2, hs=hs, it=it: e.matmul(pt2[:, 0:128], kd0[:, hs], it[:, 2, hs],
                                                                      start=True, stop=True),
                      r=[("hg_kd", h // 4), (ik, 2)], w=[pk2])
                S.add("dve", lambda e, pt2=pt2, h=h: e.scalar_tensor_tensor(
                    out=Sst[:, h, :], in0=Sst[:, h, :], scalar=dec[:, 2 * h:2 * h + 1], in1=pt2[:, 0:128],
                    op0=ALU.mult, op1=ALU.add), r=[pk2, ("hg_dec",), ("hg_S", h)], w=[("hg_S", h)])
                S.add("act", lambda e, h=h: e.copy(out=Sbf[1][:, h, :], in_=Sst[:, h, :]),
                      r=[("hg_S", h)], w=[("hg_Sbf", 1, h)])
                obank = acc[h // 4][:, (h % 4) * 128:(h % 4 + 1) * 128]
                okey = ("acc", h // 4, h % 4 // 2)
                S.add("pe", lambda e, obank=obank, at=at, hs=hs, it=it: e.matmul(obank, at[:], it[:, 2, hs],
                                                                                 start=True, stop=False),
                      r=[ak, (ik, 2)], w=[okey])
                S.add("pe", lambda e, obank=obank, h=h: e.matmul(obank, qgT0[:, h, :], Sbf[0][:, h, :],
                                                                 start=False, stop=False),
                      r=[("hg_qgT0",), ("hg_Sbf", 0, h)], w=[okey])
                S.add("pe", lambda e, obank=obank, h=h: e.matmul(obank, qgT1[:, h, :], Sbf[1][:, h, :],
                                                                 start=False, stop=True),
                      r=[("hg_qgT1",), ("hg_Sbf", 1, h)], w=[okey])
                pk3, pt3 = mm.next()
                S.add("pe", lambda e, pt3=pt3, hs=hs, it=it: e.matmul(pt3[:, 0:128], kd1[:, hs],
                                                                      it[:, 2, hs], start=True, stop=True),
                      r=[("hg_kd", h // 4), (ik, 2)], w=[pk3])
                S.add("dve", lambda e, pt3=pt3, h=h: e.scalar_tensor_tensor(
                    out=Sst[:, h, :], in0=Sst[:, h, :], scalar=dec[:, 2 * h + 1:2 * h + 2], in1=pt3[:, 0:128],
                    op0=ALU.mult, op1=ALU.add), r=[pk3, ("hg_dec",), ("hg_S", h)], w=[("hg_S", h)])
                S.add("act", lambda e, h=h: e.copy(out=Sbf[0][:, h, :], in_=Sst[:, h, :]),
                      r=[("hg_S", h)], w=[("hg_Sbf", 0, h)])
                sk, sm = small.next()
                S.add("act", lambda e, obank=obank, hs=hs: e.copy(out=ofl[:, hs], in_=obank),
                      r=[okey], w=[("hg_of", h)])
                S.add("act", lambda e, sm=sm, hs=hs: e.activation(out=osq[:], in_=ofl[:, hs], func=AF.Square,
                                                                  accum_out=sm[:, 0:1]),
                      r=[("hg_of", h)], w=[("hg_osq",), sk])
                S.add("act", lambda e, sm=sm: e.activation(out=sm[:, 1:2], in_=sm[:, 0:1], func=AF.Sqrt,
                                                           bias=epsT[:, 1:2], scale=1.0 / 128.0),
                      r=[sk, ("eps",)], w=[(sk, "b")])
                S.add("dve", lambda e, sm=sm: e.reciprocal(out=sm[:, 2:3], in_=sm[:, 1:2]),
                      r=[(sk, "b")], w=[(sk, "c")])
                S.add("dve", lambda e, sm=sm, hs=hs: e.scalar_tensor_tensor(
                    out=ofl[:, hs], in0=ofl[:, hs], scalar=sm[:, 2:3], in1=gnt[:, hs], op0=ALU.mult, op1=ALU.mult),
                    r=[("hg_of", h), (sk, "c"), ("hg_gn",)], w=[("hg_of", h)])
                S.add("pool", lambda e, hs=hs, it=it, ot=ot: e.tensor_tensor(out=ot[:, hs], in0=ofl[:, hs],
                                                                             in1=it[:, 3, hs], op=ALU.mult),
                      r=[("hg_of", h), (ik, 3)], w=[ok])
            store(obuf[t * 128:(t + 1) * 128, :], ot[:], [ok], [("obuf", t)])
        phase_c(li, w["hg_out"][j], h_src, h_dst)

    for n, li in enumerate(layers):
        h_src = x_in if n == 0 else hbuf
        h_dst = out_d if n == len(layers) - 1 else hbuf
        kind, j = li % 3, li // 3
        if kind == 0:
            da_layer(li, j, h_src, h_dst)
        elif kind == 1:
            hg_layer(li, j, h_src, h_dst)
        else:
            swa_layer(li, j, h_src, h_dst)

    S.emit(nc, st)
    st.close()
    return nc


def host_consts(rel_bias, sw_sinks, ln_g, ln_b, da_lambda, da_subln, hg_gnorm, hg_lower_bound):
    f = np.float32
    c = {}
    c["ident"] = np.eye(128, dtype=f)
    k = np.arange(128)[:, None]
    q = np.arange(128)[None, :]
    rel_d = q - k
    bd = t5_bucket_np(rel_d)
    bp = t5_bucket_np(rel_d + 128)
    tabD = np.empty((128, 16, 128), f)
    tabP = np.empty((128, 16, 128), f)
    swT = np.empty((128, 16, 256), f)
    for m in range(16):
        tabD[:, m, :] = np.where(rel_d >= 0, rel_bias[bd, m], f(NEG))
        tabP[:, m, :] = rel_bias[bp, m]
        swT[:, m, 0:128] = tabD[:, m, :]
        swT[:, m, 128:256] = np.where(rel_d + 128 < 128, rel_bias[bp, m], f(NEG))
    c["tabD"], c["tabP"], c["swT"] = tabD, tabP, swT
    c["c31"] = np.ascontiguousarray(np.broadcast_to(rel_bias[31][None, :], (128, 16))).astype(f)
    c["sinks"] = np.ascontiguousarray(np.broadcast_to(sw_sinks[0][None, :], (128, 16))).astype(f)
    s = np.arange(128)[:, None]
    t = np.arange(128)[None, :]
    same = (s // 64) == (t // 64)
    hgc = np.zeros((128, 4, 128), f)
    hgc[:, 0, :] = (same & (s <= t)).astype(f)
    hgc[:, 1, :] = (same & (s > t)).astype(f)
    hgc[:, 2, :] = (same & (s <= t)).astype(f)
    hgc[:, 3, 0:64] = (np.arange(128) < 64).astype(f)[:, None]
    hgc[:, 3, 64:128] = (np.arange(128) >= 64).astype(f)[:, None]
    c["hgc"] = hgc
    lnbc = np.empty((DEPTH, 4, 128, D), f)
    for i in range(DEPTH):
        lnbc[i, 0] = ln_g[i, 0][None, :]
        lnbc[i, 1] = ln_b[i, 0][None, :]
        lnbc[i, 2] = ln_g[i, 1][None, :]
        lnbc[i, 3] = ln_b[i, 1][None, :]
    c["lnbc"] = lnbc
    c["lam"] = np.ascontiguousarray(np.broadcast_to(da_lambda.reshape(2, 1, 256), (2, 128, 256))).astype(f)
    c["subln"] = np.ascontiguousarray(np.broadcast_to(da_subln.reshape(2, 1, 128), (2, 128, 128))).astype(f)
    c["gnorm"] = np.ascontiguousarray(np.broadcast_to(np.tile(hg_gnorm[0], 8)[None, :], (128, D))).astype(f)
    c["hglb"] = np.ascontiguousarray(np.broadcast_to(hg_lower_bound[None, :, :], (128, 4, D))).astype(f)
    return c


_NC_CACHE = {}


def run_layers(layers, h_in, inputs):
    key = tuple(layers)
    if key not in _NC_CACHE:
        _NC_CACHE[key] = build_program(list(layers))
    nc = _NC_CACHE[key]
    f = np.float32
    g = lambda k: np.ascontiguousarray(np.asarray(inputs[k], dtype=f))
    consts = host_consts(g("rel_bias"), g("sw_sinks"), g("ln_g"), g("ln_b"), g("da_lambda"), g("da_subln"),
                         g("hg_gnorm"), g("hg_lower_bound"))
    shared = dict(consts)
    for k in ("da_w_in", "da_w_out", "hg_w_in", "hg_w_out", "sw_w_in", "sw_w_out", "w_up", "w_down", "w_ple",
              "w_ple_gate"):
        shared[k] = g(k)
    p = g("p")
    in_maps = []
    for c in range(8):
        b = c // 2
        m = dict(shared)
        m["x"] = np.ascontiguousarray(h_in[b])
        m["p"] = np.ascontiguousarray(p[:, b])
        in_maps.append(m)
    res = run_bass_kernel_spmd(nc, in_maps, core_ids=list(range(8)))
    out = np.empty((4, T, D), f)
    for b in range(4):
        out[b, 0:T // 2] = res.results[2 * b]["out"][0:T // 2]
        out[b, T // 2:T] = res.results[2 * b + 1]["out"][T // 2:T]
    return out


def kernel(**inputs):
    x = np.ascontiguousarray(np.asarray(inputs["x"], dtype=np.float32))
    return run_layers((0, 1, 2, 3), x, inputs)
```

```python
import math
from contextlib import ExitStack
import numpy as np
import concourse.bass as bass
import concourse.mybir as mybir
from concourse.bass_utils import run_bass_kernel_spmd

F32 = mybir.dt.float32
BF16 = mybir.dt.bfloat16
ALU = mybir.AluOpType
AF = mybir.ActivationFunctionType
AX = mybir.AxisListType

T = 4096
D = 1024
NT = T // 128
NG = T // 512
DEPTH = 4
ALPHA = (2 * DEPTH) ** 0.25
LN_EPS = 1e-5
RMS_EPS = 1e-6
NEG = -30000.0
DBG = {}


class Op:
    __slots__ = ("eng", "fn", "deps", "dma", "sig", "sem", "tick", "prev_dma")


class Sched:
    ENG = ["pe", "act", "dve", "pool", "sp"]
    NDS = 8

    def __init__(self):
        self.ops = {e: [] for e in self.ENG}
        self.lastw = {}
        self.readers = {}
        self.ndma = {e: 0 for e in self.ENG}
        self.pending = {e: [] for e in self.ENG}
        self.recent_dma = {e: [] for e in self.ENG}

    def fence(self):
        lasts = []
        for e in self.ENG:
            comp = [o for o in self.ops[e] if not o.dma]
            if comp:
                lasts.append(comp[-1])
            lasts.extend(self.recent_dma[e])
        for e in self.ENG:
            self.pending[e] = list(lasts)

    def add(self, eng, fn, r=(), w=(), dma=False):
        op = Op()
        op.eng, op.fn, op.dma, op.sig = eng, fn, dma, False
        op.deps = []
        op.sem = None
        op.tick = 0
        op.prev_dma = None

        def dep(d, raw):
            if d is op or d in op.deps:
                return
            if (not dma) and (not d.dma) and d.eng == eng:
                if eng == "pe" or not raw:
                    return
            op.deps.append(d)

        for k in r:
            lw = self.lastw.get(k)
            if lw is not None:
                dep(lw, True)
        for k in w:
            lw = self.lastw.get(k)
            if lw is not None:
                dep(lw, False)
            for rd in self.readers.get(k, ()):
                dep(rd, False)
        for k in r:
            self.readers.setdefault(k, []).append(op)
        for k in w:
            self.lastw[k] = op
            self.readers[k] = []
        if self.pending[eng]:
            for d in self.pending[eng]:
                if d is op or d in op.deps:
                    continue
                if (not dma) and (not d.dma) and d.eng == eng:
                    continue
                op.deps.append(d)
            self.pending[eng] = []
        for d in op.deps:
            d.sig = True
        if dma:
            op.sig = True
            self.recent_dma[eng] = (self.recent_dma[eng] + [op])[-self.NDS:]
        self.ops[eng].append(op)
        return op

    def emit(self, nc, stack):
        esem = {e: stack.enter_context(nc.semaphore("s_" + e)) for e in self.ENG}
        dsem = {e: [stack.enter_context(nc.semaphore("d_%s%d" % (e, i))) for i in range(self.NDS)]
                for e in self.ENG}
        fin = stack.enter_context(nc.semaphore("fin"))
        for e in self.ENG:
            cnt = 0
            nd = 0
            for op in self.ops[e]:
                if op.dma:
                    op.sem = dsem[e][nd % self.NDS]
                    op.tick = 16 * (nd // self.NDS + 1)
                    nd += 1
                elif op.sig:
                    cnt += 1
                    op.sem = esem[e]
                    op.tick = cnt
        block = stack.enter_context(nc.Block())

        def run(e, eng):
            known = {}
            for op in self.ops[e]:
                waits = [(d.sem, d.tick) for d in op.deps]
                if op.dma and op.tick > 16:
                    waits.append((op.sem, op.tick - 16))
                for sem, val in waits:
                    if known.get(id(sem), 0) >= val:
                        continue
                    eng.wait_ge(sem, val)
                    known[id(sem)] = val
                ins = op.fn(eng)
                if op.dma:
                    ins.then_inc(op.sem, 16)
                elif op.sig:
                    ins.then_inc(op.sem, 1)
            nd = 0
            last = {}
            for op in self.ops[e]:
                if op.dma:
                    last[id(op.sem)] = (op.sem, op.tick)
            for sem, val in last.values():
                if known.get(id(sem), 0) < val:
                    eng.wait_ge(sem, val)

        @block.tensor
        def _(eng):
            run("pe", eng)

        @block.scalar
        def _(eng):
            run("act", eng)

        @block.vector
        def _(eng):
            run("dve", eng)

        @block.gpsimd
        def _(eng):
            run("pool", eng)

        @block.sync
        def _(eng):
            run("sp", eng)


class Rot:
    def __init__(self, name, tiles):
        self.name, self.tiles, self.i = name, tiles, 0

    def next(self):
        i = self.i % len(self.tiles)
        self.i += 1
        return (self.name, i), self.tiles[i]


def t5_bucket_np(rel):
    n = np.maximum(rel, 0)
    nf = np.maximum(n, 1).astype(np.float32)
    large = 16 + (np.log(nf / np.float32(16)) / np.float32(math.log(8.0)) * np.float32(16)).astype(np.int32)
    large = np.minimum(large, 31)
    return np.where(n < 16, n, large)


def build_program(layers, in_from_x=True):
    nc = bass.Bass("TRN2", target_bir_lowering=False)
    S = Sched()
    st = ExitStack()

    def din(name, shape, dt=F32):
        return nc.dram_tensor(name, list(shape), dt, kind="ExternalInput").ap()

    def dscr(name, shape, dt):
        return nc.dram_tensor(name, list(shape), dt, kind="Internal").ap()

    x_in = din("x", [T, D])
    p_in = din("p", [DEPTH, T, 256])
    out_d = nc.dram_tensor("out", [T, D], F32, kind="ExternalOutput").ap()
    ident_d = din("ident", [128, 128])
    tabD_d = din("tabD", [128, 16, 128])
    tabP_d = din("tabP", [128, 16, 128])
    c31_d = din("c31", [128, 16])
    swT_d = din("swT", [128, 16, 256])
    esk_d = din("sinks", [128, 16])
    hgc_d = din("hgc", [128, 4, 128])
    lnb_d = din("lnbc", [DEPTH, 4, 128, D])
    lam_d = din("lam", [2, 128, 256])
    subln_d = din("subln", [2, 128, 128])
    gn_d = din("gnorm", [128, D])
    lb_d = din("hglb", [128, 4, D])
    w = {}
    w["da_in"] = din("da_w_in", [2, D, 3072])
    w["da_out"] = din("da_w_out", [2, D, D])
    w["hg_in"] = din("hg_w_in", [1, D, 4096])
    w["hg_out"] = din("hg_w_out", [1, D, D])
    w["sw_in"] = din("sw_w_in", [1, D, 1280])
    w["sw_out"] = din("sw_w_out", [1, D, D])
    w["up"] = din("w_up", [DEPTH, D, 4096])
    w["down"] = din("w_down", [DEPTH, 4096, D])
    w["ple"] = din("w_ple", [DEPTH, 256, D])
    w["gate"] = din("w_ple_gate", [DEPTH, D, D])

    hbuf = dscr("hbuf", [T, D], F32)
    obuf = dscr("obuf", [T, D], BF16)
    qT_d = dscr("qT", [8, 128, T], BF16)
    kT_d = dscr("kT", [8, 128, T], BF16)
    s1_d = dscr("s1", [T, D], BF16)
    s2_d = dscr("s2", [T, D], BF16)
    s3_d = dscr("s3", [T, D], BF16)
    s4_d = dscr("s4", [T, D], BF16)
    lf_d = dscr("lf", [T, D], F32)

    def sb(name, shape, dt):
        return st.enter_context(nc.sbuf_tensor("sb_" + name, list(shape), dt))

    def ps(name, shape, dt):
        return st.enter_context(nc.psum_tensor("ps_" + name, list(shape), dt))

    mm = Rot("mm", [ps("mm%d" % i, [128, 512], F32) for i in range(2)])
    acc = [ps("acc%d" % i, [128, 512], F32) for i in range(4)]
    trb = ps("trb", [128, 1024], BF16)
    aux = ps("aux", [128, 512], F32)

    ident = sb("ident", [128, 128], BF16)
    identf = sb("identf", [128, 128], F32)
    c31 = sb("c31", [128, 16], F32)
    epsT = sb("epsT", [128, 2], F32)
    zeros = sb("zeros", [128, 512], BF16)
    wsl = Rot("w", [sb("w%d" % i, [128, 8, 512], BF16) for i in range(5)])
    ev512 = Rot("ev", [sb("ev%d" % i, [128, 512], BF16) for i in range(4)])
    evf = Rot("evf", [sb("evf%d" % i, [128, 512], F32) for i in range(3)])
    small = Rot("sm", [sb("sm%d" % i, [128, 16], F32) for i in range(6)])
    ARENA_N = 73728
    arena_t = sb("arena", [128, ARENA_N], BF16)

    class Arena:
        off = 0
        cnt = 0

    def take(shape, dt):
        n = 1
        for s_ in shape[1:]:
            n *= s_
        ne = n * 2 if dt == F32 else n
        ap = arena_t[:, Arena.off:Arena.off + ne]
        Arena.off += ne
        assert Arena.off <= ARENA_N, Arena.off
        if dt == F32:
            ap = ap.bitcast(F32)
        if len(shape) == 3:
            ap = ap.rearrange("p (a b) -> p a b", b=shape[2])
        return ap

    def takes(name, n, shape, dt):
        Arena.cnt += 1
        return Rot("%s_%d" % (name, Arena.cnt), [take(shape, dt) for _ in range(n)])

    com = {}

    def alloc_common():
        com["stg_f"] = takes("stgf", 1, [128, 4, D], F32)
        com["stg_b"] = takes("stgb", 2, [128, 4, D], BF16)
        com["xT"] = takes("xT", 2, [128, 8, 512], BF16)

    def new_phase(mark):
        S.fence()
        Arena.off = mark

    const_ops = []

    def dma(eng, out, in_, r, wk):
        return S.add(eng, lambda e, o=out, i=in_: e.dma_start(out=o, in_=i), r=r, w=wk, dma=True)

    dma("sp", identf[:], ident_d[:, :], [], [("identf",)])
    S.add("dve", lambda e: e.tensor_copy(out=ident[:], in_=identf[:]), r=[("identf",)], w=[("ident",)])
    dma("sp", c31[:], c31_d[:, :], [], [("c31",)])
    S.add("pool", lambda e: e.memset(zeros[:], 0.0), r=[], w=[("zeros",)])
    S.add("pool", lambda e: e.memset(epsT[:, 0:1], LN_EPS), r=[], w=[("eps",)])
    S.add("pool", lambda e: e.memset(epsT[:, 1:2], RMS_EPS), r=[], w=[("eps",)])

    def transposes(src_tile, src_key, nkc, xt_tile, xt_key, tt):
        for kc in range(nkc):
            S.add("pe", lambda e, kc=kc: e.transpose(trb[:, kc * 128:(kc + 1) * 128],
                                                     src_tile[:, tt, kc * 128:(kc + 1) * 128], ident[:]),
                  r=[src_key, ("ident",)], w=[("trb", kc)])
        S.add("dve", lambda e: e.tensor_copy(
            out=xt_tile[:, 0:nkc, tt * 128:(tt + 1) * 128],
            in_=trb[:, 0:nkc * 128].rearrange("p (k t) -> p k t", t=128)),
            r=[("trb", kc) for kc in range(nkc)], w=[xt_key])

    def load_T(src_rows_ap, src_keys, is_f32, nfeat=D):
        nkc = nfeat // 128
        bkey, bt = com["stg_b"].next()
        if is_f32:
            fkey, ft = com["stg_f"].next()
            dma("sp", ft[:, :, 0:nfeat], src_rows_ap.rearrange("(t p) d -> p t d", p=128), src_keys, [fkey])
            for tt in range(4):
                S.add("act", lambda e, tt=tt: e.copy(out=bt[:, tt, 0:nfeat], in_=ft[:, tt, 0:nfeat]),
                      r=[fkey], w=[(bkey, tt)])
        else:
            dma("sp", bt[:, :, 0:nfeat], src_rows_ap.rearrange("(t p) d -> p t d", p=128), src_keys,
                [(bkey, tt) for tt in range(4)])
        xkey, xt = com["xT"].next()
        for tt in range(4):
            transposes(bt, (bkey, tt), nkc, xt, (xkey, tt), tt)
        return xkey, xt

    def proj(xt, xkeys, KC, wd, N, mode, evac, lhs_fn=None):
        npieces = max(1, KC // 8)
        kper = min(KC, 8)
        c0 = 0
        while c0 < N:
            ncols = min(512, N - c0)
            for kp in range(npieces):
                wkey, wt = wsl.next()
                dma("pool", wt[:, 0:kper, 0:ncols],
                    wd[kp * 1024:kp * 1024 + kper * 128, c0:c0 + ncols].rearrange("(c p) n -> p c n", p=128),
                    [], [wkey])
                if mode == "tok":
                    for tt in range(4):
                        pk = [("acc", tt, 0), ("acc", tt, 1)]
                        for k in range(kper):
                            kk = kp * 8 + k
                            S.add("pe", lambda e, tt=tt, k=k, kk=kk, wt=wt, ncols=ncols, kp=kp: e.matmul(
                                acc[tt][:, 0:ncols], xt[:, kk, tt * 128:(tt + 1) * 128], wt[:, k, 0:ncols],
                                start=(kk == 0), stop=(kk == KC - 1)),
                                r=[wkey] + xkeys, w=pk)
                        if kp == npieces - 1:
                            evac(tt, c0, ncols, acc[tt], pk)
                else:
                    for sbk in range(ncols // 128):
                        pk, pt = mm.next()
                        for k in range(kper):
                            S.add("pe", lambda e, k=k, sbk=sbk, wt=wt, pt=pt: e.matmul(
                                pt[:, :], wt[:, k, sbk * 128:(sbk + 1) * 128], xt[:, k, :],
                                start=(k == 0), stop=(k == kper - 1)),
                                r=[wkey] + xkeys, w=[pk])
                        evac((c0 // 128) + sbk, pt, [pk])
            c0 += ncols

    def store(dst_ap, src_ap, rkeys, wkeys, eng="sp"):
        dma(eng, dst_ap, src_ap, rkeys, wkeys)

    def layer_norm(ht, hkey, tt, gi):
        k1, sm = small.next()
        lnbc = com["lnbc"]
        xin = ht[:, tt, :]
        S.add("dve", lambda e: e.bn_stats(out=sm[:, 0:6], in_=ht[:, tt, 0:512]), r=[(hkey, tt)], w=[k1])
        k2, sm2 = small.next()
        S.add("dve", lambda e: e.bn_stats(out=sm2[:, 0:6], in_=ht[:, tt, 512:1024]), r=[(hkey, tt)], w=[k2])
        k3, sm3 = small.next()
        S.add("dve", lambda e: e.tensor_copy(out=sm3[:, 0:6], in_=sm[:, 0:6]), r=[k1], w=[k3])
        S.add("dve", lambda e: e.tensor_copy(out=sm3[:, 6:12], in_=sm2[:, 0:6]), r=[k2], w=[k3])
        k4, sm4 = small.next()
        S.add("dve", lambda e: e.bn_aggr(out=sm4[:, 0:2], in_=sm3[:, 0:12]), r=[k3], w=[k4])
        S.add("act", lambda e: e.activation(out=sm4[:, 3:4], in_=sm4[:, 1:2], func=AF.Sqrt, bias=epsT[:, 0:1],
                                            scale=1.0), r=[k4, ("eps",)], w=[(k4, "s")])
        S.add("dve", lambda e: e.reciprocal(out=sm4[:, 2:3], in_=sm4[:, 3:4]), r=[(k4, "s")], w=[k4])
        S.add("dve", lambda e: e.tensor_scalar(out=xin, in0=xin, scalar1=sm4[:, 0:1], scalar2=sm4[:, 2:3],
                                               op0=ALU.subtract, op1=ALU.mult), r=[k4, (hkey, tt)], w=[(hkey, tt)])
        S.add("dve", lambda e: e.tensor_tensor(out=xin, in0=xin, in1=lnbc[:, gi, :], op=ALU.mult),
              r=[(hkey, tt), ("lnbc",)], w=[(hkey, tt)])
        S.add("dve", lambda e: e.tensor_tensor(out=xin, in0=xin, in1=lnbc[:, gi + 1, :], op=ALU.add),
              r=[(hkey, tt), ("lnbc",)], w=[(hkey, tt)])

    def phase_c(li, w_out_ap, h_src, h_dst):
        new_phase(Arena.lmark)
        alloc_common()
        stg_b, xT = com["stg_b"], com["xT"]
        uT = take([128, 32, 512], BF16)
        gate = take([128, 4, D], F32)
        lnbc = take([128, 4, D], F32)
        com["lnbc"] = lnbc
        hres = takes("hres", 1, [128, 4, D], F32)
        dma("sp", lnbc, lnb_d[li].rearrange("g p d -> p g d"), [], [("lnbc",)])
        for tg in range(NG):
            rows = slice(tg * 512, (tg + 1) * 512)
            okeys = [("obuf", tg * 4 + tt) for tt in range(4)]
            xk, xt = load_T(obuf[rows, :], okeys, False)
            hkey, ht = hres.next()
            dma("sp", ht, h_src[rows, :].rearrange("(t p) d -> p t d", p=128),
                [("hbuf", tg)], [(hkey, tt) for tt in range(4)])

            def ev_res(tt, c0, ncols, pt, pk, ht=ht, hkey=hkey):
                S.add("dve", lambda e: e.scalar_tensor_tensor(
                    out=ht[:, tt, c0:c0 + ncols], in0=ht[:, tt, c0:c0 + ncols], scalar=ALPHA,
                    in1=pt[:, 0:ncols], op0=ALU.mult, op1=ALU.add), r=pk + [(hkey, tt)], w=[(hkey, tt)])

            proj(xt, [(xk, tt) for tt in range(4)], 8, w_out_ap, D, "tok", ev_res)

            def norm_and_T(gi, ht=ht, hkey=hkey):
                bkey, bt = stg_b.next()
                for tt in range(4):
                    layer_norm(ht, hkey, tt, gi)
                    S.add("act", lambda e, tt=tt: e.copy(out=bt[:, tt, :], in_=ht[:, tt, :]),
                          r=[(hkey, tt)], w=[(bkey, tt)])
                xk2, xt2 = xT.next()
                for tt in range(4):
                    transposes(bt, (bkey, tt), 8, xt2, (xk2, tt), tt)
                return xk2, xt2

            xk1, xt1 = norm_and_T(0)

            def ev_up(fc, pt, pk):
                fk, ft = evf.next()
                S.add("act", lambda e: e.activation(out=ft[:], in_=pt[:, :], func=AF.Relu), r=pk, w=[fk])
                S.add("dve", lambda e: e.tensor_tensor(out=uT[:, fc, :], in0=ft[:], in1=ft[:], op=ALU.mult),
                      r=[fk], w=[("uT", fc)])

            proj(xt1, [(xk1, tt) for tt in range(4)], 8, w["up"][li], 4096, "feat", ev_up)
            proj(uT, [("uT", fc) for fc in range(32)], 32, w["down"][li], D, "tok", ev_res)
            xk2, xt2 = norm_and_T(2)

            def ev_gate(tt, c0, ncols, pt, pk):
                S.add("act", lambda e: e.activation(out=gate[:, tt, c0:c0 + ncols], in_=pt[:, 0:ncols],
                                                    func=AF.Sigmoid), r=pk, w=[("gate", tt)])

            proj(xt2, [(xk2, tt) for tt in range(4)], 8, w["gate"][li], D, "tok", ev_gate)
            pk_, pxt = load_T(p_in[li, rows, :], [], True, nfeat=256)

            def ev_ple(tt, c0, ncols, pt, pk, ht=ht, hkey=hkey):
                fk, ft = evf.next()
                S.add("dve", lambda e: e.tensor_tensor(out=ft[:, 0:ncols], in0=pt[:, 0:ncols],
                                                       in1=gate[:, tt, c0:c0 + ncols], op=ALU.mult),
                      r=pk + [("gate", tt)], w=[fk])
                S.add("dve", lambda e: e.tensor_tensor(out=ht[:, tt, c0:c0 + ncols], in0=ht[:, tt, c0:c0 + ncols],
                                                       in1=ft[:, 0:ncols], op=ALU.add),
                      r=[fk, (hkey, tt)], w=[(hkey, tt)])

            proj(pxt, [(pk_, tt) for tt in range(4)], 2, w["ple"][li], D, "tok", ev_ple)
            store(h_dst[rows, :].rearrange("(t p) d -> p t d", p=128), ht,
                  [(hkey, tt) for tt in range(4)], [("hbuf", tg)] if h_dst is hbuf else [("outd", tg)])

    def swa_layer(li, j, h_src, h_dst):
        new_phase(0)
        Arena.lmark = Arena.off
        alloc_common()
        for tg in range(NG):
            rows = slice(tg * 512, (tg + 1) * 512)
            xk, xt = load_T(h_src[rows, :], [("hbuf", tg)], True)

            def ev(tt, c0, ncols, pt, pk, tg=tg):
                ek, et = ev512.next()
                r0 = tg * 512 + tt * 128
                if c0 < 1024:
                    S.add("act", lambda e: e.activation(out=et[:, 0:ncols], in_=pt[:, 0:ncols], func=AF.Copy,
                                                        scale=0.125), r=pk, w=[ek])
                    store(s1_d[r0:r0 + 128, c0:c0 + ncols], et[:, 0:ncols], [ek], [("s1", tg * 4 + tt, c0)])
                else:
                    S.add("act", lambda e: e.copy(out=et[:, 0:ncols], in_=pt[:, 0:ncols]), r=pk, w=[ek])
                    store(s2_d[r0:r0 + 128, 0:ncols], et[:, 0:ncols], [ek], [("s2", tg * 4 + tt)])

            proj(xt, [(xk, tt) for tt in range(4)], 8, w["sw_in"][j], 1280, "tok", ev)
        new_phase(Arena.lmark)
        swT = take([128, 16, 256], F32)
        esk = take([128, 16], F32)
        qTt = take([128, 8, 128], BF16)
        kdup = take([128, 4, 128], BF16)
        kTd = [take([128, 4, 128], BF16) for i in range(2)]
        vx = [take([128, 2, 66], BF16) for i in range(2)]
        osb = takes("sw_o", 2, [128, D], BF16)
        qin = takes("sw_qin", 2, [128, D], BF16)
        kvin = takes("sw_kvin", 2, [128, 256], BF16)
        ptb = takes("sw_pt", 3, [128, 256], BF16)
        dma("sp", swT, swT_d[:, :, :], [], [("swT",)])
        dma("sp", esk, esk_d[:, :], [], [("esk",)])
        S.add("act", lambda e: e.activation(out=esk, in_=esk, func=AF.Exp), r=[("esk",)], w=[("esk2",)])
        for i in range(2):
            S.add("pool", lambda e, i=i: e.memset(vx[i][:, :, 64:65], 1.0), r=[], w=[("sw_vx", i)])
        S.add("pool", lambda e: e.memset(kdup[:], 0.0), r=[], w=[("sw_kdup", kh) for kh in range(2)])
        for t in range(NT):
            cur, prv = t % 2, (t + 1) % 2
            qk, qt = qin.next()
            dma("sp", qt[:], s1_d[t * 128:(t + 1) * 128, :], [("s1", t, 0), ("s1", t, 512)], [qk])
            kvk, kvt = kvin.next()
            dma("sp", kvt[:], s2_d[t * 128:(t + 1) * 128, 0:256], [("s2", t)], [kvk])
            for hc in range(8):
                S.add("pe", lambda e, hc=hc, qt=qt: e.transpose(trb[:, hc * 128:(hc + 1) * 128],
                                                               qt[:, hc * 128:(hc + 1) * 128], ident[:]),
                      r=[qk, ("ident",)], w=[("trb", hc)])
            S.add("dve", lambda e: e.tensor_copy(out=qTt[:], in_=trb[:, :].rearrange("p (k t) -> p k t", t=128)),
                  r=[("trb", hc) for hc in range(8)], w=[("sw_qT",)])
            for kh in range(2):
                for r_ in range(2):
                    S.add("pool", lambda e, kh=kh, r_=r_, kvt=kvt: e.tensor_copy(
                        out=kdup[:, kh * 2 + r_, r_ * 64:(r_ + 1) * 64], in_=kvt[:, kh * 64:(kh + 1) * 64]),
                        r=[kvk], w=[("sw_kdup", kh)])
            for k4 in range(4):
                S.add("pe", lambda e, k4=k4: e.transpose(trb[:, k4 * 128:(k4 + 1) * 128], kdup[:, k4, :], ident[:]),
                      r=[("sw_kdup", k4 // 2), ("ident",)], w=[("trb", k4)])
            S.add("dve", lambda e, cur=cur: e.tensor_copy(
                out=kTd[cur][:], in_=trb[:, 0:512].rearrange("p (k t) -> p k t", t=128)),
                r=[("trb", k4) for k4 in range(4)], w=[("sw_kTd", cur)])
            S.add("act", lambda e, cur=cur, kvt=kvt: e.copy(
                out=vx[cur][:, :, 0:64], in_=kvt[:, 128:256].rearrange("p (k d) -> p k d", d=64)),
                r=[kvk], w=[("sw_vx", cur)])
            ok, ot = osb.next()
            nb = [6, 6, 4]
            for hq in range(16):
                kvh = hq // 8
                base = (hq % 2) * 64
                nkeys = 256 if t > 0 else 128
                pk, pt = mm.next()
                S.add("pe", lambda e, pt=pt, base=base, kvh=kvh, hq=hq, cur=cur: e.matmul(
                    pt[:, 0:128], kTd[cur][:, kvh * 2 + hq % 2, :], qTt[:, hq // 2, :],
                    start=True, stop=True), r=[("sw_kTd", cur), ("sw_qT",)], w=[pk])
                if t > 0:
                    S.add("pe", lambda e, pt=pt, base=base, kvh=kvh, hq=hq, prv=prv: e.matmul(
                        pt[:, 128:256], kTd[prv][:, kvh * 2 + hq % 2, :], qTt[:, hq // 2, :],
                        start=True, stop=True), r=[("sw_kTd", prv), ("sw_qT",)], w=[pk])
                fk, ft = evf.next()
                S.add("dve", lambda e, pt=pt, ft=ft, hq=hq, nkeys=nkeys: e.tensor_tensor(
                    out=ft[:, 0:nkeys], in0=pt[:, 0:nkeys], in1=swT[:, hq, 0:nkeys], op=ALU.add),
                    r=[pk, ("swT",)], w=[fk])
                ptk, ptt = ptb.next()
                S.add("act", lambda e, ft=ft, ptt=ptt, nkeys=nkeys: e.activation(
                    out=ptt[:, 0:nkeys], in_=ft[:, 0:nkeys], func=AF.Exp), r=[fk], w=[ptk])
                bank, slot = hq // 6, hq % 6
                S.add("pe", lambda e, ptt=ptt, bank=bank, slot=slot, kvh=kvh, cur=cur, t=t: e.matmul(
                    acc[bank][:, slot * 65:(slot + 1) * 65], ptt[:, 0:128], vx[cur][:, kvh, 0:65],
                    start=True, stop=(t == 0)), r=[ptk, ("sw_vx", cur)], w=[("acc", bank, 0), ("acc", bank, 1)])
                if t > 0:
                    S.add("pe", lambda e, ptt=ptt, bank=bank, slot=slot, kvh=kvh, prv=prv: e.matmul(
                        acc[bank][:, slot * 65:(slot + 1) * 65], ptt[:, 128:256], vx[prv][:, kvh, 0:65],
                        start=False, stop=True), r=[ptk, ("sw_vx", prv)], w=[("acc", bank, 0), ("acc", bank, 1)])
            for bank in range(3):
                n = nb[bank]
                sk, sm = small.next()
                av = acc[bank][:, 0:n * 65].rearrange("p (h d) -> p h d", d=65)
                S.add("dve", lambda e, av=av, sm=sm, n=n, bank=bank: e.tensor_tensor(
                    out=sm[:, 0:n], in0=av[:, :, 64], in1=esk[:, bank * 6:bank * 6 + n], op=ALU.add),
                    r=[("acc", bank, 0), ("acc", bank, 1), ("esk2",)], w=[sk])
                S.add("dve", lambda e, sm=sm, n=n: e.reciprocal(out=sm[:, 8:8 + n], in_=sm[:, 0:n]), r=[sk], w=[sk])
                for s_ in range(n):
                    hq = bank * 6 + s_
                    S.add("dve", lambda e, av=av, sm=sm, s_=s_, hq=hq, ot=ot: e.tensor_scalar(
                        out=ot[:, hq * 64:(hq + 1) * 64], in0=av[:, s_, 0:64], scalar1=sm[:, 8 + s_:9 + s_],
                        scalar2=None, op0=ALU.mult), r=[("acc", bank, 0), ("acc", bank, 1), sk], w=[ok])
            store(obuf[t * 128:(t + 1) * 128, :], ot[:], [ok], [("obuf", t)])
        phase_c(li, w["sw_out"][j], h_src, h_dst)

    def da_layer(li, j, h_src, h_dst):
        lam_init = 0.8 - 0.6 * math.exp(-0.3 * li)
        new_phase(0)
        lamt = take([128, 256], F32)
        lsm = take([128, 8], F32)
        sublnS = take([128, 128], F32)
        Arena.lmark = Arena.off
        alloc_common()
        dma("sp", lamt, lam_d[j], [], [("lamt",)])
        dma("sp", sublnS, subln_d[j], [], [("subln",)])
        S.add("dve", lambda e: e.tensor_tensor(out=lamt[:, 0:64], in0=lamt[:, 0:64], in1=lamt[:, 64:128],
                                               op=ALU.mult), r=[("lamt",)], w=[("lamt",)])
        S.add("dve", lambda e: e.tensor_tensor(out=lamt[:, 128:192], in0=lamt[:, 128:192], in1=lamt[:, 192:256],
                                               op=ALU.mult), r=[("lamt",)], w=[("lamt",)])
        S.add("dve", lambda e: e.tensor_reduce(out=lsm[:, 0:1], in_=lamt[:, 0:64], axis=AX.X, op=ALU.add),
              r=[("lamt",)], w=[("lsm",)])
        S.add("dve", lambda e: e.tensor_reduce(out=lsm[:, 1:2], in_=lamt[:, 128:192], axis=AX.X, op=ALU.add),
              r=[("lamt",)], w=[("lsm",)])
        S.add("act", lambda e: e.activation(out=lsm[:, 2:4], in_=lsm[:, 0:2], func=AF.Exp),
              r=[("lsm",)], w=[("lsm2",)])
        S.add("dve", lambda e: e.scalar_tensor_tensor(out=lsm[:, 4:5], in0=lsm[:, 3:4], scalar=-lam_init,
                                                      in1=lsm[:, 2:3], op0=ALU.add, op1=ALU.subtract),
              r=[("lsm2",)], w=[("lsm3",)])
        S.add("dve", lambda e: e.tensor_scalar(out=sublnS[:], in0=sublnS[:], scalar1=(1.0 - lam_init),
                                               scalar2=None, op0=ALU.mult), r=[("subln",)], w=[("subln",)])
        for tg in range(NG):
            rows = slice(tg * 512, (tg + 1) * 512)
            xk, xt = load_T(h_src[rows, :], [("hbuf", tg)], True)
            xkeys = [(xk, tt) for tt in range(4)]

            def ev_qk(fc, pt, pk, tg=tg):
                ek, et = ev512.next()
                if fc < 8:
                    S.add("act", lambda e: e.activation(out=et[:], in_=pt[:, :], func=AF.Copy, scale=0.125),
                          r=pk, w=[ek])
                    store(qT_d[fc, :, tg * 512:(tg + 1) * 512], et[:], [ek], [("qT", fc, tg)])
                else:
                    S.add("act", lambda e: e.copy(out=et[:], in_=pt[:, :]), r=pk, w=[ek])
                    store(kT_d[fc - 8, :, tg * 512:(tg + 1) * 512], et[:], [ek], [("kT", fc - 8, tg)])

            proj(xt, xkeys, 8, w["da_in"][j][:, 0:2048], 2048, "feat", ev_qk)

            def ev_v(tt, c0, ncols, pt, pk, tg=tg):
                ek, et = ev512.next()
                r0 = tg * 512 + tt * 128
                S.add("act", lambda e: e.copy(out=et[:, 0:ncols], in_=pt[:, 0:ncols]), r=pk, w=[ek])
                store(s1_d[r0:r0 + 128, c0:c0 + ncols], et[:, 0:ncols], [ek], [("s1", tg * 4 + tt, c0)])

            proj(xt, xkeys, 8, w["da_in"][j][:, 2048:3072], 1024, "tok", ev_v)
        new_phase(Arena.lmark)
        tabD = take([128, 16, 128], F32)
        tabP = take([128, 16, 128], F32)
        kTs = [[take([128, T], BF16) for c in range(2)] for i in range(2)]
        vxs = [take([128, 32, 132], BF16) for i in range(2)]
        qTg = takes("da_q", 2, [128, 512], BF16)
        ptb = takes("da_pt", 4, [128, 512], BF16)
        osb = takes("da_o", 3, [128, 128], BF16)
        tmpf = takes("da_tf", 4, [128, 128], F32)
        dma("sp", tabD, tabD_d[:, :, :], [], [("tabD",)])
        dma("sp", tabP, tabP_d[:, :, :], [], [("tabP",)])
        for i in range(2):
            S.add("pool", lambda e, i=i: e.memset(vxs[i][:, :, 128:129], 1.0), r=[], w=[("da_kv", i)])
            S.add("pool", lambda e, i=i: e.memset(kTs[i][0][64:128, :], 0.0), r=[], w=[("da_kv", i)])
            S.add("pool", lambda e, i=i: e.memset(kTs[i][1][0:64, :], 0.0), r=[], w=[("da_kv", i)])
        for h in range(DBG.get("da_heads", 8)):
            sl = h % 2
            kkey = ("da_kv", sl)
            dma("sp", kTs[sl][0][0:64, :], kT_d[h, 0:64, :], [("kT", h, tg) for tg in range(NG)], [kkey])
            dma("sp", kTs[sl][1][64:128, :], kT_d[h, 64:128, :], [("kT", h, tg) for tg in range(NG)], [kkey])
            for q4 in range(NT // 8):
                dma("sp", vxs[sl][:, q4 * 8:(q4 + 1) * 8, 0:128],
                    s1_d[q4 * 1024:(q4 + 1) * 1024, h * 128:(h + 1) * 128].rearrange("(t p) v -> p t v", p=128),
                    [("s1", t, c0) for t in range(q4 * 8, q4 * 8 + 8) for c0 in (0, 512)], [kkey])
            for g in range(DBG.get("da_groups", NG)):
                qk, qt = qTg.next()
                dma("sp", qt[:], qT_d[h, :, g * 512:(g + 1) * 512], [("qT", h, g)], [qk])
                for bank in range(4):
                    S.add("pe", lambda e, bank=bank: e.matmul(acc[bank][:, :], zeros[:, 0:128], zeros[:, :],
                                                              start=True, stop=False),
                          r=[("zeros",)], w=[("acc", bank, 0), ("acc", bank, 1)])
                for c in range(2):
                    m = 2 * h + c
                    for kb in range(4 * g + 4):
                        j0 = max(0, kb - 4 * g)
                        pk, pt = mm.next()
                        S.add("pe", lambda e, pt=pt, c=c, kb=kb, j0=j0, qt=qt, sl=sl: e.matmul(
                            pt[:, j0 * 128:512], kTs[sl][c][:, kb * 128:(kb + 1) * 128],
                            qt[:, j0 * 128:512], start=True, stop=True),
                            r=[kkey, qk], w=[pk])
                        if DBG.get("da_stage", 4) < 2:
                            continue
                        ptk, ptt = ptb.next()
                        jc = j0
                        for jj in range(j0, 4):
                            dlt = 4 * g + jj - kb
                            if dlt >= 2:
                                break
                            tab = tabD if dlt == 0 else tabP
                            fk, ft = tmpf.next()
                            S.add("dve", lambda e, pt=pt, ft=ft, jj=jj, tab=tab, m=m: e.tensor_tensor(
                                out=ft[:], in0=pt[:, jj * 128:(jj + 1) * 128], in1=tab[:, m, :], op=ALU.add),
                                r=[pk, ("tabD",), ("tabP",)], w=[fk])
                            S.add("act", lambda e, ft=ft, ptt=ptt, jj=jj: e.activation(
                                out=ptt[:, jj * 128:(jj + 1) * 128], in_=ft[:], func=AF.Exp),
                                r=[fk], w=[(ptk, jj)])
                            jc = jj + 1
                        if jc < 4:
                            S.add("act", lambda e, pt=pt, ptt=ptt, jc=jc, m=m: e.activation(
                                out=ptt[:, jc * 128:512], in_=pt[:, jc * 128:512], func=AF.Exp,
                                bias=c31[:, m:m + 1], scale=1.0),
                                r=[pk, ("c31",)], w=[(ptk, jj) for jj in range(jc, 4)])
                        for jj in range(j0, 4 if DBG.get("da_stage", 4) >= 3 else 0):
                            bank = 2 * c + jj // 2
                            off = (jj % 2) * 256
                            S.add("pe", lambda e, ptt=ptt, jj=jj, bank=bank, off=off, kb=kb, sl=sl, g=g: e.matmul(
                                acc[bank][:, off:off + 129], ptt[:, jj * 128:(jj + 1) * 128], vxs[sl][:, kb, 0:129],
                                start=False, stop=(kb == 4 * g + jj and jj % 2 == 1)),
                                r=[(ptk, jj), kkey], w=[("acc", bank, jj % 2)])
                for jj in range(4 if DBG.get("da_stage", 4) >= 4 else 0):
                    a1 = acc[jj // 2][:, (jj % 2) * 256:(jj % 2) * 256 + 129]
                    a2 = acc[2 + jj // 2][:, (jj % 2) * 256:(jj % 2) * 256 + 129]
                    k1 = ("acc", jj // 2, jj % 2)
                    k2 = ("acc", 2 + jj // 2, jj % 2)
                    sk, sm = small.next()
                    S.add("dve", lambda e, a1=a1, sm=sm: e.reciprocal(out=sm[:, 0:1], in_=a1[:, 128:129]),
                          r=[k1], w=[sk])
                    S.add("dve", lambda e, a2=a2, sm=sm: e.reciprocal(out=sm[:, 1:2], in_=a2[:, 128:129]),
                          r=[k2], w=[sk])
                    S.add("dve", lambda e, sm=sm: e.tensor_tensor(out=sm[:, 2:3], in0=sm[:, 1:2], in1=lsm[:, 4:5],
                                                                  op=ALU.mult), r=[sk, ("lsm3",)], w=[sk])
                    fk, ft = tmpf.next()
                    S.add("dve", lambda e, a2=a2, sm=sm, ft=ft: e.tensor_scalar(
                        out=ft[:], in0=a2[:, 0:128], scalar1=sm[:, 2:3], scalar2=None, op0=ALU.mult),
                        r=[k2, sk], w=[fk])
                    fk2, ft2 = tmpf.next()
                    S.add("dve", lambda e, a1=a1, sm=sm, ft=ft, ft2=ft2: e.scalar_tensor_tensor(
                        out=ft2[:], in0=a1[:, 0:128], scalar=sm[:, 0:1], in1=ft[:], op0=ALU.mult, op1=ALU.add),
                        r=[k1, sk, fk], w=[fk2])
                    S.add("act", lambda e, ft=ft, ft2=ft2, sm=sm: e.activation(
                        out=ft[:], in_=ft2[:], func=AF.Square, accum_out=sm[:, 3:4]), r=[fk2, sk], w=[fk, (sk, "b")])
                    S.add("act", lambda e, sm=sm: e.activation(
                        out=sm[:, 4:5], in_=sm[:, 3:4], func=AF.Sqrt, bias=epsT[:, 1:2], scale=1.0 / 128.0),
                        r=[(sk, "b"), ("eps",)], w=[(sk, "c")])
                    S.add("dve", lambda e, sm=sm: e.reciprocal(out=sm[:, 5:6], in_=sm[:, 4:5]),
                          r=[(sk, "c")], w=[(sk, "d")])
                    ok, ot = osb.next()
                    S.add("dve", lambda e, sm=sm, ft2=ft2, ot=ot: e.scalar_tensor_tensor(
                        out=ot[:], in0=ft2[:], scalar=sm[:, 5:6], in1=sublnS[:], op0=ALU.mult, op1=ALU.mult),
                        r=[fk2, (sk, "d"), ("subln",)], w=[ok])
                    t = 4 * g + jj
                    store(obuf[t * 128:(t + 1) * 128, h * 128:(h + 1) * 128], ot[:], [ok], [("obuf", t)])
        phase_c(li, w["da_out"][j], h_src, h_dst)

    def hg_layer(li, j, h_src, h_dst):
        new_phase(0)
        hgc = take([128, 4, 128], F32)
        lbv = take([128, D], F32)
        oml = take([128, D], F32)
        gnt = take([128, D], F32)
        Arena.lmark = Arena.off
        lbt = take([128, 4, D], F32)
        dma("sp", hgc, hgc_d[:, :, :], [], [("hgc",)])
        dma("sp", gnt, gn_d[:, :], [], [("hg_gn",)])
        dma("sp", lbt, lb_d[:, :, :], [], [("lbt",)])
        S.add("act", lambda e: e.activation(out=lbt, in_=lbt, func=AF.Exp), r=[("lbt",)], w=[("lbt",)])
        S.add("dve", lambda e: e.tensor_tensor(out=oml[:], in0=lbt[:, 0, :], in1=lbt[:, 1, :], op=ALU.add),
              r=[("lbt",)], w=[("oml",)])
        S.add("dve", lambda e: e.tensor_tensor(out=oml[:], in0=oml[:], in1=lbt[:, 2, :], op=ALU.add),
              r=[("oml",), ("lbt",)], w=[("oml",)])
        S.add("dve", lambda e: e.tensor_tensor(out=oml[:], in0=oml[:], in1=lbt[:, 3, :], op=ALU.add),
              r=[("oml",), ("lbt",)], w=[("oml",)])
        S.add("dve", lambda e: e.reciprocal(out=oml[:], in_=oml[:]), r=[("oml",)], w=[("oml",)])
        S.add("dve", lambda e: e.tensor_copy(out=lbv[:], in_=lbt[:, 1, :]), r=[("lbt",)], w=[("lbv",)])
        for d_ in range(2, li + 1):
            S.add("dve", lambda e, d_=d_: e.tensor_tensor(out=lbv[:], in0=lbv[:], in1=lbt[:, d_, :], op=ALU.add),
                  r=[("lbt",), ("lbv",)], w=[("lbv",)])
        S.add("dve", lambda e: e.tensor_tensor(out=lbv[:], in0=lbv[:], in1=oml[:], op=ALU.mult),
              r=[("lbv",), ("oml",)], w=[("lbv",)])
        S.add("dve", lambda e: e.tensor_scalar(out=oml[:], in0=lbv[:], scalar1=-1.0, scalar2=1.0,
                                               op0=ALU.mult, op1=ALU.add), r=[("lbv",)], w=[("oml",)])
        new_phase(Arena.lmark)
        alloc_common()
        for tg in range(NG):
            rows = slice(tg * 512, (tg + 1) * 512)
            xk, xt = load_T(h_src[rows, :], [("hbuf", tg)], True)

            def ev(tt, c0, ncols, pt, pk, tg=tg):
                r0 = tg * 512 + tt * 128
                t = tg * 4 + tt
                kind = c0 // 1024
                cc = c0 % 1024
                if kind == 1:
                    fk, ft = evf.next()
                    S.add("act", lambda e: e.activation(out=ft[:], in_=pt[:, :], func=AF.Sigmoid), r=pk, w=[fk])
                    S.add("dve", lambda e: e.tensor_tensor(out=ft[:], in0=ft[:], in1=oml[:, cc:cc + 512],
                                                           op=ALU.mult), r=[fk, ("oml",)], w=[fk])
                    S.add("dve", lambda e: e.tensor_tensor(out=ft[:], in0=ft[:], in1=lbv[:, cc:cc + 512],
                                                           op=ALU.add), r=[fk, ("lbv",)], w=[fk])
                    ek, et = ev512.next()
                    S.add("dve", lambda e: e.tensor_scalar(out=et[:], in0=ft[:], scalar1=-1.0, scalar2=1.0,
                                                           op0=ALU.mult, op1=ALU.add), r=[fk], w=[ek])
                    store(s2_d[r0:r0 + 128, cc:cc + 512], et[:], [ek], [("s2", t, cc)])
                    fk2, ft2 = evf.next()
                    S.add("act", lambda e: e.activation(out=ft2[:], in_=ft[:], func=AF.Ln), r=[fk], w=[fk2])
                    store(lf_d[r0:r0 + 128, cc:cc + 512], ft2[:], [fk2], [("lf", t, cc)])
                else:
                    ek, et = ev512.next()
                    dst = {0: s1_d, 2: s3_d, 3: s4_d}[kind]
                    nm = {0: "s1", 2: "s3", 3: "s4"}[kind]
                    if kind == 2:
                        S.add("act", lambda e: e.copy(out=et[:], in_=pt[:, :]), r=pk, w=[ek])
                    else:
                        S.add("act", lambda e: e.activation(out=et[:], in_=pt[:, :], func=AF.Silu), r=pk, w=[ek])
                    store(dst[r0:r0 + 128, cc:cc + 512], et[:], [ek], [(nm, t, cc)])

            proj(xt, [(xk, tt) for tt in range(4)], 8, w["hg_in"][j], 4096, "tok", ev)
        new_phase(Arena.lmark)
        T2, T3, M2, IND = hgc[:, 0, :], hgc[:, 1, :], hgc[:, 2, :], hgc[:, 3, 0:2]
        lfb = takes("hg_lf", 2, [128, D], F32)
        inb = takes("hg_in", 2, [128, 4, D], BF16)
        exb = take([128, 3, D], F32)
        qg = take([128, D], BF16)
        kg = take([128, D], BF16)
        kd0 = take([128, D], BF16)
        kd1 = take([128, D], BF16)
        qgT = take([128, 8, 128], BF16)
        qgT0 = take([128, 8, 128], BF16)
        qgT1 = take([128, 8, 128], BF16)
        kgT = take([128, 8, 128], BF16)
        dec = take([128, 16], F32)
        Sst = take([128, 8, 128], F32)
        Sbf = [take([128, 8, 128], BF16) for i in range(2)]
        ATb = takes("hg_AT", 3, [128, 128], BF16)
        osq = take([128, 128], F32)
        ofl = take([128, D], F32)
        osb = takes("hg_o", 2, [128, D], BF16)
        lhi = take([128, D], BF16)
        llo = take([128, D], BF16)
        cstb = take([128, 4, 128], BF16)
        S.add("dve", lambda e: e.tensor_copy(out=cstb, in_=hgc), r=[("hgc",)], w=[("cstb",)])
        T2b, T3b, INDW = cstb[:, 0, :], cstb[:, 1, :], cstb[:, 3, :]
        S.add("pool", lambda e: e.memset(Sst[:], 0.0), r=[], w=[("hg_S", h) for h in range(8)])
        S.add("pool", lambda e: e.memset(Sbf[0][:], 0.0), r=[], w=[("hg_Sbf", 0, h) for h in range(8)])
        S.add("pool", lambda e: e.memset(qgT0[:], 0.0), r=[], w=[("hg_qgT0",)])
        S.add("pool", lambda e: e.memset(qgT1[:], 0.0), r=[], w=[("hg_qgT1",)])
        S.add("pool", lambda e: e.memset(kd0[:], 0.0), r=[], w=[("hg_kd", 0), ("hg_kd", 1)])
        S.add("pool", lambda e: e.memset(kd1[:], 0.0), r=[], w=[("hg_kd", 0), ("hg_kd", 1)])
        for t in range(DBG.get("hg_tiles", NT)):
            lk, lt = lfb.next()
            dma("sp", lt[:], lf_d[t * 128:(t + 1) * 128, :], [("lf", t, 0), ("lf", t, 512)], [lk])
            ik, it = inb.next()
            for n_, (nm, src) in enumerate((("s1", s1_d), ("s2", s2_d), ("s3", s3_d), ("s4", s4_d))):
                dma("sp", it[:, n_, :], src[t * 128:(t + 1) * 128, :], [(nm, t, 0), (nm, t, 512)], [(ik, n_)])
            S.add("act", lambda e, lt=lt: e.copy(out=lhi, in_=lt), r=[lk], w=[("lhi",)])
            S.add("dve", lambda e, lt=lt: e.tensor_tensor(out=llo, in0=lt, in1=lhi, op=ALU.subtract),
                  r=[lk, ("lhi",)], w=[("llo",)])
            for hf in range(2):
                for part, pkey, first in ((lhi, "lhi", True), (llo, "llo", False)):
                    S.add("pe", lambda e, hf=hf, part=part, first=first: e.matmul(
                        acc[hf][:, :], T2b, part[:, hf * 512:(hf + 1) * 512], start=first, stop=not first),
                        r=[(pkey,), ("cstb",)], w=[("acc", hf, 0), ("acc", hf, 1)])
                for part, pkey, first in ((lhi, "lhi", True), (llo, "llo", False)):
                    S.add("pe", lambda e, hf=hf, part=part, first=first: e.matmul(
                        acc[2 + hf][:, :], T3b, part[:, hf * 512:(hf + 1) * 512], start=first, stop=not first),
                        r=[(pkey,), ("cstb",)], w=[("acc", 2 + hf, 0), ("acc", 2 + hf, 1)])
            for hf in range(2):
                cs = slice(hf * 512, (hf + 1) * 512)
                S.add("act", lambda e, hf=hf, cs=cs: e.activation(out=exb[:, 0, cs], in_=acc[hf][:, :], func=AF.Exp),
                      r=[("acc", hf, 0), ("acc", hf, 1)], w=[("hg_ex", 0, hf)])
                S.add("act", lambda e, hf=hf, cs=cs: e.activation(out=exb[:, 2, cs], in_=acc[2 + hf][:, :],
                                                                  func=AF.Exp),
                      r=[("acc", 2 + hf, 0), ("acc", 2 + hf, 1)], w=[("hg_ex", 2, hf)])
                S.add("dve", lambda e, cs=cs, it=it: e.tensor_tensor(out=qg[:, cs], in0=it[:, 0, cs],
                                                                     in1=exb[:, 0, cs], op=ALU.mult),
                      r=[(ik, 0), ("hg_ex", 0, hf)], w=[("hg_qg", hf)])
                S.add("dve", lambda e, cs=cs: e.reciprocal(out=exb[:, 1, cs], in_=exb[:, 0, cs]),
                      r=[("hg_ex", 0, hf)], w=[("hg_ex", 1, hf)])
                S.add("dve", lambda e, cs=cs, it=it: e.tensor_tensor(out=kg[:, cs], in0=it[:, 1, cs],
                                                                     in1=exb[:, 1, cs], op=ALU.mult),
                      r=[(ik, 1), ("hg_ex", 1, hf)], w=[("hg_kg", hf)])
                S.add("dve", lambda e, cs=cs, it=it: e.tensor_tensor(out=kd0[0:64, cs], in0=it[0:64, 1, cs],
                                                                     in1=exb[0:64, 2, cs], op=ALU.mult),
                      r=[(ik, 1), ("hg_ex", 2, hf)], w=[("hg_kd", hf)])
                S.add("dve", lambda e, cs=cs, it=it: e.tensor_tensor(out=kd1[64:128, cs], in0=it[64:128, 1, cs],
                                                                     in1=exb[64:128, 2, cs], op=ALU.mult),
                      r=[(ik, 1), ("hg_ex", 2, hf)], w=[("hg_kd", hf)])
            for h in range(8):
                dk, dt_ = mm.next()
                for part, pkey, first in ((lhi, "lhi", True), (llo, "llo", False)):
                    S.add("pe", lambda e, h=h, part=part, first=first, dt_=dt_: e.matmul(
                        dt_[:, 0:128], part[:, h * 128:(h + 1) * 128], INDW, start=first, stop=not first),
                        r=[(pkey,), ("cstb",)], w=[dk])
                for c_ in range(2):
                    S.add("act", lambda e, h=h, dt_=dt_, c_=c_: e.activation(
                        out=dec[:, 2 * h + c_:2 * h + c_ + 1], in_=dt_[:, 64 * c_:64 * c_ + 1],
                        func=AF.Exp), r=[dk], w=[("hg_dec",)])
            for h in range(8):
                S.add("pe", lambda e, h=h: e.transpose(trb[:, h * 128:(h + 1) * 128], qg[:, h * 128:(h + 1) * 128],
                                                       ident[:]),
                      r=[("hg_qg", h // 4), ("ident",)], w=[("trb", h)])
            trv = trb[:, :].rearrange("p (k t) -> p k t", t=128)
            S.add("dve", lambda e: e.tensor_copy(out=qgT[:], in_=trv), r=[("trb", h) for h in range(8)],
                  w=[("hg_qgT",)])
            S.add("act", lambda e: e.copy(out=qgT0[:, :, 0:64], in_=trv[:, :, 0:64]),
                  r=[("trb", h) for h in range(8)], w=[("hg_qgT0",)])
            S.add("act", lambda e: e.copy(out=qgT1[:, :, 64:128], in_=trv[:, :, 64:128]),
                  r=[("trb", h) for h in range(8)], w=[("hg_qgT1",)])
            for h in range(8):
                S.add("pe", lambda e, h=h: e.transpose(trb[:, h * 128:(h + 1) * 128], kg[:, h * 128:(h + 1) * 128],
                                                       ident[:]),
                      r=[("hg_kg", h // 4), ("ident",)], w=[("trb", h)])
            S.add("dve", lambda e: e.tensor_copy(out=kgT[:], in_=trv), r=[("trb", h) for h in range(8)],
                  w=[("hg_kgT",)])
            ok, ot = osb.next()
            for h in range(8):
                hs = slice(h * 128, (h + 1) * 128)
                pk, pt = mm.next()
                S.add("pe", lambda e, pt=pt, h=h: e.matmul(pt[:, 0:128], kgT[:, h, :], qgT[:, h, :],
                                                           start=True, stop=True),
                      r=[("hg_kgT",), ("hg_qgT",)], w=[pk])
                ak, at = ATb.next()
                S.add("dve", lambda e, pt=pt, at=at: e.tensor_tensor(out=at[:], in0=pt[:, 0:128], in1=M2,
                                                                     op=ALU.mult), r=[pk, ("hgc",)], w=[ak])
                pk2, pt2 = mm.next()
                S.add("pe", lambda e, pt2=pt2, hs=hs, it=it: e.matmul(pt2[:, 0:128], kd0[:, hs], it[:, 2, hs],
                                                                      start=True, stop=True),
                      r=[("hg_kd", h // 4), (ik, 2)], w=[pk2])
                S.add("dve", lambda e, pt2=pt2, h=h: e.scalar_tensor_tensor(
                    out=Sst[:, h, :], in0=Sst[:, h, :], scalar=dec[:, 2 * h:2 * h + 1], in1=pt2[:, 0:128],
                    op0=ALU.mult, op1=ALU.add), r=[pk2, ("hg_dec",), ("hg_S", h)], w=[("hg_S", h)])
                S.add("act", lambda e, h=h: e.copy(out=Sbf[1][:, h, :], in_=Sst[:, h, :]),
                      r=[("hg_S", h)], w=[("hg_Sbf", 1, h)])
                obank = acc[h // 4][:, (h % 4) * 128:(h % 4 + 1) * 128]
                okey = ("acc", h // 4, h % 4 // 2)
                S.add("pe", lambda e, obank=obank, at=at, hs=hs, it=it: e.matmul(obank, at[:], it[:, 2, hs],
                                                                                 start=True, stop=False),
                      r=[ak, (ik, 2)], w=[okey])
                S.add("pe", lambda e, obank=obank, h=h: e.matmul(obank, qgT0[:, h, :], Sbf[0][:, h, :],
                                                                 start=False, stop=False),
                      r=[("hg_qgT0",), ("hg_Sbf", 0, h)], w=[okey])
                S.add("pe", lambda e, obank=obank, h=h: e.matmul(obank, qgT1[:, h, :], Sbf[1][:, h, :],
                                                                 start=False, stop=True),
                      r=[("hg_qgT1",), ("hg_Sbf", 1, h)], w=[okey])
                pk3, pt3 = mm.next()
                S.add("pe", lambda e, pt3=pt3, hs=hs, it=it: e.matmul(pt3[:, 0:128], kd1[:, hs],
                                                                      it[:, 2, hs], start=True, stop=True),
                      r=[("hg_kd", h // 4), (ik, 2)], w=[pk3])
                S.add("dve", lambda e, pt3=pt3, h=h: e.scalar_tensor_tensor(
                    out=Sst[:, h, :], in0=Sst[:, h, :], scalar=dec[:, 2 * h + 1:2 * h + 2], in1=pt3[:, 0:128],
                    op0=ALU.mult, op1=ALU.add), r=[pk3, ("hg_dec",), ("hg_S", h)], w=[("hg_S", h)])
                S.add("act", lambda e, h=h: e.copy(out=Sbf[0][:, h, :], in_=Sst[:, h, :]),
                      r=[("hg_S", h)], w=[("hg_Sbf", 0, h)])
                sk, sm = small.next()
                S.add("act", lambda e, obank=obank, hs=hs: e.copy(out=ofl[:, hs], in_=obank),
                      r=[okey], w=[("hg_of", h)])
                S.add("act", lambda e, sm=sm, hs=hs: e.activation(out=osq[:], in_=ofl[:, hs], func=AF.Square,
                                                                  accum_out=sm[:, 0:1]),
                      r=[("hg_of", h)], w=[("hg_osq",), sk])
                S.add("act", lambda e, sm=sm: e.activation(out=sm[:, 1:2], in_=sm[:, 0:1], func=AF.Sqrt,
                                                           bias=epsT[:, 1:2], scale=1.0 / 128.0),
                      r=[sk, ("eps",)], w=[(sk, "b")])
                S.add("dve", lambda e, sm=sm: e.reciprocal(out=sm[:, 2:3], in_=sm[:, 1:2]),
                      r=[(sk, "b")], w=[(sk, "c")])
                S.add("dve", lambda e, sm=sm, hs=hs: e.scalar_tensor_tensor(
                    out=ofl[:, hs], in0=ofl[:, hs], scalar=sm[:, 2:3], in1=gnt[:, hs], op0=ALU.mult, op1=ALU.mult),
                    r=[("hg_of", h), (sk, "c"), ("hg_gn",)], w=[("hg_of", h)])
                S.add("pool", lambda e, hs=hs, it=it, ot=ot: e.tensor_tensor(out=ot[:, hs], in0=ofl[:, hs],
                                                                             in1=it[:, 3, hs], op=ALU.mult),
                      r=[("hg_of", h), (ik, 3)], w=[ok])
            store(obuf[t * 128:(t + 1) * 128, :], ot[:], [ok], [("obuf", t)])
        phase_c(li, w["hg_out"][j], h_src, h_dst)

    for n, li in enumerate(layers):
        h_src = x_in if n == 0 else hbuf
        h_dst = out_d if n == len(layers) - 1 else hbuf
        kind, j = li % 3, li // 3
        if kind == 0:
            da_layer(li, j, h_src, h_dst)
        elif kind == 1:
            hg_layer(li, j, h_src, h_dst)
        else:
            swa_layer(li, j, h_src, h_dst)

    S.emit(nc, st)
    st.close()
    return nc


def host_consts(rel_bias, sw_sinks, ln_g, ln_b, da_lambda, da_subln, hg_gnorm, hg_lower_bound):
    f = np.float32
    c = {}
    c["ident"] = np.eye(128, dtype=f)
    k = np.arange(128)[:, None]
    q = np.arange(128)[None, :]
    rel_d = q - k
    bd = t5_bucket_np(rel_d)
    bp = t5_bucket_np(rel_d + 128)
    tabD = np.empty((128, 16, 128), f)
    tabP = np.empty((128, 16, 128), f)
    swT = np.empty((128, 16, 256), f)
    for m in range(16):
        tabD[:, m, :] = np.where(rel_d >= 0, rel_bias[bd, m], f(NEG))
        tabP[:, m, :] = rel_bias[bp, m]
        swT[:, m, 0:128] = tabD[:, m, :]
        swT[:, m, 128:256] = np.where(rel_d + 128 < 128, rel_bias[bp, m], f(NEG))
    c["tabD"], c["tabP"], c["swT"] = tabD, tabP, swT
    c["c31"] = np.ascontiguousarray(np.broadcast_to(rel_bias[31][None, :], (128, 16))).astype(f)
    c["sinks"] = np.ascontiguousarray(np.broadcast_to(sw_sinks[0][None, :], (128, 16))).astype(f)
    s = np.arange(128)[:, None]
    t = np.arange(128)[None, :]
    same = (s // 64) == (t // 64)
    hgc = np.zeros((128, 4, 128), f)
    hgc[:, 0, :] = (same & (s <= t)).astype(f)
    hgc[:, 1, :] = (same & (s > t)).astype(f)
    hgc[:, 2, :] = (same & (s <= t)).astype(f)
    hgc[:, 3, 0:64] = (np.arange(128) < 64).astype(f)[:, None]
    hgc[:, 3, 64:128] = (np.arange(128) >= 64).astype(f)[:, None]
    c["hgc"] = hgc
    lnbc = np.empty((DEPTH, 4, 128, D), f)
    for i in range(DEPTH):
        lnbc[i, 0] = ln_g[i, 0][None, :]
        lnbc[i, 1] = ln_b[i, 0][None, :]
        lnbc[i, 2] = ln_g[i, 1][None, :]
        lnbc[i, 3] = ln_b[i, 1][None, :]
    c["lnbc"] = lnbc
    c["lam"] = np.ascontiguousarray(np.broadcast_to(da_lambda.reshape(2, 1, 256), (2, 128, 256))).astype(f)
    c["subln"] = np.ascontiguousarray(np.broadcast_to(da_subln.reshape(2, 1, 128), (2, 128, 128))).astype(f)
    c["gnorm"] = np.ascontiguousarray(np.broadcast_to(np.tile(hg_gnorm[0], 8)[None, :], (128, D))).astype(f)
    c["hglb"] = np.ascontiguousarray(np.broadcast_to(hg_lower_bound[None, :, :], (128, 4, D))).astype(f)
    return c


_NC_CACHE = {}


def run_layers(layers, h_in, inputs):
    key = tuple(layers)
    if key not in _NC_CACHE:
        _NC_CACHE[key] = build_program(list(layers))
    nc = _NC_CACHE[key]
    f = np.float32
    g = lambda k: np.ascontiguousarray(np.asarray(inputs[k], dtype=f))
    consts = host_consts(g("rel_bias"), g("sw_sinks"), g("ln_g"), g("ln_b"), g("da_lambda"), g("da_subln"),
                         g("hg_gnorm"), g("hg_lower_bound"))
    shared = dict(consts)
    for k in ("da_w_in", "da_w_out", "hg_w_in", "hg_w_out", "sw_w_in", "sw_w_out", "w_up", "w_down", "w_ple",
              "w_ple_gate"):
        shared[k] = g(k)
    p = g("p")
    in_maps = []
    for c in range(8):
        b = c // 2
        m = dict(shared)
        m["x"] = np.ascontiguousarray(h_in[b])
        m["p"] = np.ascontiguousarray(p[:, b])
        in_maps.append(m)
    res = run_bass_kernel_spmd(nc, in_maps, core_ids=list(range(8)))
    out = np.empty((4, T, D), f)
    for b in range(4):
        out[b, 0:T // 2] = res.results[2 * b]["out"][0:T // 2]
        out[b, T // 2:T] = res.results[2 * b + 1]["out"][T // 2:T]
    return out


def kernel(**inputs):
    x = np.ascontiguousarray(np.asarray(inputs["x"], dtype=np.float32))
    return run_layers((0, 1, 2, 3), x, inputs)
```
